# Optimizing a Trainium2 kernel written in Bass

```python
import jax
import jax.numpy as jnp
from jax import lax
import numpy as np

D_MODEL = 2048
BATCH = 16
SEQ = 256
DEPTH = 2
DEC_BATCH = 8
DEC_SEQ = 4096
PAST_LEN = 256

GRID_W = 64
N_BRANCH = 4
MIX_W = D_MODEL // N_BRANCH
HEAD_DIM = 128
A_HEADS = MIX_W // HEAD_DIM
A_KV = A_HEADS // 2
A_GROUP = A_HEADS // A_KV
A_HD = HEAD_DIM
WINDOW = 128
Q_BLOCK = 128
M_HEADS = MIX_W // HEAD_DIM
M_DK = HEAD_DIM
M_DV = HEAD_DIM
R_HEADS = MIX_W // HEAD_DIM
R_DK = HEAD_DIM
R_DV = HEAD_DIM
G_HEADS = MIX_W // HEAD_DIM
G_DK = HEAD_DIM // 2
G_DV = HEAD_DIM
G_RANK = 16
GLA_TAU = 16.0
CHUNK = 64
FFN_HID = ((8 * D_MODEL // 3 + 255) // 256) * 256
ROPE_BASE = 10000.0
EPS = 1e-6

IN_SIZES = (
    A_HEADS * A_HD, A_KV * A_HD, A_KV * A_HD,
    M_HEADS * M_DK, M_HEADS * M_DK, M_HEADS * M_DV, M_HEADS * M_DV, 4 * M_HEADS,
    R_HEADS * R_DK, R_HEADS * R_DK, R_HEADS * R_DV, R_HEADS * R_DV,
    G_HEADS * G_DK, G_HEADS * G_DK, G_HEADS * G_DV, G_HEADS * G_DV, 2 * G_RANK,
)
IN_OFFSETS = tuple(int(s) for s in np.cumsum(IN_SIZES)[:-1])
D_IN = int(sum(IN_SIZES))

kernel_name = "hybrid_bidir_diffusion_step"

F32 = jnp.float32


def rms_norm(x, w):
    xf = x.astype(F32)
    y = xf * lax.rsqrt(jnp.mean(xf * xf, axis=-1, keepdims=True) + EPS)
    return (y * w.astype(F32)).astype(x.dtype)


def head_rms_norm(y, w):
    B, T, H, d = y.shape
    y = y * lax.rsqrt(jnp.mean(y * y, axis=-1, keepdims=True) + EPS)
    return y.reshape(B, T, H * d) * w.astype(F32)


def axial_rope(x):
    B, T, H, d = x.shape
    rows = T // GRID_W
    row = jnp.repeat(jnp.arange(rows), GRID_W).astype(F32)
    col = jnp.tile(jnp.arange(GRID_W), rows).astype(F32)
    quarter = d // 4
    inv = ROPE_BASE ** (-jnp.arange(quarter, dtype=F32) / quarter)

    def rot(xa, pos):
        ang = pos[:, None] * inv[None, :]
        cos = jnp.cos(ang)[None, :, None, :]
        sin = jnp.sin(ang)[None, :, None, :]
        x1, x2 = xa[..., :quarter], xa[..., quarter:]
        return jnp.concatenate([x1 * cos - x2 * sin, x2 * cos + x1 * sin], axis=-1)

    xf = x.astype(F32)
    out = jnp.concatenate([rot(xf[..., : d // 2], row), rot(xf[..., d // 2:], col)], axis=-1)
    return out.astype(x.dtype)


def to_chunks(x):
    B, T = x.shape[:2]
    x = x.reshape((B, T // CHUNK, CHUNK) + x.shape[2:])
    return jnp.moveaxis(x, (1, 2), (0, 3))


def from_chunks(y):
    y = jnp.moveaxis(y, (0, 3), (1, 2))
    B, nc, L = y.shape[:3]
    return y.reshape((B, nc * L) + y.shape[3:])


def sink_attend(s, v, sink):
    sk = sink.astype(F32)[None, :, :, None, None]
    m = jnp.maximum(jnp.max(s, axis=-1, keepdims=True), sk)
    p = jnp.exp(s - m)
    p = p / (jnp.sum(p, axis=-1, keepdims=True) + jnp.exp(sk - m))
    return jnp.einsum("bkgqs,bskd->bqkgd", p.astype(v.dtype), v)


def context_attention(q, k, v, sink):
    B, T, H, d = q.shape
    nb = T // Q_BLOCK
    scale = d ** -0.5
    qb = jnp.moveaxis(q.reshape(B, nb, Q_BLOCK, A_KV, A_GROUP, d), 1, 0)

    def block(qblk):
        s = jnp.einsum("bqkgd,bskd->bkgqs", qblk, k).astype(F32) * scale
        return sink_attend(s, v, sink)

    out = lax.map(block, qb)
    return jnp.moveaxis(out, 0, 1).reshape(B, T, H * d)


def latent_window_attention(q, k, v, k_ctx, v_ctx, sink):
    B, T, H, d = q.shape
    nb = T // Q_BLOCK
    scale = d ** -0.5
    qg = q.reshape(B, T, A_KV, A_GROUP, d)
    pad = ((0, 0), (Q_BLOCK, Q_BLOCK), (0, 0), (0, 0))
    kpad = jnp.pad(k, pad)
    vpad = jnp.pad(v, pad)
    qi = jnp.arange(Q_BLOCK)[:, None]
    kj = jnp.arange(3 * Q_BLOCK)[None, :]
    band = jnp.abs(qi + Q_BLOCK - kj) <= WINDOW

    def block(b):
        start = b * Q_BLOCK
        qb = lax.dynamic_slice_in_dim(qg, start, Q_BLOCK, axis=1)
        kb = lax.dynamic_slice_in_dim(kpad, start, 3 * Q_BLOCK, axis=1)
        vb = lax.dynamic_slice_in_dim(vpad, start, 3 * Q_BLOCK, axis=1)
        kpos = start - Q_BLOCK + jnp.arange(3 * Q_BLOCK)
        valid = band & ((kpos >= 0) & (kpos < T))[None, :]
        s_loc = jnp.einsum("bqkgd,bskd->bkgqs", qb, kb).astype(F32) * scale
        s_loc = jnp.where(valid, s_loc, -jnp.inf)
        s_ctx = jnp.einsum("bqkgd,bskd->bkgqs", qb, k_ctx.astype(qb.dtype)).astype(F32) * scale
        s = jnp.concatenate([s_loc, s_ctx], axis=-1)
        v_all = jnp.concatenate([vb, v_ctx.astype(vb.dtype)], axis=1)
        return sink_attend(s, v_all, sink)

    out = lax.map(block, jnp.arange(nb))
    return jnp.moveaxis(out, 0, 1).reshape(B, T, H * d)


def mlstm_scan(q, k, v, ig, lf, C0, n0, m0):
    dk = q.shape[-1]
    causal = jnp.tril(jnp.ones((CHUNK, CHUNK), dtype=bool))

    def step(carry, inp):
        C, n, m = carry
        qc, kc, vc, ic, fc = inp
        b = jnp.cumsum(fc, axis=-1)
        a = b + m[..., None]
        D = jnp.where(causal, b[..., :, None] - b[..., None, :] + ic[..., None, :], -jnp.inf)
        mt = jnp.maximum(a, jnp.max(D, axis=-1))
        w_state = jnp.exp(a - mt)
        s = jnp.einsum("bhtd,bhsd->bhts", qc, kc) * jnp.exp(D - mt[..., None])
        num = w_state[..., None] * jnp.einsum("bhtd,bhde->bhte", qc, C) + jnp.einsum("bhts,bhse->bhte", s, vc)
        den = w_state * jnp.einsum("bhtd,bhd->bht", qc, n) + jnp.sum(s, axis=-1)
        h = num / jnp.maximum(jnp.abs(den), jnp.exp(-mt))[..., None]
        bl = b[..., -1]
        g = bl[..., None] - b + ic
        m_new = jnp.maximum(bl + m, jnp.max(g, axis=-1))
        w_old = jnp.exp(bl + m - m_new)
        w_new = jnp.exp(g - m_new[..., None])
        C = w_old[..., None, None] * C + jnp.einsum("bhs,bhsd,bhse->bhde", w_new, kc, vc)
        n = w_old[..., None] * n + jnp.einsum("bhs,bhsd->bhd", w_new, kc)
        return (C, n, m_new), h

    xs = (to_chunks(q.astype(F32)), to_chunks(k.astype(F32)) * dk ** -0.5,
          to_chunks(v.astype(F32)), to_chunks(ig), to_chunks(lf))
    (C, n, m), h = lax.scan(step, (C0.astype(F32), n0.astype(F32), m0.astype(F32)), xs)
    return from_chunks(h), (C, n, m)


def retention_scan(q, k, v, S0, log_gamma):
    dk = q.shape[-1]
    idx = jnp.arange(CHUNK, dtype=F32)
    causal = idx[:, None] >= idx[None, :]
    lg = log_gamma.astype(F32)
    decay = jnp.exp(jnp.where(causal, (idx[:, None] - idx[None, :]) * lg[:, None, None], -jnp.inf))
    xi = jnp.exp((idx + 1.0)[None, :] * lg[:, None])
    zeta = jnp.exp((CHUNK - 1.0 - idx)[None, :] * lg[:, None])
    g_chunk = jnp.exp(CHUNK * lg)

    def step(S, inp):
        qc, kc, vc = inp
        att = jnp.einsum("bhtd,bhsd->bhts", qc, kc) * decay
        y = jnp.einsum("bhts,bhse->bhte", att, vc) + xi[:, :, None] * jnp.einsum("bhtd,bhde->bhte", qc, S)
        S = g_chunk[:, None, None] * S + jnp.einsum("bhsd,bhse->bhde", kc * zeta[:, :, None], vc)
        return S, y

    xs = (to_chunks(q.astype(F32)), to_chunks(k.astype(F32)) * dk ** -0.5, to_chunks(v.astype(F32)))
    S, y = lax.scan(step, S0.astype(F32), xs)
    return from_chunks(y), S


def gla_scan(q, k, v, log_alpha, S0):
    dk = q.shape[-1]
    causal = jnp.tril(jnp.ones((CHUNK, CHUNK), dtype=bool))[None, None, :, :, None]

    def step(S, inp):
        qc, kc, vc, ac = inp
        cum = jnp.cumsum(ac, axis=2)
        rel = jnp.exp(jnp.where(causal, cum[:, :, :, None, :] - cum[:, :, None, :, :], -jnp.inf))
        att = jnp.einsum("bhtd,bhsd,bhtsd->bhts", qc, kc, rel)
        y = jnp.einsum("bhts,bhse->bhte", att, vc) + jnp.einsum("bhtd,bhde->bhte", qc * jnp.exp(cum), S)
        last = cum[:, :, -1]
        S = jnp.exp(last)[..., None] * S + jnp.einsum("bhsd,bhse->bhde", kc * jnp.exp(last[:, :, None, :] - cum), vc)
        return S, y

    xs = (to_chunks(q.astype(F32)) * dk ** -0.5, to_chunks(k.astype(F32)),
          to_chunks(v.astype(F32)), to_chunks(log_alpha))
    S, y = lax.scan(step, S0.astype(F32), xs)
    return from_chunks(y), S


def run_direction(scan_fn, seqs, init, reverse, *extra):
    if reverse:
        seqs = tuple(jnp.flip(a, axis=1) for a in seqs)
    y, state = scan_fn(*seqs, *init, *extra)
    if reverse:
        y = jnp.flip(y, axis=1)
    return y, state


def token_mixers(h, p, l, cache):
    B, T, _ = h.shape
    is_ctx = cache is None
    z = jnp.einsum("btd,de->bte", h, p["w_in"][l])
    (aq, ak, av, mq, mk, mv, mo, mg, rq, rk, rv, rg,
     gq, gk, gv, gg, glr) = jnp.split(z, IN_OFFSETS, axis=-1)

    def heads(a, n):
        return a.reshape(B, T, n, -1)

    q_a, k_a, v_a = heads(aq, A_HEADS), heads(ak, A_KV), heads(av, A_KV)
    sink = p["attn_sink"][l].reshape(A_KV, A_GROUP)
    if is_ctx:
        y_a = context_attention(q_a, k_a, v_a, sink)
    else:
        y_a = latent_window_attention(axial_rope(q_a), axial_rope(k_a), v_a, cache[0], cache[1], sink)

    q, k, v = heads(mq, M_HEADS), heads(mk, M_HEADS), heads(mv, M_HEADS)
    gates = mg.astype(F32).reshape(B, T, 2, 2, M_HEADS) + p["mlstm_if_b"][l].astype(F32)
    h_m, st_m = [], []
    for d in range(2):
        if is_ctx:
            init = (jnp.zeros((B, M_HEADS, M_DK, M_DV), F32), jnp.zeros((B, M_HEADS, M_DK), F32),
                    jnp.zeros((B, M_HEADS), F32))
        else:
            init = (cache[2][:, d], cache[3][:, d], cache[4][:, d])
        seqs = (q, k, v, gates[:, :, d, 0], jax.nn.log_sigmoid(gates[:, :, d, 1]))
        y_d, s_d = run_direction(mlstm_scan, seqs, init, d == 1)
        h_m.append(y_d)
        st_m.append(s_d)
    y_m = jax.nn.sigmoid(mo.astype(F32)) * head_rms_norm(h_m[0] + h_m[1], p["mlstm_norm_w"][l])

    q, k, v = heads(rq, R_HEADS), heads(rk, R_HEADS), heads(rv, R_HEADS)
    h_r, st_r = [], []
    for d in range(2):
        log_gamma = jax.nn.log_sigmoid(p["ret_decay"][l, d].astype(F32))
        init = (jnp.zeros((B, R_HEADS, R_DK, R_DV), F32),) if is_ctx else (cache[5][:, d],)
        y_d, s_d = run_direction(retention_scan, (q, k, v), init, d == 1, log_gamma)
        h_r.append(y_d)
        st_r.append(s_d)
    y_r = jax.nn.silu(rg.astype(F32)) * head_rms_norm(h_r[0] + h_r[1], p["ret_norm_w"][l])

    q, k, v = heads(gq, G_HEADS), heads(gk, G_HEADS), heads(gv, G_HEADS)
    lr = glr.reshape(B, T, 2, G_RANK)
    h_g, st_g = [], []
    for d in range(2):
        logit = jnp.einsum("btr,re->bte", lr[:, :, d], p["gla_w2"][l, d]) + p["gla_b"][l, d]
        la = (jax.nn.log_sigmoid(logit.astype(F32)) / GLA_TAU).reshape(B, T, G_HEADS, G_DK)
        init = (jnp.zeros((B, G_HEADS, G_DK, G_DV), F32),) if is_ctx else (cache[6][:, d],)
        y_d, s_d = run_direction(gla_scan, (q, k, v, la), init, d == 1)
        h_g.append(y_d)
        st_g.append(s_d)
    y_g = jax.nn.silu(gg.astype(F32)) * head_rms_norm(h_g[0] + h_g[1], p["gla_norm_w"][l])

    ys = (y_a, y_m.astype(h.dtype), y_r.astype(h.dtype), y_g.astype(h.dtype))
    merged = None
    for b in range(N_BRANCH):
        gate = jax.nn.sigmoid(jnp.einsum("btd,de->bte", h, p["w_mgate"][l, b]).astype(F32)).astype(h.dtype)
        term = gate * jnp.einsum("btc,cd->btd", ys[b], p["w_br"][l, b])
        merged = term if merged is None else merged + term
    out = jnp.einsum("btd,de->bte", merged, p["w_out"][l])

    if is_ctx:
        states = (k_a, v_a,
                  jnp.stack([st_m[0][0], st_m[1][0]], axis=1),
                  jnp.stack([st_m[0][1], st_m[1][1]], axis=1),
                  jnp.stack([st_m[0][2], st_m[1][2]], axis=1),
                  jnp.stack(st_r, axis=1),
                  jnp.stack(st_g, axis=1))
    else:
        states = None
    return out, states


def layer(x, cond, p, l, cache):
    mod = jnp.einsum("...d,de->...e", jax.nn.silu(cond), p["w_ada"][l]) + p["b_ada"][l]
    sh1, sc1, g1, sh2, sc2, g2 = jnp.split(mod, 6, axis=-1)
    h = rms_norm(x, p["norm1_w"][l]) * (1 + sc1) + sh1
    mix, states = token_mixers(h, p, l, cache)
    x = x + g1 * mix
    h = rms_norm(x, p["norm2_w"][l]) * (1 + sc2) + sh2
    gt, up = jnp.split(jnp.einsum("btd,df->btf", h, p["ffn_w_gu"][l]), 2, axis=-1)
    x = x + g2 * jnp.einsum("btf,fd->btd", jax.nn.silu(gt) * up, p["ffn_w_down"][l])
    return x, states


def setup_inputs(seed: int = 0) -> dict:
    key = jax.random.key(seed)
    ks = jax.random.split(key, 32)
    D = D_MODEL

    def nrm(k, shape, s):
        return jax.random.normal(k, shape, F32) * s

    return {
        "x_prompt": nrm(ks[0], (BATCH, SEQ, D), 1.0),
        "x_sample": nrm(ks[1], (DEC_BATCH, DEC_SEQ, D), 1.0),
        "cache_attn_k": nrm(ks[2], (DEC_BATCH, DEPTH, PAST_LEN, A_KV, A_HD), 1.0),
        "cache_attn_v": nrm(ks[3], (DEC_BATCH, DEPTH, PAST_LEN, A_KV, A_HD), 1.0),
        "state_mlstm_C": nrm(ks[4], (DEC_BATCH, DEPTH, 2, M_HEADS, M_DK, M_DV), 0.1),
        "state_mlstm_n": nrm(ks[5], (DEC_BATCH, DEPTH, 2, M_HEADS, M_DK), 0.1),
        "state_mlstm_m": nrm(ks[6], (DEC_BATCH, DEPTH, 2, M_HEADS), 1.0),
        "state_ret_S": nrm(ks[7], (DEC_BATCH, DEPTH, 2, R_HEADS, R_DK, R_DV), 0.5),
        "state_gla_S": nrm(ks[8], (DEC_BATCH, DEPTH, 2, G_HEADS, G_DK, G_DV), 0.5),
        "c": nrm(ks[9], (DEC_BATCH, D), 1.0),
        "c_ctx": nrm(ks[10], (D,), 1.0),
        "w_ada": nrm(ks[11], (DEPTH, D, 6 * D), 0.5 * D ** -0.5),
        "b_ada": nrm(ks[12], (DEPTH, 6 * D), 0.02),
        "norm1_w": 1.0 + nrm(ks[13], (DEPTH, D), 0.02),
        "norm2_w": 1.0 + nrm(ks[14], (DEPTH, D), 0.02),
        "w_in": nrm(ks[15], (DEPTH, D, D_IN), D ** -0.5),
        "attn_sink": nrm(ks[16], (DEPTH, A_HEADS), 1.0),
        "mlstm_if_b": nrm(ks[17], (DEPTH, 2, 2, M_HEADS), 0.1) + jnp.array([0.0, 3.0], F32)[None, None, :, None],
        "mlstm_norm_w": 1.0 + nrm(ks[18], (DEPTH, M_HEADS * M_DV), 0.02),
        "ret_decay": jnp.log(2.0 ** (5.0 + jnp.arange(R_HEADS, dtype=F32)) - 1.0) + nrm(ks[19], (DEPTH, 2, R_HEADS), 0.05),
        "ret_norm_w": 1.0 + nrm(ks[20], (DEPTH, R_HEADS * R_DV), 0.02),
        "gla_w2": nrm(ks[21], (DEPTH, 2, G_RANK, G_HEADS * G_DK), G_RANK ** -0.5),
        "gla_b": nrm(ks[22], (DEPTH, 2, G_HEADS * G_DK), 0.1),
        "gla_norm_w": 1.0 + nrm(ks[23], (DEPTH, G_HEADS * G_DV), 0.02),
        "w_br": nrm(ks[24], (DEPTH, N_BRANCH, MIX_W, D), MIX_W ** -0.5),
        "w_mgate": nrm(ks[25], (DEPTH, N_BRANCH, D, D), D ** -0.5),
        "w_out": nrm(ks[26], (DEPTH, D, D), D ** -0.5),
        "ffn_w_gu": nrm(ks[27], (DEPTH, D, 2 * FFN_HID), D ** -0.5),
        "ffn_w_down": nrm(ks[28], (DEPTH, FFN_HID, D), FFN_HID ** -0.5),
        "final_norm_w": 1.0 + nrm(ks[29], (D,), 0.02),
    }


def reference(x_prompt, x_sample, cache_attn_k, cache_attn_v, state_mlstm_C, state_mlstm_n,
              state_mlstm_m, state_ret_S, state_gla_S, c, c_ctx, w_ada, b_ada, norm1_w, norm2_w,
              w_in, attn_sink, mlstm_if_b, mlstm_norm_w, ret_decay, ret_norm_w, gla_w2, gla_b,
              gla_norm_w, w_br, w_mgate, w_out, ffn_w_gu, ffn_w_down, final_norm_w):
    p = {"w_ada": w_ada, "b_ada": b_ada, "norm1_w": norm1_w, "norm2_w": norm2_w, "w_in": w_in,
         "attn_sink": attn_sink, "mlstm_if_b": mlstm_if_b, "mlstm_norm_w": mlstm_norm_w,
         "ret_decay": ret_decay, "ret_norm_w": ret_norm_w, "gla_w2": gla_w2, "gla_b": gla_b,
         "gla_norm_w": gla_norm_w, "w_br": w_br, "w_mgate": w_mgate, "w_out": w_out,
         "ffn_w_gu": ffn_w_gu, "ffn_w_down": ffn_w_down}

    xp = x_prompt
    ctx_states = []
    for l in range(DEPTH):
        xp, st = layer(xp, c_ctx[None, None, :], p, l, None)
        ctx_states.append(st)
    y_prompt = rms_norm(xp, final_norm_w)
    new_attn_k = jnp.stack([s[0] for s in ctx_states], axis=1)
    new_attn_v = jnp.stack([s[1] for s in ctx_states], axis=1)
    new_mlstm_C = jnp.stack([s[2] for s in ctx_states], axis=1)
    new_mlstm_n = jnp.stack([s[3] for s in ctx_states], axis=1)
    new_mlstm_m = jnp.stack([s[4] for s in ctx_states], axis=1)
    new_ret_S = jnp.stack([s[5] for s in ctx_states], axis=1)
    new_gla_S = jnp.stack([s[6] for s in ctx_states], axis=1)

    xs = x_sample
    for l in range(DEPTH):
        cache = (cache_attn_k[:, l], cache_attn_v[:, l], state_mlstm_C[:, l], state_mlstm_n[:, l],
                 state_mlstm_m[:, l], state_ret_S[:, l], state_gla_S[:, l])
        xs, _ = layer(xs, c[:, None, :], p, l, cache)
    y_sample = rms_norm(xs, final_norm_w)

    return (y_prompt, y_sample, new_attn_k, new_attn_v, new_mlstm_C, new_mlstm_n, new_mlstm_m, new_ret_S, new_gla_S)
```

```python
import contextlib
import numpy as np
import concourse.bass as bass
import concourse.mybir as mybir
from concourse.bass_utils import run_bass_kernel_spmd

F32 = mybir.dt.float32
BF16 = mybir.dt.bfloat16
ALU = mybir.AluOpType
AF = mybir.ActivationFunctionType
AX = mybir.AxisListType
PE, DVE, ACT, POOL, SP = "tensor", "vector", "scalar", "gpsimd", "sync"
ENGINES = [PE, DVE, ACT, POOL, SP]

D = 2048
KD = 16
L = 2
FF = 5632
KF = 44
DIN = 6704
EPS = 1e-6
NF = 30
NTM = 4608
O_AQ, O_AK, O_AV, O_MQ, O_MK, O_MV, O_MO, O_MG = 0, 512, 768, 1024, 1536, 2048, 2560, 3072
O_RQ, O_RK, O_RV, O_RG, O_GQ, O_GK, O_GV, O_GG, O_GLR = 3088, 3600, 4112, 4624, 5136, 5392, 5648, 6160, 6672
T_AV, T_MK, T_MV, T_MO, T_RK, T_RV, T_RG, T_GK, T_GV, T_GG = 0, 256, 768, 1280, 1792, 2304, 2816, 3328, 3584, 4096
C_ID, C_LE, C_GE, C_GT, C_LT, C_DF, C_DB, C_ONE, C_PF, C_PB, C_Z, C_MP, C_MN = range(13)
NCST = 13
import os
STREAMS = os.environ.get('KSTREAMS', 'none')


class V:
    __slots__ = ("t", "ap")

    def __init__(self, t, ap):
        self.t = t
        self.ap = ap

    def __getitem__(self, k):
        return V(self.t, self.ap[k])


class T:
    _n = 0

    def __init__(self, ap=None, name="", partial=False):
        self.ap = ap
        self.name = name
        self.partial = partial
        self.writers = []
        self.readers = []
        T._n += 1
        self.id = T._n

    def __getitem__(self, k):
        return V(self, self.ap[k])

    def v(self, ap):
        return V(self, ap)


class Op:
    __slots__ = ("eng", "fn", "deps", "needed", "val", "dma_key", "is_dma")

    def __init__(self, eng, fn, is_dma=False, dma_key=None):
        self.eng = eng
        self.fn = fn
        self.deps = []
        self.needed = False
        self.val = None
        self.is_dma = is_dma
        self.dma_key = dma_key


class FW:
    def __init__(self, nc):
        self.nc = nc
        self.ops = {e: [] for e in ENGINES}
        self.pending_dma = []
        self.nops = 0
        self.slot_of = {}
        self.dma_count = {}

    def _record(self, op, reads, writes):
        deps = []
        for t in reads:
            deps.extend(t.writers)
        for t in writes:
            if t.partial:
                deps.extend(t.readers)
            else:
                deps.extend(t.writers)
                deps.extend(t.readers)
        seen = set()
        for d in deps:
            if d is op or id(d) in seen:
                continue
            seen.add(id(d))
            if (not d.is_dma) and d.eng == op.eng and not op.is_dma and op.eng == PE:
                continue
            op.deps.append(d)
            d.needed = True
        for t in reads:
            t.readers.append(op)
            if len(t.readers) > 64:
                t.readers = t.readers[-64:] if False else t.readers
        for t in writes:
            if t.partial:
                if t.readers:
                    t.writers = [op]
                    t.readers = []
                else:
                    t.writers.append(op)
            else:
                t.writers = [op]
                t.readers = []
        self.ops[op.eng].append(op)
        self.nops += 1
        return op

    def op(self, eng, fn, reads=(), writes=()):
        return self._record(Op(eng, fn), list(reads), list(writes))

    def dma(self, eng, out_ap, in_ap, reads=(), writes=(), key=None):
        if key.id not in self.slot_of:
            self.slot_of[key.id] = len(self.slot_of)
        slot = self.slot_of[key.id]
        o = Op(eng, lambda e: e.dma_start(out=out_ap, in_=in_ap, allow_slow_non_contiguous=True), is_dma=True, dma_key=slot)
        self.dma_count[slot] = self.dma_count.get(slot, 0) + 16
        o.val = self.dma_count[slot]
        o.needed = True
        self.pending_dma.append(o)
        return self._record(o, list(reads), list(writes))

    def barrier(self):
        b = Op(SP, lambda e: e.nop())
        for e in ENGINES:
            if e != SP and self.ops[e]:
                d = self.ops[e][-1]
                d.needed = True
                b.deps.append(d)
        for d in self.pending_dma:
            b.deps.append(d)
        self.pending_dma = []
        self.slot_of = {}
        b.needed = True
        self.ops[SP].append(b)
        for e in ENGINES:
            if e != SP:
                o = Op(e, lambda en: en.nop())
                o.deps.append(b)
                self.ops[e].append(o)

    def emit(self):
        nc = self.nc
        dma_counts = {}
        for e in ENGINES:
            cnt = 0
            for o in self.ops[e]:
                if o.is_dma:
                    dma_counts[o.dma_key] = 1
                elif o.needed:
                    cnt += 1
                    o.val = cnt
        sems = {}
        stack = contextlib.ExitStack()
        with stack:
            for e in ENGINES:
                sems[("eng", e)] = stack.enter_context(nc.semaphore("c_" + e))
            for k in dma_counts:
                sems[("dma", k)] = stack.enter_context(nc.semaphore("d_%d" % k))
            self.n_sems = len(sems)
            final = {}
            for e in ENGINES:
                for o in self.ops[e]:
                    if o.val is not None:
                        k = ("dma", o.dma_key) if o.is_dma else ("eng", o.eng)
                        final[k] = max(final.get(k, 0), o.val)
            block = stack.enter_context(nc.Block())

            def run(engname, eng):
                waited = {}
                for o in self.ops[engname]:
                    need = {}
                    for d in o.deps:
                        k = ("dma", d.dma_key) if d.is_dma else ("eng", d.eng)
                        if d.val > need.get(k, 0):
                            need[k] = d.val
                    for k, v in need.items():
                        if waited.get(k, 0) < v:
                            eng.wait_ge(sems[k], v)
                            waited[k] = v
                    inst = o.fn(eng)
                    if o.is_dma:
                        inst.then_inc(sems[("dma", o.dma_key)], 16)
                    elif o.needed:
                        inst.then_inc(sems[("eng", engname)], 1)
                if engname == SP:
                    for k, v in final.items():
                        if waited.get(k, 0) < v:
                            eng.wait_ge(sems[k], v)

            @block.tensor
            def _(eng):
                run(PE, eng)

            @block.vector
            def _(eng):
                run(DVE, eng)

            @block.scalar
            def _(eng):
                run(ACT, eng)

            @block.gpsimd
            def _(eng):
                run(POOL, eng)

            @block.sync
            def _(eng):
                run(SP, eng)


def _ts(xs):
    return [x.t for x in xs if isinstance(x, V)]


def _a(x):
    return x.ap if isinstance(x, V) else x


class K:
    def __init__(self, TS, TP, NP, nlayers=L, debug=False):
        self.debug = debug
        self.TS, self.TP, self.NP = TS, TP, NP
        self.NT = TS + TP * NP
        assert TS % 512 == 0 and TP * NP == 512 and TP % 128 == 0
        self.NTT = self.NT // 512
        self.nl = nlayers
        self.nc = bass.Bass("TRN2", target_bir_lowering=False)
        self.fw = FW(self.nc)
        self.seqs = [(0, TS, -1)] + [(TS + i * TP, TP, i) for i in range(NP)]

    def tt(self, eng, out, a, b, op):
        self.fw.op(eng, lambda e: e.tensor_tensor(_a(out), _a(a), _a(b), op), _ts([a, b]), _ts([out]))

    def ts(self, eng, out, a, s1, op0, s2=None, op1=None, accum=None):
        def fn(e):
            kw = {}
            if accum is not None:
                kw["accum_out"] = _a(accum)
            if op1 is None:
                return e.tensor_scalar(_a(out), _a(a), _a(s1), None, op0, **kw)
            return e.tensor_scalar(_a(out), _a(a), _a(s1), _a(s2), op0, op1, **kw)
        self.fw.op(eng, fn, _ts([a, s1, s2]), _ts([out, accum]))

    def stt(self, eng, out, a, s, b, op0, op1):
        self.fw.op(eng, lambda e: e.scalar_tensor_tensor(_a(out), _a(a), _a(s), _a(b), op0, op1), _ts([a, s, b]), _ts([out]))

    def act(self, out, a, func, bias=None, scale=None, accum=None):
        def fn(e):
            kw = {}
            if bias is not None:
                kw["bias"] = _a(bias)
            if scale is not None:
                kw["scale"] = _a(scale)
            if accum is not None:
                kw["accum_out"] = _a(accum)
            return e.activation(out=_a(out), in_=_a(a), func=func, **kw)
        self.fw.op(ACT, fn, _ts([a, bias, scale]), _ts([out, accum]))

    def cp(self, eng, out, a):
        if eng == ACT:
            self.fw.op(ACT, lambda e: e.copy(_a(out), _a(a)), _ts([a]), _ts([out]))
        else:
            self.fw.op(eng, lambda e: e.tensor_copy(_a(out), _a(a)), _ts([a]), _ts([out]))

    def memset(self, eng, out, val):
        self.fw.op(eng, lambda e: e.memset(_a(out), val), [], _ts([out]))

    def red(self, out, a, op):
        self.fw.op(DVE, lambda e: e.tensor_reduce(_a(out), _a(a), AX.X, op), _ts([a]), _ts([out]))

    def mm(self, out, pairs, extra_reads=()):
        def fn(e):
            n = len(pairs)
            inst = None
            for i, (l, r) in enumerate(pairs):
                inst = e.matmul(_a(out), _a(l), _a(r), start=(i == 0), stop=(i == n - 1))
            return inst
        rd = []
        for l, r in pairs:
            rd += _ts([l, r])
        self.fw.op(PE, fn, rd + list(extra_reads), _ts([out]))

    def tr(self, out, a, ident):
        self.fw.op(PE, lambda e: e.transpose(_a(out), _a(a), _a(ident)), _ts([a, ident]), _ts([out]))

    def ld(self, out, src_ap, src_ts=(), eng=SP):
        self.fw.dma(eng, _a(out), src_ap, reads=list(src_ts), writes=[out.t], key=out.t)

    def st(self, dst_ap, a, dst_ts=(), eng=SP):
        self.fw.dma(eng, dst_ap, _a(a), reads=[a.t], writes=list(dst_ts), key=a.t)

    def uniq(self, name):
        self._u = getattr(self, "_u", 0) + 1
        return "%s_u%d" % (name, self._u)

    def sb(self, st, name, shape, dt=F32):
        return T(st.enter_context(self.nc.sbuf_tensor(self.uniq(name), shape, dt)), name)

    def ps(self, st, name, shape, dt=F32):
        return T(st.enter_context(self.nc.psum_tensor(self.uniq(name), shape, dt)), name)

    def psbank(self, st, name, dt=F32):
        return st.enter_context(self.nc.psum_tensor(self.uniq(name), [128, 512 if dt == F32 else 1024], dt))

    def sub(self, bank, a, b, n3=None):
        ap = bank[:, a:b]
        if n3:
            ap = ap.rearrange("p (a b) -> p a b", a=n3)
        return T(ap)

    def recip(self, out, a):
        self.fw.op(DVE, lambda e: e.reciprocal(_a(out), _a(a)), _ts([a]), _ts([out]))

    def dram(self, name, shape, dt, kind="Internal"):
        if self.debug and kind == "Internal":
            kind = "ExternalOutput"
        return self.nc.dram_tensor(name, list(shape), dt, kind=kind).ap()

    def build(self):
        nc, fw = self.nc, self.fw
        NT, NTT, TS, TP, NP, nl = self.NT, self.NTT, self.TS, self.TP, self.NP, self.nl
        I = lambda n, s, dt=F32: self.dram(n, s, dt, "ExternalInput")
        O = lambda n, s, dt=F32: self.dram(n, s, dt, "ExternalOutput")
        d = self.d = {}
        d["x"] = I("x", [NT, D])
        d["cak"] = I("cak", [L, 256, 256]); d["cav"] = I("cav", [L, 256, 256])
        d["smC"] = I("smC", [L, 2, 4, 128, 128]); d["smn"] = I("smn", [L, 2, 4, 128]); d["smm"] = I("smm", [L, 8])
        d["srS"] = I("srS", [L, 2, 4, 128, 128]); d["sgS"] = I("sgS", [L, 2, 4, 64, 128])
        d["cond"] = I("cond", [2, D])
        d["w_ada"] = I("w_ada", [L, D, 6 * D]); d["b_ada"] = I("b_ada", [L, 6 * D])
        d["norm1_w"] = I("norm1_w", [L, D]); d["norm2_w"] = I("norm2_w", [L, D])
        d["w_in"] = I("w_in", [L, D, DIN]); d["attn_sink"] = I("attn_sink", [L, 4])
        d["mlstm_if_b"] = I("mlstm_if_b", [L, 16]); d["mlstm_norm_w"] = I("mlstm_norm_w", [L, 512])
        d["ret_decay"] = I("ret_decay", [L, 8]); d["ret_norm_w"] = I("ret_norm_w", [L, 512])
        d["gla_w2"] = I("gla_w2", [L, 2, 16, 256]); d["gla_b"] = I("gla_b", [L, 2, 256]); d["gla_norm_w"] = I("gla_norm_w", [L, 512])
        d["w_br"] = I("w_br", [L, 4, 512, D]); d["w_mgate"] = I("w_mgate", [L, 4, D, D]); d["w_out"] = I("w_out", [L, D, D])
        d["ffn_w_gu"] = I("ffn_w_gu", [L, D, 2 * FF]); d["ffn_w_down"] = I("ffn_w_down", [L, FF, D])
        d["final_norm_w"] = I("final_norm_w", [D])
        d["cst"] = I("cst", [128, NCST * 128]); d["ropeC"] = I("ropeC", [128, TS]); d["ropeS"] = I("ropeS", [128, TS])
        d["perm"] = I("perm", [128, 128])
        d["y"] = O("y", [NT, D])
        d["o_ak"] = O("o_ak", [NP, L, TP, 256]); d["o_av"] = O("o_av", [NP, L, TP, 256])
        d["o_mC"] = O("o_mC", [NP, L, 2, 4, 128, 128]); d["o_mn"] = O("o_mn", [NP, L, 2, 4, 128]); d["o_mm"] = O("o_mm", [NP, L, 8])
        d["o_rS"] = O("o_rS", [NP, L, 2, 4, 128, 128]); d["o_gS"] = O("o_gS", [NP, L, 2, 4, 64, 128])
        d["xres"] = self.dram("xres", [NT, D], F32)
        d["hT"] = self.dram("hT", [NTT, 128, KD, 512], BF16)
        d["zfm"] = self.dram("zfm", [NTT, 128, NF, 512], BF16)
        d["ztm"] = self.dram("ztm", [NT, NTM], BF16)
        d["mg"] = self.dram("mg", [NT, 16], F32)
        d["glr"] = self.dram("glr", [2, 16, NT], F32)
        d["ysT"] = self.dram("ysT", [NTT, 128, 16, 512], BF16)
        d["mT"] = self.dram("mT", [NTT, 128, KD, 512], BF16)
        d["aT"] = self.dram("aT", [NTT, 128, KF, 512], BF16)
        d["gb"] = self.dram("gb", [L, 2, 2, D], F32)
        if self.debug:
            d["modv_o"] = self.dram("modv_o", [128, L * 2 * 4 * KD], F32)
            d["dbg_acc"] = self.dram("dbg_acc", [128, (TS // 128) * 256], F32)
            d["dbg_fm"] = self.dram("dbg_fm", [128, 4 * TS], BF16)
            d["dbg_tm"] = self.dram("dbg_tm", [128, (TS // 128) * 512], BF16)
            d["dbg_g"] = self.dram("dbg_g", [128, 5 * (TS // 128) * 8], F32)
        self.t_x = [T(name="x%d" % i, partial=True) for i in range(NT // 128)]
        self.t_hT = [T(name="hT%d" % i, partial=True) for i in range(NTT)]
        self.t_z = [T(name="z%d" % i, partial=True) for i in range(NTT)]
        self.t_ys = [T(name="ys%d" % i, partial=True) for i in range(NTT)]
        self.t_mT = [T(name="mT%d" % i, partial=True) for i in range(NTT)]
        self.t_aT = [T(name="aT%d" % i, partial=True) for i in range(NTT)]
        self.t_gb = T(name="gb", partial=True)
        self.t_out = T(name="outs", partial=True)

        with contextlib.ExitStack() as gst:
            self.cst = self.sb(gst, "cst", [128, NCST * 128], F32)
            self.cstb = self.sb(gst, "cstb", [128, NCST * 128], BF16)
            self.modv = self.sb(gst, "modv", [128, L * 2 * 4 * KD], F32)
            self.epsc = self.sb(gst, "epsc", [128, 2], F32)
            self.memset(DVE, self.epsc[:, 0:1], EPS)
            self.memset(DVE, self.epsc[:, 1:2], 1.0)
            self.ld(self.cst[:], d["cst"][:, :])
            self.ld(self.cstb[:], d["cst"][:, :], eng=POOL)
            fw.barrier()
            self.phase_ada()
            for l in range(nl):
                self.phase_norm(l, 0)
                self.phase_win(l)
                self.phase_mix(l)
                self.phase_merge(l)
                self.phase_wout(l)
                self.phase_norm(l, 1)
                self.phase_gu(l)
                self.phase_down(l)
            self.phase_final()
            fw.emit()
        return nc

    def C(self, blk, bf=False):
        t = self.cstb if bf else self.cst
        return t[:, blk * 128:(blk + 1) * 128]

    def mv(self, l, cond, which):
        o = ((l * 2 + cond) * 4 + which) * KD
        return o

    def cond_of_tile(self, tt):
        return 0 if tt * 512 < self.TS else 1

    def phase_ada(self):
        d, fw, nl = self.d, self.fw, self.nl
        with contextlib.ExitStack() as st:
            condT = self.sb(st, "condT", [128, 2, KD], F32)
            scT = self.sb(st, "scT", [128, KD, 2], BF16)
            rep = self.sb(st, "rep", [128, 2, KD, 128], BF16)
            ones = self.sb(st, "ones_a", [128, 128], BF16)
            wg = [self.sb(st, "wg%d" % i, [128, KD, 512], BF16) for i in range(2)]
            bfm = self.sb(st, "bfm", [128, L, 48], F32)
            nw = self.sb(st, "nw", [128, L, 2, KD], F32)
            brow = [self.sb(st, "brow%d" % i, [128, 512], F32) for i in range(2)]
            gst_ = [self.sb(st, "gst%d" % i, [128, 512], F32) for i in range(2)]
            psA = [self.ps(st, "psA%d" % i, [128, 512]) for i in range(2)]
            psB = [self.ps(st, "psB%d" % i, [128, 8]) for i in range(2)]
            bfm2 = self.sb(st, "bfm2", [128, L, 48], F32)
            for c in range(2):
                self.ld(condT[:, c, :], d["cond"][c].rearrange("(k p) -> p k", p=128))
            for l_ in range(L):
                bv = d["b_ada"][l_].rearrange("(j p) -> p j", p=128)
                self.ld(bfm[:, l_, :], bv[:, 0:48])
                self.ld(bfm2[:, l_, :], bv[:, 48:96])
                self.ld(nw[:, l_, 0, :], d["norm1_w"][l_].rearrange("(k p) -> p k", p=128))
                self.ld(nw[:, l_, 1, :], d["norm2_w"][l_].rearrange("(k p) -> p k", p=128))
            scF = self.sb(st, "scF", [128, KD, 2], F32)
            self.act(scF[:], condT.v(condT.ap[:].rearrange("p c k -> p k c")), AF.Silu)
            self.cp(DVE, scT[:], scF[:])
            self.memset(DVE, ones[:], 1.0)
            for c in range(2):
                for k in range(KD):
                    self.ts(DVE, rep[:, c, k, :], ones[:], scF[:, k, c:c + 1], ALU.mult)
            gi = 0
            for l in range(nl):
                wv = d["w_ada"][l].rearrange("(k p) n -> p k n", p=128)
                for g in range(24):
                    w = wg[gi % 2]
                    self.ld(w[:], wv[:, :, g * 512:(g + 1) * 512], eng=POOL)
                    seg = g // 4
                    if seg in (2, 5):
                        for c in range(2):
                            p = psA[c]
                            self.mm(p[:], [(rep[:, c, k, :], w[:, k, :]) for k in range(KD)])
                            br = brow[c]
                            self.ld(br[:], d["b_ada"][l, g * 512:(g + 1) * 512].partition_broadcast(128))
                            gs = gst_[c]
                            self.tt(DVE, gs[:], p[:], br[:], ALU.add)
                            col = (g % 4) * 512
                            self.st(d["gb"][l, 0 if seg == 2 else 1, c:c + 1, col:col + 512], gs[0:1, :], [self.t_gb])
                    else:
                        which = {0: 0, 1: 1, 3: 2, 4: 3}[seg]
                        for j in range(4):
                            kc = (g % 4) * 4 + j
                            p = psB[j % 2]
                            self.mm(p[:, 0:2], [(w[:, k, j * 128:(j + 1) * 128], scT[:, k, :]) for k in range(KD)])
                            bsrc = bfm if seg < 3 else bfm2
                            bj = (seg * 16 + kc) if seg < 3 else ((seg - 3) * 16 + kc)
                            for c in range(2):
                                o = self.mv(l, c, which) + kc
                                dst = self.modv[:, o:o + 1]
                                if which in (0, 2):
                                    self.tt(DVE, dst, p[:, c:c + 1], bsrc[:, l, bj:bj + 1], ALU.add)
                                else:
                                    self.stt(DVE, dst, p[:, c:c + 1], 1.0, bsrc[:, l, bj:bj + 1], ALU.add, ALU.add)
                                    self.tt(DVE, dst, dst, nw[:, l, 0 if which == 1 else 1, kc:kc + 1], ALU.mult)
                    gi += 1
            if self.debug:
                self.st(d["modv_o"][:, :], self.modv[:], [self.t_out])
        fw.barrier()

    def norm_block(self, st_, xb, cond, l, which, hst, tb, bufs, i):
        junk, ss, xn, pt = bufs
        self.act(junk[:], xb, AF.Square, accum=ss[:, 0:1])
        self.act(ss[:, 1:2], ss[:, 0:1], AF.Sqrt, bias=self.epsc[:, 0:1], scale=1.0 / D)
        self.recip(ss[:, 2:3], ss[:, 1:2])
        self.ts(DVE, xn[:], xb, ss[:, 2:3], ALU.mult)
        for q in range(4):
            p = pt[(i * 4 + q) % len(pt)]
            for j in range(4):
                kc = q * 4 + j
                self.tr(p[:, j, :], xn[:, kc * 128:(kc + 1) * 128], self.C(C_ID, True))
            for j in range(4):
                kc = q * 4 + j
                osc = self.mv(l, cond, 1 + 2 * which) + kc
                osh = self.mv(l, cond, 0 + 2 * which) + kc
                dst = hst[:, kc, tb * 128:(tb + 1) * 128]
                if j % 2 == 0:
                    self.act(dst, p[:, j, :], AF.Identity, bias=self.modv[:, osh:osh + 1], scale=self.modv[:, osc:osc + 1])
                else:
                    self.ts(DVE, dst, p[:, j, :], self.modv[:, osc:osc + 1], ALU.mult, self.modv[:, osh:osh + 1], ALU.add)

    def phase_norm(self, l, which):
        d, fw = self.d, self.fw
        xsrc = d["x"] if (l == 0 and which == 0) else d["xres"]
        with contextlib.ExitStack() as st:
            xb = [self.sb(st, "xb%d" % i, [128, D], F32) for i in range(3)]
            junk = self.sb(st, "junk", [128, D], BF16)
            ss = [self.sb(st, "ss%d" % i, [128, 4], F32) for i in range(2)]
            xn = [self.sb(st, "xn%d" % i, [128, D], BF16) for i in range(2)]
            pt = [self.ps(st, "pt%d" % i, [128, 4, 128], BF16) for i in range(4)]
            hst = [self.sb(st, "hst%d" % i, [128, KD, 512], BF16) for i in range(2)]
            nb = self.NT // 128
            self.ld(xb[0][:], xsrc[0:128, :], [self.t_x[0]])
            for i in range(nb):
                if i + 1 < nb:
                    self.ld(xb[(i + 1) % 3][:], xsrc[(i + 1) * 128:(i + 2) * 128, :], [self.t_x[i + 1]])
                tt, tb = i // 4, i % 4
                h = hst[tt % 2]
                self.norm_block(st, xb[i % 3][:], self.cond_of_tile(tt), l, which, h, tb, (junk, ss[i % 2], xn[i % 2], pt), i)
                if tb == 3:
                    self.st(d["hT"][tt], h[:], [self.t_hT[tt]])
        fw.barrier()

    def wload(self, wt, src3):
        self.ld(wt, src3, eng=POOL)

    def phase_win(self, l):
        d, fw, NTT = self.d, self.fw, self.NTT
        wv = d["w_in"][l].rearrange("(k p) n -> p k n", p=128)
        S = 128 ** -0.5
        S64 = 64 ** -0.5
        groups = [
            (0, 1024, [(O_AQ + i * 128, 128, i, 1.0, "rope") for i in range(4)] + [(O_AK + i * 128, 128, 4 + i, 1.0, "rope") for i in range(2)],
             [(O_AV, 256, T_AV, AF.Identity, 1.0)], True),
            (1024, 2048, [(O_MQ + i * 128, 128, 6 + i, 1.0, "") for i in range(4)] + [(O_MK + i * 128, 128, 10 + i, S, "") for i in range(4)],
             [(O_MK, 512, T_MK, AF.Identity, S)], False),
            (2048, 3072, [], [(O_MV, 512, T_MV, AF.Identity, 1.0), (O_MO, 512, T_MO, AF.Sigmoid, 1.0)], False),
            (3072, 4112, [(O_RQ + i * 128, 128, 14 + i, 1.0, "") for i in range(4)] + [(O_RK + i * 128, 128, 18 + i, S, "") for i in range(4)],
             [(O_MG, 16, -1, AF.Identity, 1.0), (O_RK, 512, T_RK, AF.Identity, S)], False),
            (4112, 5136, [], [(O_RV, 512, T_RV, AF.Identity, 1.0), (O_RG, 512, T_RG, AF.Silu, 1.0)], False),
            (5136, 6160, [(O_GQ + i * 64, 64, 22 + i, S64, "") for i in range(4)] + [(O_GK + i * 64, 64, 26 + i, 1.0, "") for i in range(4)],
             [(O_GK, 256, T_GK, AF.Identity, 1.0), (O_GV, 512, T_GV, AF.Identity, 1.0)], False),
            (6160, 6704, [(O_GLR, 16, -2, 1.0, "glr"), (O_GLR + 16, 16, -3, 1.0, "glr")],
             [(O_GG, 512, T_GG, AF.Silu, 1.0)], False),
        ]
        with contextlib.ExitStack() as st:
            wb = [self.sb(st, "wb%d" % i, [128, KD, 1040], BF16) for i in range(2)]
            hb = [self.sb(st, "hb%d" % i, [128, KD, 512], BF16) for i in range(2)]
            zst = [self.sb(st, "zst%d" % i, [128, 8, 512], BF16) for i in range(2)]
            tst = [self.sb(st, "tst%d" % i, [128, 4, 1024], BF16) for i in range(2)]
            mgst = [self.sb(st, "mgst%d" % i, [128, 4, 16], F32) for i in range(2)]
            glst = [self.sb(st, "glst%d" % i, [16, 2, 512], F32) for i in range(2)]
            kvst = [self.sb(st, "kvst%d" % i, [128, 512], F32) for i in range(2)]
            rc = [self.sb(st, "rc%d" % i, [128, 512], F32) for i in range(2)]
            rs = [self.sb(st, "rs%d" % i, [128, 512], F32) for i in range(2)]
            qf = [self.sb(st, "qf%d" % i, [128, 512], F32) for i in range(2)]
            t1 = [self.sb(st, "t1%d" % i, [128, 512], F32) for i in range(2)]
            permf = self.sb(st, "permf", [128, 128], F32)
            pp = [self.ps(st, "pp%d" % i, [128, 512]) for i in range(6)]
            pq = [self.ps(st, "pq%d" % i, [128, 512]) for i in range(2)]
            self.ld(permf[:], d["perm"][:, :])
            self.wload(wb[0][:, :, 0:1024], wv[:, :, 0:1024])
            pi = 0
            it = 0
            for gi, (c0, c1, fmj, tmj, isattn) in enumerate(groups):
                w = wb[gi % 2]
                if gi + 1 < len(groups):
                    n0, n1 = groups[gi + 1][0], groups[gi + 1][1]
                    self.wload(wb[(gi + 1) % 2][:, :, 0:n1 - n0], wv[:, :, n0:n1])
                self.ld(hb[it % 2][:], d["hT"][0], [self.t_hT[0]])
                for tt in range(NTT):
                    h = hb[it % 2]
                    if tt + 1 < NTT:
                        self.ld(hb[(it + 1) % 2][:], d["hT"][tt + 1], [self.t_hT[tt + 1]])
                    is_s = self.cond_of_tile(tt) == 0
                    zs = zst[it % 2]
                    ts_ = tst[it % 2]
                    if isattn and is_s:
                        self.ld(rc[it % 2][:], d["ropeC"][:, tt * 512:(tt + 1) * 512])
                        self.ld(rs[it % 2][:], d["ropeS"][:, tt * 512:(tt + 1) * 512])
                    for ji, (col, rows, fmc, scale, kind) in enumerate(fmj):
                        p = pp[pi % 6]; pi += 1
                        cc = col - c0
                        self.mm(p[0:rows, :], [(w[:, k, cc:cc + rows], h[:, k, :]) for k in range(KD)])
                        if kind == "glr":
                            self.cp(DVE, glst[it % 2][:, -2 - fmc, :], p[0:16, :])
                        elif kind == "rope" and is_s:
                            q = qf[ji % 2]
                            self.cp(ACT, q[:], p[:])
                            p2 = pq[ji % 2]
                            self.mm(p2[:], [(permf[:], q[:])])
                            tq = t1[ji % 2]
                            self.tt(DVE, tq[:], q[:], rc[it % 2][:], ALU.mult)
                            self.tt(DVE, q[:], p2[:], rs[it % 2][:], ALU.mult)
                            self.tt(DVE, zs[:, ji, :], tq[:], q[:], ALU.add)
                        else:
                            self.ts(DVE, zs[0:rows, ji, :], p[0:rows, :], scale, ALU.mult)
                    if fmj and fmj[0][4] != "glr":
                        f0 = fmj[0][2]
                        self.st(d["zfm"][tt][:, f0:f0 + len(fmj), :], zs[:, 0:len(fmj), :], [self.t_z[tt]])
                    if fmj and fmj[0][4] == "glr":
                        self.st(d["glr"][:, :, tt * 512:(tt + 1) * 512].rearrange("a r t -> r a t"), glst[it % 2][:], [self.t_z[tt]])
                    off = 0
                    for (col, ncols, tmc, func, scale) in tmj:
                        cc = col - c0
                        for tb in range(4):
                            p = pp[pi % 6]; pi += 1
                            self.mm(p[:, 0:ncols], [(h[:, k, tb * 128:(tb + 1) * 128], w[:, k, cc:cc + ncols]) for k in range(KD)])
                            if tmc == -1:
                                self.cp(ACT, mgst[it % 2][:, tb, :], p[:, 0:16])
                            else:
                                self.act(ts_[:, tb, off:off + ncols], p[:, 0:ncols], func, scale=scale)
                        if tmc == -1:
                            self.st(d["mg"][tt * 512:(tt + 1) * 512, :].rearrange("(b p) c -> p b c", p=128), mgst[it % 2][:], [self.t_z[tt]])
                        else:
                            self.st(d["ztm"][tt * 512:(tt + 1) * 512, tmc:tmc + ncols].rearrange("(b p) c -> p b c", p=128),
                                    ts_[:, :, off:off + ncols], [self.t_z[tt]])
                            off += ncols
                    if isattn and not is_s:
                        cc = O_AK - c0
                        for tb in range(4):
                            p = pp[pi % 6]; pi += 1
                            self.mm(p[:], [(h[:, k, tb * 128:(tb + 1) * 128], w[:, k, cc:cc + 512]) for k in range(KD)])
                            kv = kvst[tb % 2]
                            self.cp(ACT, kv[:], p[:])
                            pr, r0 = (tb * 128) // self.TP, (tb * 128) % self.TP
                            self.st(d["o_ak"][pr, l, r0:r0 + 128, :], kv[:, 0:256], [self.t_out])
                            self.st(d["o_av"][pr, l, r0:r0 + 128, :], kv[:, 256:512], [self.t_out])
                    it += 1
        fw.barrier()

    def phase_merge(self, l):
        d, fw, NTT = self.d, self.fw, self.NTT
        with contextlib.ExitStack() as st:
            wg = [self.sb(st, "wmg%d" % i, [128, 4, KD, 256], BF16) for i in range(2)]
            wr = [self.sb(st, "wbr%d" % i, [128, 4, 4, 256], BF16) for i in range(2)]
            hb = [self.sb(st, "hb%d" % i, [128, KD, 512], BF16) for i in range(2)]
            yb = [self.sb(st, "yb%d" % i, [128, 16, 512], BF16) for i in range(2)]
            gsb = [self.sb(st, "gsb%d" % i, [128, 512], F32) for i in range(3)]
            acc = [self.sb(st, "macc%d" % i, [128, 512], F32) for i in range(2)]
            mst = [self.sb(st, "mst%d" % i, [128, 2, 512], BF16) for i in range(2)]
            pg = [self.ps(st, "pg%d" % i, [128, 512]) for i in range(4)]
            pb = [self.ps(st, "pb%d" % i, [128, 512]) for i in range(4)]

            def wl(g):
                c0 = g * 256
                for b in range(4):
                    self.wload(wg[g % 2][:, b, :, :], d["w_mgate"][l, b].rearrange("(k p) n -> p k n", p=128)[:, :, c0:c0 + 256])
                self.wload(wr[g % 2][:], d["w_br"][l].rearrange("b (k p) n -> p b k n", p=128)[:, :, :, c0:c0 + 256])
            wl(0)
            it = 0
            gi = 0
            for g in range(8):
                if g + 1 < 8:
                    wl(g + 1)
                self.ld(hb[it % 2][:], d["hT"][0], [self.t_hT[0]])
                self.ld(yb[it % 2][:], d["ysT"][0], [self.t_ys[0]])
                for tt in range(NTT):
                    h, y = hb[it % 2], yb[it % 2]
                    if tt + 1 < NTT:
                        self.ld(hb[(it + 1) % 2][:], d["hT"][tt + 1], [self.t_hT[tt + 1]])
                        self.ld(yb[(it + 1) % 2][:], d["ysT"][tt + 1], [self.t_ys[tt + 1]])
                    ms = mst[it % 2]
                    for j in range(2):
                        a = acc[j]
                        for b in range(4):
                            p1 = pg[gi % 4]; p2 = pb[gi % 4]
                            self.mm(p1[:], [(wg[g % 2][:, b, k, j * 128:(j + 1) * 128], h[:, k, :]) for k in range(KD)])
                            self.mm(p2[:], [(wr[g % 2][:, b, k, j * 128:(j + 1) * 128], y[:, b * 4 + k, :]) for k in range(4)])
                            gs = gsb[gi % 3]
                            self.act(gs[:], p1[:], AF.Sigmoid)
                            if b == 0:
                                self.tt(DVE, a[:], gs[:], p2[:], ALU.mult)
                            elif b < 3:
                                self.tt(DVE, gs[:], gs[:], p2[:], ALU.mult)
                                self.tt(POOL, a[:], a[:], gs[:], ALU.add)
                            else:
                                self.tt(DVE, gs[:], gs[:], p2[:], ALU.mult)
                                self.tt(POOL, ms[:, j, :], a[:], gs[:], ALU.add)
                            gi += 1
                    self.st(d["mT"][tt][:, g * 2:g * 2 + 2, :], ms[:], [self.t_mT[tt]])
                    it += 1
        fw.barrier()

    def phase_resid(self, l, wsrc, nk, asrc, t_a, gsel, name):
        d, fw, NTT = self.d, self.fw, self.NTT
        xsrc = d["x"] if (l == 0 and gsel == 0) else d["xres"]
        wv = wsrc.rearrange("(k p) n -> p k n", p=128)
        with contextlib.ExitStack() as st:
            wb = [self.sb(st, "rw%d" % i, [128, nk, 512], BF16) for i in range(2)]
            ab = [self.sb(st, "ra%d" % i, [128, nk, 256], BF16) for i in range(2)]
            xb = [self.sb(st, "rx%d" % i, [128, 2, 512], F32) for i in range(3)]
            gbt = self.sb(st, "rg", [128, 2, D], F32)
            tmp = [self.sb(st, "rt%d" % i, [128, 512], F32) for i in range(2)]
            pp = [self.ps(st, "rp%d" % i, [128, 512]) for i in range(4)]
            for c in range(2):
                self.ld(gbt[:, c, :], d["gb"][l, gsel, c, :].partition_broadcast(128), [self.t_gb])
            self.wload(wb[0][:], wv[:, :, 0:512])
            nh = NTT * 2
            it = 0
            pi = 0
            for g in range(4):
                w = wb[g % 2]
                if g + 1 < 4:
                    self.wload(wb[(g + 1) % 2][:], wv[:, :, (g + 1) * 512:(g + 2) * 512])
                self.ld(ab[it % 2][:], asrc[0][:, :, 0:256], [t_a[0]])
                for hh in range(nh):
                    tt, half = hh // 2, hh % 2
                    a = ab[it % 2]
                    if hh + 1 < nh:
                        self.ld(ab[(it + 1) % 2][:], asrc[(hh + 1) // 2][:, :, ((hh + 1) % 2) * 256:((hh + 1) % 2) * 256 + 256], [t_a[(hh + 1) // 2]])
                    x = xb[it % 3]
                    r0 = hh * 256
                    xin = xsrc
                    self.ld(x[:], xin[r0:r0 + 256, g * 512:(g + 1) * 512].rearrange("(b p) c -> p b c", p=128), [self.t_x[hh * 2], self.t_x[hh * 2 + 1]])
                    cond = self.cond_of_tile(tt)
                    for tb in range(2):
                        p = pp[pi % 4]; pi += 1
                        self.mm(p[:], [(a[:, k, tb * 128:(tb + 1) * 128], w[:, k, :]) for k in range(nk)])
                        tm_ = tmp[pi % 2]
                        self.tt(DVE, tm_[:], p[:], gbt[:, cond, g * 512:(g + 1) * 512], ALU.mult)
                        self.tt(POOL, x[:, tb, :], x[:, tb, :], tm_[:], ALU.add)
                    self.st(d["xres"][r0:r0 + 256, g * 512:(g + 1) * 512].rearrange("(b p) c -> p b c", p=128), x[:], [self.t_x[hh * 2], self.t_x[hh * 2 + 1]])
                    it += 1
        fw.barrier()

    def phase_wout(self, l):
        d, fw, NTT = self.d, self.fw, self.NTT
        xsrc = d["x"] if l == 0 else d["xres"]
        wv = d["w_out"][l].rearrange("(k p) n -> p k n", p=128)
        with contextlib.ExitStack() as st:
            wb = self.sb(st, "ow", [128, KD, D], BF16)
            ab = [self.sb(st, "oa%d" % i, [128, KD, 512], BF16) for i in range(2)]
            xb = [self.sb(st, "ox%d" % i, [128, D], F32) for i in range(3)]
            gbt = self.sb(st, "og", [128, 2, D], F32)
            tmp = [self.sb(st, "ot%d" % i, [128, 512], F32) for i in range(2)]
            pp = [self.ps(st, "op%d" % i, [128, 512]) for i in range(4)]
            for c in range(2):
                self.ld(gbt[:, c, :], d["gb"][l, 0, c, :].partition_broadcast(128), [self.t_gb])
            for g in range(4):
                self.wload(wb[:, :, g * 512:(g + 1) * 512], wv[:, :, g * 512:(g + 1) * 512])
            nb = self.NT // 128
            self.ld(ab[0][:], d["mT"][0], [self.t_mT[0]])
            self.ld(xb[0][:], xsrc[0:128, :], [self.t_x[0]])
            pi = 0
            for i in range(nb):
                tt, tb = i // 4, i % 4
                if tb == 0 and tt + 1 < NTT:
                    self.ld(ab[(tt + 1) % 2][:], d["mT"][tt + 1], [self.t_mT[tt + 1]])
                if i + 1 < nb:
                    self.ld(xb[(i + 1) % 3][:], xsrc[(i + 1) * 128:(i + 2) * 128, :], [self.t_x[i + 1]])
                a, x = ab[tt % 2], xb[i % 3]
                cond = self.cond_of_tile(tt)
                for g in range(4):
                    p = pp[pi % 4]; pi += 1
                    self.mm(p[:], [(a[:, k, tb * 128:(tb + 1) * 128], wb[:, k, g * 512:(g + 1) * 512]) for k in range(KD)])
                    tm_ = tmp[pi % 2]
                    self.tt(DVE, tm_[:], p[:], gbt[:, cond, g * 512:(g + 1) * 512], ALU.mult)
                    self.tt(POOL, x[:, g * 512:(g + 1) * 512], x[:, g * 512:(g + 1) * 512], tm_[:], ALU.add)
                self.st(d["xres"][i * 128:(i + 1) * 128, :], x[:], [self.t_x[i]])
        fw.barrier()

    def phase_down(self, l):
        self.phase_resid(l, self.d["ffn_w_down"][l], KF, self.d["aT"], self.t_aT, 1, "down")

    def phase_gu(self, l):
        d, fw, NTT = self.d, self.fw, self.NTT
        wv = d["ffn_w_gu"][l].rearrange("(k p) n -> p k n", p=128)
        with contextlib.ExitStack() as st:
            wb = [self.sb(st, "gw%d" % i, [128, KD, 1024], BF16) for i in range(2)]
            hb = [self.sb(st, "hb%d" % i, [128, KD, 512], BF16) for i in range(2)]
            sg = [self.sb(st, "sg%d" % i, [128, 512], F32) for i in range(3)]
            ast = [self.sb(st, "ast%d" % i, [128, 4, 512], BF16) for i in range(2)]
            pg = [self.ps(st, "pg%d" % i, [128, 512]) for i in range(4)]
            pu = [self.ps(st, "pu%d" % i, [128, 512]) for i in range(4)]

            def wl(g):
                self.wload(wb[g % 2][:, :, 0:512], wv[:, :, g * 512:(g + 1) * 512])
                self.wload(wb[g % 2][:, :, 512:1024], wv[:, :, FF + g * 512:FF + (g + 1) * 512])
            wl(0)
            it = 0
            pi = 0
            for g in range(11):
                w = wb[g % 2]
                if g + 1 < 11:
                    wl(g + 1)
                self.ld(hb[it % 2][:], d["hT"][0], [self.t_hT[0]])
                for tt in range(NTT):
                    h = hb[it % 2]
                    if tt + 1 < NTT:
                        self.ld(hb[(it + 1) % 2][:], d["hT"][tt + 1], [self.t_hT[tt + 1]])
                    a = ast[it % 2]
                    for j in range(4):
                        p1 = pg[pi % 4]; p2 = pu[pi % 4]; s = sg[pi % 3]; pi += 1
                        self.mm(p1[:], [(w[:, k, j * 128:(j + 1) * 128], h[:, k, :]) for k in range(KD)])
                        self.mm(p2[:], [(w[:, k, 512 + j * 128:512 + (j + 1) * 128], h[:, k, :]) for k in range(KD)])
                        self.act(s[:], p1[:], AF.Silu)
                        self.tt(DVE, a[:, j, :], s[:], p2[:], ALU.mult)
                    self.st(d["aT"][tt][:, g * 4:g * 4 + 4, :], a[:], [self.t_aT[tt]])
                    it += 1
        fw.barrier()

    def phase_final(self):
        d, fw = self.d, self.fw
        with contextlib.ExitStack() as st:
            xb = [self.sb(st, "fx%d" % i, [128, D], F32) for i in range(3)]
            junk = self.sb(st, "fjunk", [128, D], BF16)
            ss = [self.sb(st, "fss%d" % i, [128, 4], F32) for i in range(2)]
            fw_ = self.sb(st, "fnw", [128, D], F32)
            self.ld(fw_[:], d["final_norm_w"].partition_broadcast(128))
            nb = self.NT // 128
            self.ld(xb[0][:], d["xres"][0:128, :], [self.t_x[0]])
            for i in range(nb):
                if i + 1 < nb:
                    self.ld(xb[(i + 1) % 3][:], d["xres"][(i + 1) * 128:(i + 2) * 128, :], [self.t_x[i + 1]])
                x, s = xb[i % 3], ss[i % 2]
                self.act(junk[:], x[:], AF.Square, accum=s[:, 0:1])
                self.act(s[:, 1:2], s[:, 0:1], AF.Sqrt, bias=self.epsc[:, 0:1], scale=1.0 / D)
                self.recip(s[:, 2:3], s[:, 1:2])
                self.stt(DVE, x[:], x[:], s[:, 2:3], fw_[:], ALU.mult, ALU.mult)
                self.st(d["y"][i * 128:(i + 1) * 128, :], x[:], [self.t_out])
        fw.barrier()

    def phase_mix(self, l):
        _mix_impl(self, l)


def _mix_impl(self, l):
    fw = self.fw
    for (t0, Tn, pidx) in self.seqs:
        _attn(self, l, t0, Tn, pidx)
        fw.barrier()
        for h0 in (0, 2):
            (_mlstm if "m" in STREAMS else _mlstm_old)(self, l, t0, Tn, pidx, h0)
            fw.barrier()
            (_ret if "r" in STREAMS else _ret_old)(self, l, t0, Tn, pidx, h0)
            fw.barrier()
            (_gla if "g" in STREAMS else _gla_old)(self, l, t0, Tn, pidx, h0)
            fw.barrier()


def _tile_of(self, t0, blk):
    g = t0 // 128 + blk
    return g // 4, g % 4


def _ztiles(self, t0, Tn):
    return [self.t_z[tt] for tt in range(t0 // 512, (t0 + Tn - 1) // 512 + 1)]


def _to_ysT(self, y, br, h0, nh, tt, tb, t0, Tn, b, yst, ptr):
    d = self.d
    p = ptr[b % 2]
    for h in range(nh):
        self.tr(p[:, h, :], y[:, h * 128:(h + 1) * 128], self.C(C_ID, True))
    ys = yst[tt % 2]
    self.cp(ACT, ys[:, :, tb * 128:(tb + 1) * 128], p[:, 0:nh, :])
    last = (b == Tn // 128 - 1)
    if tb == 3 or last:
        tb0 = (t0 // 128) % 4 if (tt == (t0 // 128) // 4) else 0
        c0 = br * 4 + h0
        self.st(d["ysT"][tt][:, c0:c0 + nh, tb0 * 128:(tb + 1) * 128], ys[:, :, tb0 * 128:(tb + 1) * 128], [self.t_ys[tt]])


def _finish_branch(self, st, br, h0, l, t0, Tn, acc, gate_col, nw_name, nm, accB=None):
    d = self.d
    nblk = Tn // 128
    nwb = self.sb(st, nm + "nwb", [128, 256], F32)
    self.ld(nwb[:], d[nw_name][l, h0 * 128:h0 * 128 + 256].partition_broadcast(128))
    gt = [self.sb(st, nm + "gt%d" % i, [128, 256], BF16) for i in range(2)]
    ssq = [self.sb(st, nm + "ssq%d" % i, [128, 8], F32) for i in range(2)]
    junk = self.sb(st, nm + "fj", [128, 128], BF16)
    yb = [self.sb(st, nm + "yb%d" % i, [128, 256], BF16) for i in range(2)]
    yst = [self.sb(st, nm + "yst%d" % i, [128, 2, 512], BF16) for i in range(2)]
    bank = self.psbank(st, nm + "ptr", BF16)
    ptr = [self.sub(bank, i * 256, (i + 1) * 256, 2) for i in range(2)]
    def stage_a(b):
        tt, tb = _tile_of(self, t0, b)
        g = gt[b % 2]
        r0 = t0 + b * 128
        gc = gate_col + h0 * 128
        self.ld(g[:], d["ztm"][r0:r0 + 128, gc:gc + 256], [self.t_z[tt]])
        s = ssq[b % 2]
        for h in range(2):
            self.act(junk[:], acc[:, b, h * 128:(h + 1) * 128], AF.Square, accum=s[:, h:h + 1])
        self.act(s[:, 2:4], s[:, 0:2], AF.Sqrt, bias=self.epsc[:, 0:1], scale=1.0 / 128)
        self.recip(s[:, 4:6], s[:, 2:4])
        for h in range(2):
            self.ts(DVE, acc[:, b, h * 128:(h + 1) * 128], acc[:, b, h * 128:(h + 1) * 128], s[:, 4 + h:5 + h], ALU.mult)
        self.tt(POOL, acc[:, b, :], acc[:, b, :], nwb[:], ALU.mult)
        self.tt(DVE, yb[b % 2][:], acc[:, b, :], g[:], ALU.mult)

    stage_a(0)
    for b in range(nblk):
        if b + 1 < nblk:
            stage_a(b + 1)
        tt, tb = _tile_of(self, t0, b)
        _to_ysT(self, yb[b % 2], br, h0, 2, tt, tb, t0, Tn, b, yst, ptr)


def _scan_loads(self, st, nm, t0, Tn, fmchunks, tmcols, rows=128):
    d = self.d
    nblk = Tn // 128
    ntm = sum(n for _, n in tmcols)
    fm = self.sb(st, nm + "_fm", [128, sum(n for _, n in fmchunks), Tn], BF16)
    tm = self.sb(st, nm + "_tm", [128, nblk, ntm], BF16)
    for b4 in range(0, Tn, 512):
        n = min(512, Tn - b4)
        tt = (t0 + b4) // 512
        o = (t0 + b4) % 512
        fo = 0
        for (c0, cn) in fmchunks:
            self.ld(fm[0:rows, fo:fo + cn, b4:b4 + n], d["zfm"][tt][0:rows, c0:c0 + cn, o:o + n], [self.t_z[tt]])
            fo += cn
        nb4 = n // 128
        off = 0
        for (c0, cn) in tmcols:
            self.ld(tm[:, b4 // 128:b4 // 128 + nb4, off:off + cn],
                    d["ztm"][t0 + b4:t0 + b4 + n, c0:c0 + cn].rearrange("(b p) c -> p b c", p=128), [self.t_z[tt]])
            off += cn
    return fm, tm


def _attn(self, l, t0, Tn, pidx):
    d = self.d
    is_s = pidx < 0
    nblk = Tn // 128
    SC = 128 ** -0.5
    with contextlib.ExitStack() as st:
        qk = self.sb(st, "a_qk", [128, 6, Tn], BF16)
        v = self.sb(st, "a_v", [128, nblk, 256], BF16)
        snk = self.sb(st, "a_snk", [128, 8], F32)
        kc = self.sb(st, "a_kc", [128, 2, 256], BF16)
        vc = self.sb(st, "a_vc", [128, 2, 256], BF16)
        ktm = self.sb(st, "a_ktm", [128, 2, 256], BF16)
        ssb = [self.sb(st, "a_s%d" % i, [128, 384], F32) for i in range(2)]
        pb = [self.sb(st, "a_p%d" % i, [128, 640], BF16) for i in range(2)]
        pT = [self.sb(st, "a_pT%d" % i, [128, 5, 128], BF16) for i in range(2)]
        sm = [self.sb(st, "a_sm%d" % i, [128, 8], F32) for i in range(4)]
        y = [self.sb(st, "a_y%d" % i, [128, 512], BF16) for i in range(2)]
        yst = [self.sb(st, "a_yst%d" % i, [128, 4, 512], BF16) for i in range(2)]
        ps_s = [T(self.psbank(st, "a_pss%d" % i)) for i in range(2)]
        bk = self.psbank(st, "a_psc")
        ps_c = [self.sub(bk, i * 256, (i + 1) * 256) for i in range(2)]
        ps_t = self.sub(self.psbank(st, "a_pst", BF16), 0, 640, 5)
        bk = self.psbank(st, "a_pso")
        ps_o = [self.sub(bk, i * 128, (i + 1) * 128) for i in range(2)]
        bk = self.psbank(st, "a_ptr", BF16)
        ptr = [self.sub(bk, i * 512, (i + 1) * 512, 4) for i in range(2)]
        for b4 in range(0, Tn, 512):
            n = min(512, Tn - b4)
            tt = (t0 + b4) // 512
            o = (t0 + b4) % 512
            self.ld(qk[:, :, b4:b4 + n], d["zfm"][tt][:, 0:6, o:o + n], [self.t_z[tt]])
        self.ld(v[:], d["ztm"][t0:t0 + Tn, T_AV:T_AV + 256].rearrange("(b p) c -> p b c", p=128), _ztiles(self, t0, Tn))
        self.ld(snk[:, 0:4], d["attn_sink"][l, :].partition_broadcast(128))
        self.ts(DVE, snk[:, 4:8], snk[:, 0:4], -1.0, ALU.mult)
        if is_s:
            self.ld(ktm[:], d["cak"][l].rearrange("(b p) c -> p b c", p=128), eng=POOL)
            self.ld(vc[:], d["cav"][l].rearrange("(b p) c -> p b c", p=128), eng=POOL)
            for kb in range(2):
                for kv in range(2):
                    self.tr(ps_t[:, kv, :], ktm[:, kb, kv * 128:(kv + 1) * 128], self.C(C_ID, True))
                self.cp(DVE, kc[:, :, kb * 128:(kb + 1) * 128], ps_t[:, 0:2, :])
        it = 0
        for b in range(nblk):
            tt, tb = _tile_of(self, t0, b)
            yb = y[b % 2]
            for h in range(4):
                kv = h // 2
                q = qk[:, h, b * 128:(b + 1) * 128]
                smt = sm[it % 4]
                p = pb[it % 2]
                pss = ps_s[it % 2]
                if is_s:
                    k0 = max(0, b - 1)
                    k1 = min(nblk, b + 2)
                    nk = (k1 - k0) * 128
                    self.mm(pss[:, 0:nk], [(q, qk[:, 4 + kv, k0 * 128:k1 * 128])])
                    psc = ps_c[it % 2]
                    self.mm(psc[:], [(q, kc[:, kv, :])])
                    s = ssb[it % 2]
                    off = 0
                    if b > 0:
                        self.tt(DVE, s[:, 0:128], pss[:, 0:128], self.C(C_MP), ALU.add)
                        off = 128
                    self.cp(DVE, s[:, off:off + 128], pss[:, off:off + 128])
                    if b + 1 < nblk:
                        self.tt(DVE, s[:, off + 128:off + 256], pss[:, off + 128:off + 256], self.C(C_MN), ALU.add)
                    self.red(smt[:, 0:1], s[:, 0:nk], ALU.max)
                    self.red(smt[:, 1:2], psc[:], ALU.max)
                    self.tt(DVE, smt[:, 0:1], smt[:, 0:1], smt[:, 1:2], ALU.max)
                    self.ts(DVE, smt[:, 2:3], smt[:, 0:1], -SC, ALU.mult, snk[:, 4 + h:5 + h], ALU.min)
                    self.act(p[:, 0:nk], s[:, 0:nk], AF.Exp, bias=smt[:, 2:3], scale=SC, accum=smt[:, 3:4])
                    self.act(p[:, nk:nk + 256], psc[:], AF.Exp, bias=smt[:, 2:3], scale=SC, accum=smt[:, 4:5])
                    self.act(smt[:, 5:6], snk[:, h:h + 1], AF.Exp, bias=smt[:, 2:3])
                    self.tt(DVE, smt[:, 3:4], smt[:, 3:4], smt[:, 4:5], ALU.add)
                    self.tt(DVE, smt[:, 3:4], smt[:, 3:4], smt[:, 5:6], ALU.add)
                    ntot = nk + 256
                    vlist = [v[:, kb, kv * 128:(kv + 1) * 128] for kb in range(k0, k1)] + [vc[:, kb, kv * 128:(kv + 1) * 128] for kb in range(2)]
                else:
                    self.mm(pss[:, 0:Tn], [(q, qk[:, 4 + kv, :])])
                    self.red(smt[:, 0:1], pss[:, 0:Tn], ALU.max)
                    self.ts(DVE, smt[:, 2:3], smt[:, 0:1], -SC, ALU.mult, snk[:, 4 + h:5 + h], ALU.min)
                    self.act(p[:, 0:Tn], pss[:, 0:Tn], AF.Exp, bias=smt[:, 2:3], scale=SC, accum=smt[:, 3:4])
                    self.act(smt[:, 5:6], snk[:, h:h + 1], AF.Exp, bias=smt[:, 2:3])
                    self.tt(DVE, smt[:, 3:4], smt[:, 3:4], smt[:, 5:6], ALU.add)
                    ntot = Tn
                    vlist = [v[:, kb, kv * 128:(kv + 1) * 128] for kb in range(nblk)]
                self.recip(smt[:, 6:7], smt[:, 3:4])
                nkb = ntot // 128
                for kb in range(nkb):
                    self.tr(ps_t[:, kb, :], p[:, kb * 128:(kb + 1) * 128], self.C(C_ID, True))
                ptt = pT[it % 2]
                self.cp(ACT, ptt[:, 0:nkb, :], ps_t[:, 0:nkb, :])
                pso = ps_o[it % 2]
                self.mm(pso[:], [(ptt[:, kb, :], vlist[kb]) for kb in range(nkb)])
                self.ts(DVE, yb[:, h * 128:(h + 1) * 128], pso[:], smt[:, 6:7], ALU.mult)
                it += 1
            _to_ysT(self, yb, 0, 0, 4, tt, tb, t0, Tn, b, yst, ptr)


def _run_streams(gens):
    gens = list(gens)
    if os.environ.get("KSEQ"):
        for g in gens:
            for _ in g:
                pass
        return
    while gens:
        for g in list(gens):
            try:
                next(g)
            except StopIteration:
                gens.remove(g)


def _mlstm(self, l, t0, Tn, pidx, h0):
    d = self.d
    is_s = pidx < 0
    nblk = Tn // 128
    with contextlib.ExitStack() as st:
        acc = self.sb(st, "m_acc", [128, nblk, 256], F32)
        accB = self.sb(st, "m_accB", [128, nblk, 256], F32)
        with contextlib.ExitStack() as st2:
            fm, tm = _scan_loads(self, st2, "m", t0, Tn, [(6 + h0, 2), (10 + h0, 2)], [(T_MK + h0 * 128, 256), (T_MV + h0 * 128, 256)])
            G = self.sb(st2, "m_G", [128, nblk, 16], F32)
            gb = self.sb(st2, "m_gb", [128, 16], F32)
            lf = self.sb(st2, "m_lf", [128, nblk, 8], F32)
            ig = self.sb(st2, "m_ig", [128, nblk, 8], F32)
            bb = self.sb(st2, "m_b", [128, nblk, 8], F32)
            bl = self.sb(st2, "m_bl", [128, nblk, 8], F32)
            ew = self.sb(st2, "m_ew", [128, nblk, 8], F32)
            wn = self.sb(st2, "m_wn", [128, nblk, 8], F32)
            enb = self.sb(st2, "m_enb", [128, nblk, 8], F32)
            ebl = self.sb(st2, "m_ebl", [128, nblk, 8], F32)
            tmpg = self.sb(st2, "m_tmpg", [128, nblk, 8], F32)
            vaug = self.sb(st2, "m_vaug", [128, nblk, 2, 129], BF16)
            em0 = self.sb(st2, "m_em0", [128, 16], F32)
            bk0 = self.psbank(st2, "m_ps0")
            ps_g = self.sub(bk0, 0, 16)
            self.ld(G[:], d["mg"][t0:t0 + Tn, :].rearrange("(b p) c -> p b c", p=128), _ztiles(self, t0, Tn))
            self.ld(gb[:], d["mlstm_if_b"][l, :].partition_broadcast(128))
            for b in range(nblk):
                self.tt(DVE, G[:, b, :], G[:, b, :], gb[:], ALU.add)
                for dd in range(2):
                    self.cp(DVE, ig[:, b, dd * 4:dd * 4 + 4], G[:, b, dd * 8:dd * 8 + 4])
                    self.act(lf[:, b, dd * 4:dd * 4 + 4], G[:, b, dd * 8 + 4:dd * 8 + 8], AF.Exp, scale=-1.0)
            self.act(lf[:], lf[:], AF.Ln, bias=self.epsc[:, 1:2])
            self.ts(DVE, lf[:], lf[:], -1.0, ALU.mult)
            for b in range(nblk):
                self.mm(ps_g[:, 0:4], [(self.C(C_LE), lf[:, b, 0:4])])
                self.mm(ps_g[:, 4:8], [(self.C(C_GE), lf[:, b, 4:8])])
                self.mm(ps_g[:, 8:16], [(self.C(C_ONE), lf[:, b, :])])
                self.cp(DVE, bb[:, b, :], ps_g[:, 0:8])
                self.cp(DVE, bl[:, b, :], ps_g[:, 8:16])
            self.tt(DVE, tmpg[:], ig[:], bb[:], ALU.subtract)
            self.act(ew[:], tmpg[:], AF.Exp)
            self.tt(DVE, tmpg[:], tmpg[:], bl[:], ALU.add)
            self.act(wn[:], tmpg[:], AF.Exp)
            self.act(enb[:], bb[:], AF.Exp, scale=-1.0)
            self.act(ebl[:], bl[:], AF.Exp)
            self.memset(POOL, vaug[:], 1.0)
            for b in range(nblk):
                self.cp(POOL, vaug[:, b, :, 0:128], tm.v(tm.ap[:, b, 256:512].rearrange("p (h e) -> p h e", h=2)))
            if not is_s:
                assert nblk == 2
                mcol = self.sb(st2, "m_mcol", [8, 4], F32)
                gT = self.sb(st2, "m_gT", [8, 2], F32)
                blc = self.sb(st2, "m_blc", [8, 2], F32)
                mF = self.sb(st2, "m_mF", [8, 2], F32)
                mrep = self.sb(st2, "m_mrep", [8, 128], F32)
                ps_t = T(bk0[0:8, 16:144])
                ps_c = T(bk0[0:8, 144:146])
                ps_b = self.sub(bk0, 160, 168)
                for b in range(nblk):
                    self.tr(ps_t[:], tmpg[:, b, :], self.C(C_ID))
                    self.red(gT[:, b:b + 1], ps_t[:], ALU.max)
                    self.mm(ps_c[:, b:b + 1], [(lf[:, b, :], self.C(C_ONE)[:, 0:1])])
                self.cp(DVE, blc[:], ps_c[:])
                for (col, b0, b1) in ((0, 0, 1), (1, 1, 0)):
                    self.tt(DVE, mcol[:, 0:1], blc[:, b0:b0 + 1], gT[:, b0:b0 + 1], ALU.max)
                    self.tt(DVE, mcol[:, 1:2], mcol[:, 0:1], blc[:, b1:b1 + 1], ALU.add)
                    self.tt(DVE, mF[:, col:col + 1], mcol[:, 1:2], gT[:, b1:b1 + 1], ALU.max)
                self.tt(DVE, mF[:, 0:1], mF[:, 0:1], self.C(C_Z)[0:8, 2:3], ALU.mult)
                self.tt(DVE, mF[:, 1:2], mF[:, 1:2], self.C(C_Z)[0:8, 3:4], ALU.mult)
                self.tt(DVE, mcol[:, 2:3], mF[:, 0:1], mF[:, 1:2], ALU.add)
                if h0 == 0:
                    self.st(d["o_mm"][pidx, l, :].rearrange("(p o) -> p o", o=1), mcol[:, 2:3], [self.t_out])
                self.ts(DVE, mrep[:], self.C(C_ONE)[0:8, :], mcol[:, 2:3], ALU.mult)
                self.mm(ps_b[:], [(mrep[:], self.C(C_ID)[0:8, 0:8])])
                self.act(em0[:, 8:16], ps_b[:], AF.Exp, scale=-1.0)
            else:
                self.ld(em0[:, 0:8], d["smm"][l, :].partition_broadcast(128))
                self.act(em0[:, 0:8], em0[:, 0:8], AF.Exp)

            def stream(dd, h):
                j = dd * 4 + h0 + h
                S = self.sb(st2, "m_S%d" % j, [128, 129], F32)
                Sb = [self.sb(st2, "m_Sb%d_%d" % (j, i), [128, 129], BF16) for i in range(2)]
                P = [self.sb(st2, "m_P%d_%d" % (j, i), [128, 128], BF16) for i in range(2)]
                kw = [self.sb(st2, "m_kw%d_%d" % (j, i), [128, 128], BF16) for i in range(2)]
                dn = [self.sb(st2, "m_dn%d_%d" % (j, i), [128, 4], F32) for i in range(2)]
                bk = self.psbank(st2, "m_psS%d" % j)
                pa, po, pS = self.sub(bk, 0, 128), self.sub(bk, 128, 257), self.sub(bk, 320, 449)
                A = acc if dd == 0 else accB
                if is_s:
                    self.ld(S[:, 0:128], d["smC"][l, dd, h0 + h])
                    self.ld(S[:, 128:129], d["smn"][l, dd, h0 + h].rearrange("(k o) -> k o", o=1))
                    self.ts(DVE, S[:], S[:], em0[:, j:j + 1], ALU.mult)
                else:
                    self.memset(DVE, S[:], 0.0)
                self.cp(ACT, Sb[0][:], S[:])
                yield
                order = list(range(nblk)) if dd == 0 else list(range(nblk - 1, -1, -1))
                mask = self.C(C_LE) if dd == 0 else self.C(C_GE)
                for bi, b in enumerate(order):
                    sl = slice(b * 128, (b + 1) * 128)
                    Pm, kwt, dnt = P[bi % 2], kw[bi % 2], dn[bi % 2]
                    sb_cur, sb_nxt = Sb[bi % 2], Sb[(bi + 1) % 2]
                    self.mm(pa[:], [(fm[:, 2 + h, sl], fm[:, h, sl])])
                    self.ts(POOL, kwt[:], tm[:, b, h * 128:(h + 1) * 128], wn[:, b, j:j + 1], ALU.mult)
                    yield
                    self.stt(DVE, Pm[:], pa[:], ew[:, b, j:j + 1], mask, ALU.mult, ALU.mult)
                    self.mm(pS[:], [(kwt[:], vaug[:, b, h, :])])
                    yield
                    self.mm(po[:], [(Pm[:], vaug[:, b, h, :]), (fm[:, h, sl], sb_cur[:])])
                    self.stt(DVE, S[:], S[:], ebl[:, b, j:j + 1], pS[:], ALU.mult, ALU.add)
                    yield
                    self.act(dnt[:, 0:1], po[:, 128:129], AF.Abs)
                    self.cp(ACT, sb_nxt[:], S[:])
                    yield
                    self.tt(DVE, dnt[:, 0:1], dnt[:, 0:1], enb[:, b, j:j + 1], ALU.max)
                    self.recip(dnt[:, 1:2], dnt[:, 0:1])
                    yield
                    self.act(A[:, b, h * 128:(h + 1) * 128], po[:, 0:128], AF.Identity, scale=dnt[:, 1:2])
                    yield
                if not is_s:
                    self.ts(DVE, S[:], S[:], em0[:, 8 + j:9 + j], ALU.mult)
                    self.st(d["o_mC"][pidx, l, dd, h0 + h], S[:, 0:128], [self.t_out])
                    self.st(d["o_mn"][pidx, l, dd, h0 + h].rearrange("(k o) -> k o", o=1), S[:, 128:129], [self.t_out])
            _run_streams([stream(dd, h) for dd in range(2) for h in range(2)])
        self.fw.barrier()
        with contextlib.ExitStack() as st3:
            _finish_branch(self, st3, 1, h0, l, t0, Tn, acc, T_MO, "mlstm_norm_w", "mB", accB)


def _ret(self, l, t0, Tn, pidx, h0):
    d = self.d
    is_s = pidx < 0
    nblk = Tn // 128
    with contextlib.ExitStack() as st:
        acc = self.sb(st, "r_acc", [128, nblk, 256], F32)
        accB = self.sb(st, "r_accB", [128, nblk, 256], F32)
        with contextlib.ExitStack() as st2:
            fm, tm = _scan_loads(self, st2, "r", t0, Tn, [(14 + h0, 2), (18 + h0, 2)], [(T_RK + h0 * 128, 256), (T_RV + h0 * 128, 256)])
            lg = self.sb(st2, "r_lg", [128, 8], F32)
            gc = self.sb(st2, "r_gc", [128, 8], F32)
            self.ld(lg[:], d["ret_decay"][l, :].partition_broadcast(128))
            self.act(lg[:], lg[:], AF.Exp, scale=-1.0)
            self.act(lg[:], lg[:], AF.Ln, bias=self.epsc[:, 1:2])
            self.ts(DVE, lg[:], lg[:], -1.0, ALU.mult)
            self.act(gc[:], lg[:], AF.Exp, scale=128.0)

            def stream(dd, h):
                j = dd * 4 + h0 + h
                M = self.sb(st2, "r_M%d" % j, [128, 128], F32)
                xi = self.sb(st2, "r_xi%d" % j, [128, 128], F32)
                ze = self.sb(st2, "r_ze%d" % j, [128, 1], F32)
                S = self.sb(st2, "r_S%d" % j, [128, 128], F32)
                Sb = [self.sb(st2, "r_Sb%d_%d" % (j, i), [128, 128], BF16) for i in range(2)]
                P = [self.sb(st2, "r_P%d_%d" % (j, i), [128, 128], BF16) for i in range(2)]
                qx = [self.sb(st2, "r_qx%d_%d" % (j, i), [128, 128], BF16) for i in range(2)]
                kz = [self.sb(st2, "r_kz%d_%d" % (j, i), [128, 128], BF16) for i in range(2)]
                bk = self.psbank(st2, "r_psS%d" % j)
                pa, po, pS = self.sub(bk, 0, 128), self.sub(bk, 128, 256), self.sub(bk, 256, 384)
                A = acc if dd == 0 else accB
                self.act(M[:], self.C(C_DF if dd == 0 else C_DB), AF.Exp, scale=lg[:, j:j + 1])
                self.act(xi[:], self.C(C_PF if dd == 0 else C_PB), AF.Exp, scale=lg[:, j:j + 1])
                self.act(ze[:], self.C(C_Z)[:, dd:dd + 1], AF.Exp, scale=lg[:, j:j + 1])
                if is_s:
                    self.ld(S[:], d["srS"][l, dd, h0 + h])
                else:
                    self.memset(DVE, S[:], 0.0)
                self.cp(ACT, Sb[0][:], S[:])
                yield
                order = list(range(nblk)) if dd == 0 else list(range(nblk - 1, -1, -1))
                for bi, b in enumerate(order):
                    sl = slice(b * 128, (b + 1) * 128)
                    Pm, q_, kzt = P[bi % 2], qx[bi % 2], kz[bi % 2]
                    sb_cur, sb_nxt = Sb[bi % 2], Sb[(bi + 1) % 2]
                    self.mm(pa[:], [(fm[:, 2 + h, sl], fm[:, h, sl])])
                    self.ts(POOL, kzt[:], tm[:, b, h * 128:(h + 1) * 128], ze[:, 0:1], ALU.mult)
                    self.tt(POOL, q_[:], fm[:, h, sl], xi[:], ALU.mult)
                    yield
                    self.tt(DVE, Pm[:], pa[:], M[:], ALU.mult)
                    self.mm(pS[:], [(kzt[:], tm[:, b, 256 + h * 128:256 + (h + 1) * 128])])
                    yield
                    self.mm(po[:], [(Pm[:], tm[:, b, 256 + h * 128:256 + (h + 1) * 128]), (q_[:], sb_cur[:])])
                    self.stt(DVE, S[:], S[:], gc[:, j:j + 1], pS[:], ALU.mult, ALU.add)
                    yield
                    self.cp(ACT, A[:, b, h * 128:(h + 1) * 128], po[:])
                    self.cp(ACT, sb_nxt[:], S[:])
                    yield
                if not is_s:
                    self.st(d["o_rS"][pidx, l, dd, h0 + h], S[:], [self.t_out])
            _run_streams([stream(dd, h) for dd in range(2) for h in range(2)])
        self.fw.barrier()
        with contextlib.ExitStack() as st3:
            _finish_branch(self, st3, 2, h0, l, t0, Tn, acc, T_RG, "ret_norm_w", "rB", accB)


def _gla(self, l, t0, Tn, pidx, h0):
    d = self.d
    is_s = pidx < 0
    nblk = Tn // 128
    with contextlib.ExitStack() as st:
        acc = self.sb(st, "g_acc", [128, nblk, 256], F32)
        accB = self.sb(st, "g_accB", [128, nblk, 256], F32)
        with contextlib.ExitStack() as st2:
            fm, tm = _scan_loads(self, st2, "g", t0, Tn, [(22 + h0, 2), (26 + h0, 2)], [(T_GK + h0 * 64, 128), (T_GV + h0 * 128, 256)], rows=64)
            lr = self.sb(st2, "g_lr", [16, Tn], F32)
            w2 = self.sb(st2, "g_w2", [16, 2, 256], F32)
            bg = self.sb(st2, "g_bg", [1, 2, 256], F32)
            la = self.sb(st2, "g_la", [128, 2, nblk, 128], F32)
            bkl = self.psbank(st2, "g_psl")
            ps_l = [self.sub(bkl, i * 128, (i + 1) * 128) for i in range(4)]
            bkc = self.psbank(st2, "g_psc")
            self.ld(w2[:], d["gla_w2"][l].rearrange("a r c -> r a c"))
            self.ld(bg[:], d["gla_b"][l:l + 1, :, :])
            c0 = h0 * 64
            for dd in range(2):
                self.ld(lr[:], d["glr"][dd, :, t0:t0 + Tn], _ztiles(self, t0, Tn))
                for b in range(nblk):
                    sl = slice(b * 128, (b + 1) * 128)
                    p = ps_l[b % 4]
                    self.mm(p[:], [(lr[:, sl], w2[:, dd, c0:c0 + 128]), (self.C(C_ONE)[0:1, :], bg[:, dd, c0:c0 + 128])])
                    self.act(la[:, dd, b, :], p[:], AF.Exp, scale=-1.0)
            self.act(la[:], la[:], AF.Ln, bias=self.epsc[:, 1:2])

            def stream(dd, h, si):
                j = dd * 4 + h0 + h
                S = self.sb(st2, "g_S%d" % j, [64, 128], F32)
                Sb = [self.sb(st2, "g_Sb%d_%d" % (j, i), [64, 128], BF16) for i in range(2)]
                eq = [self.sb(st2, "g_eq%d_%d" % (j, i), [64, 128], F32) for i in range(2)]
                ek = [self.sb(st2, "g_ek%d_%d" % (j, i), [64, 128], F32) for i in range(2)]
                qp = [self.sb(st2, "g_qp%d_%d" % (j, i), [64, 128], BF16) for i in range(2)]
                kp = [self.sb(st2, "g_kp%d_%d" % (j, i), [64, 128], BF16) for i in range(2)]
                ekk = [self.sb(st2, "g_ekk%d_%d" % (j, i), [128, 64], F32) for i in range(2)]
                kpp = [self.sb(st2, "g_kpp%d_%d" % (j, i), [128, 64], BF16) for i in range(2)]
                P = [self.sb(st2, "g_P%d_%d" % (j, i), [128, 128], BF16) for i in range(2)]
                bk = self.psbank(st2, "g_psS%d" % j)
                pr, pa, po = self.sub(bk, 0, 64), self.sub(bk, 64, 192), self.sub(bk, 192, 320)
                pS = T(bk[0:64, 320:448])
                pc = T(bkc[0:64, si * 128:(si + 1) * 128])
                A = acc if dd == 0 else accB
                if is_s:
                    self.ld(S[:], d["sgS"][l, dd, h0 + h])
                else:
                    self.memset(DVE, S[:], 0.0)
                self.cp(ACT, Sb[0][:], S[:])
                yield
                order = list(range(nblk)) if dd == 0 else list(range(nblk - 1, -1, -1))
                tri = self.C(C_LE) if dd == 0 else self.C(C_GE)
                trs = self.C(C_GT) if dd == 0 else self.C(C_LT)
                for bi, b in enumerate(order):
                    sl = slice(b * 128, (b + 1) * 128)
                    i2 = bi % 2
                    sb_cur, sb_nxt = Sb[i2], Sb[(bi + 1) % 2]
                    lah = la[:, dd, b, h * 64:(h + 1) * 64]
                    self.mm(pc[:], [(lah, tri)])
                    self.mm(pr[:], [(trs, lah)])
                    yield
                    self.act(eq[i2][:], pc[:], AF.Exp, scale=-1.0 / 16)
                    self.act(ek[i2][:], pc[:], AF.Exp, scale=1.0 / 16)
                    self.act(ekk[i2][:], pr[:], AF.Exp, scale=-1.0 / 16)
                    yield
                    self.tt(DVE, qp[i2][:], fm[0:64, h, sl], eq[i2][:], ALU.mult)
                    self.tt(POOL, kp[i2][:], fm[0:64, 2 + h, sl], ek[i2][:], ALU.mult)
                    self.tt(DVE, kpp[i2][:], tm[:, b, h * 64:(h + 1) * 64], ekk[i2][:], ALU.mult)
                    yield
                    self.mm(pa[:], [(kp[i2][:], qp[i2][:])])
                    self.mm(pS[:], [(kpp[i2][:], tm[:, b, 128 + h * 128:128 + (h + 1) * 128])])
                    yield
                    self.tt(DVE, P[i2][:], pa[:], tri, ALU.mult)
                    lastc = eq[i2][:, 127:128] if dd == 0 else eq[i2][:, 0:1]
                    self.stt(DVE, S[:], S[:], lastc, pS[:], ALU.mult, ALU.add)
                    yield
                    self.mm(po[:], [(P[i2][:], tm[:, b, 128 + h * 128:128 + (h + 1) * 128]), (qp[i2][:], sb_cur[:])])
                    self.cp(ACT, sb_nxt[:], S[:])
                    yield
                    self.cp(ACT, A[:, b, h * 128:(h + 1) * 128], po[:])
                    yield
                if not is_s:
                    self.st(d["o_gS"][pidx, l, dd, h0 + h], S[:], [self.t_out])
            _run_streams([stream(dd, h, dd * 2 + h) for dd in range(2) for h in range(2)])
        self.fw.barrier()
        with contextlib.ExitStack() as st3:
            _finish_branch(self, st3, 3, h0, l, t0, Tn, acc, T_GG, "gla_norm_w", "gB", accB)


def _mlstm_old(self, l, t0, Tn, pidx, h0):
    d = self.d
    is_s = pidx < 0
    nblk = Tn // 128
    with contextlib.ExitStack() as st:
        acc = self.sb(st, "m_acc", [128, nblk, 256], F32)
        with contextlib.ExitStack() as st2:
            fm, tm = _scan_loads(self, st2, "m", t0, Tn, [(6 + h0, 2), (10 + h0, 2)], [(T_MK + h0 * 128, 256), (T_MV + h0 * 128, 256)])
            G = self.sb(st2, "m_G", [128, nblk, 16], F32)
            gb = self.sb(st2, "m_gb", [128, 16], F32)
            lf = self.sb(st2, "m_lf", [128, nblk, 8], F32)
            ig = self.sb(st2, "m_ig", [128, nblk, 8], F32)
            bb = self.sb(st2, "m_b", [128, nblk, 8], F32)
            bl = self.sb(st2, "m_bl", [128, nblk, 8], F32)
            ew = self.sb(st2, "m_ew", [128, nblk, 8], F32)
            wn = self.sb(st2, "m_wn", [128, nblk, 8], F32)
            enb = self.sb(st2, "m_enb", [128, nblk, 8], F32)
            ebl = self.sb(st2, "m_ebl", [128, nblk, 8], F32)
            tmpg = self.sb(st2, "m_tmpg", [128, nblk, 8], F32)
            vaug = self.sb(st2, "m_vaug", [128, nblk, 2, 129], BF16)
            S = self.sb(st2, "m_S", [128, 8, 129], F32)
            Sb = [self.sb(st2, "m_Sb%d" % i, [128, 8, 129], BF16) for i in range(2)]
            em0 = self.sb(st2, "m_em0", [128, 16], F32)
            P = [self.sb(st2, "m_P%d" % i, [128, 128], BF16) for i in range(3)]
            kw = [self.sb(st2, "m_kw%d" % i, [128, 128], BF16) for i in range(3)]
            dn = [self.sb(st2, "m_dn%d" % i, [128, 4], F32) for i in range(4)]
            bk0 = self.psbank(st2, "m_ps0")
            ps_g = self.sub(bk0, 0, 16)
            bk = self.psbank(st2, "m_psa")
            ps_a1 = self.sub(bk, 0, 256, 2)
            ps_a = [ps_a1, ps_a1]
            ps_o = [self.sub(self.psbank(st2, "m_pso%d" % i), 0, 256) for i in range(2)]
            ps_s = [self.sub(self.psbank(st2, "m_pss%d" % i), 0, 256) for i in range(2)]
            self.ld(G[:], d["mg"][t0:t0 + Tn, :].rearrange("(b p) c -> p b c", p=128), _ztiles(self, t0, Tn))
            self.ld(gb[:], d["mlstm_if_b"][l, :].partition_broadcast(128))
            for b in range(nblk):
                self.tt(DVE, G[:, b, :], G[:, b, :], gb[:], ALU.add)
                for dd in range(2):
                    self.cp(DVE, ig[:, b, dd * 4:dd * 4 + 4], G[:, b, dd * 8:dd * 8 + 4])
                    self.act(lf[:, b, dd * 4:dd * 4 + 4], G[:, b, dd * 8 + 4:dd * 8 + 8], AF.Exp, scale=-1.0)
            self.act(lf[:], lf[:], AF.Ln, bias=self.epsc[:, 1:2])
            self.ts(DVE, lf[:], lf[:], -1.0, ALU.mult)
            for b in range(nblk):
                self.mm(ps_g[:, 0:4], [(self.C(C_LE), lf[:, b, 0:4])])
                self.mm(ps_g[:, 4:8], [(self.C(C_GE), lf[:, b, 4:8])])
                self.mm(ps_g[:, 8:16], [(self.C(C_ONE), lf[:, b, :])])
                self.cp(DVE, bb[:, b, :], ps_g[:, 0:8])
                self.cp(DVE, bl[:, b, :], ps_g[:, 8:16])
            self.tt(DVE, tmpg[:], ig[:], bb[:], ALU.subtract)
            self.act(ew[:], tmpg[:], AF.Exp)
            self.tt(DVE, tmpg[:], tmpg[:], bl[:], ALU.add)
            self.act(wn[:], tmpg[:], AF.Exp)
            self.act(enb[:], bb[:], AF.Exp, scale=-1.0)
            self.act(ebl[:], bl[:], AF.Exp)
            self.memset(POOL, vaug[:], 1.0)
            for b in range(nblk):
                self.cp(POOL, vaug[:, b, :, 0:128], tm.v(tm.ap[:, b, 256:512].rearrange("p (h e) -> p h e", h=2)))
            if not is_s:
                assert nblk == 2
                mcol = self.sb(st2, "m_mcol", [8, 4], F32)
                gT = self.sb(st2, "m_gT", [8, 2], F32)
                blc = self.sb(st2, "m_blc", [8, 2], F32)
                mF = self.sb(st2, "m_mF", [8, 2], F32)
                mrep = self.sb(st2, "m_mrep", [8, 128], F32)
                ps_t = T(bk0[0:8, 16:144])
                ps_c = T(bk0[0:8, 144:146])
                ps_b = self.sub(bk0, 160, 168)
                for b in range(nblk):
                    self.tr(ps_t[:], tmpg[:, b, :], self.C(C_ID))
                    self.red(gT[:, b:b + 1], ps_t[:], ALU.max)
                    self.mm(ps_c[:, b:b + 1], [(lf[:, b, :], self.C(C_ONE)[:, 0:1])])
                self.cp(DVE, blc[:], ps_c[:])
                for (col, b0, b1) in ((0, 0, 1), (1, 1, 0)):
                    self.tt(DVE, mcol[:, 0:1], blc[:, b0:b0 + 1], gT[:, b0:b0 + 1], ALU.max)
                    self.tt(DVE, mcol[:, 1:2], mcol[:, 0:1], blc[:, b1:b1 + 1], ALU.add)
                    self.tt(DVE, mF[:, col:col + 1], mcol[:, 1:2], gT[:, b1:b1 + 1], ALU.max)
                self.tt(DVE, mF[:, 0:1], mF[:, 0:1], self.C(C_Z)[0:8, 2:3], ALU.mult)
                self.tt(DVE, mF[:, 1:2], mF[:, 1:2], self.C(C_Z)[0:8, 3:4], ALU.mult)
                self.tt(DVE, mcol[:, 2:3], mF[:, 0:1], mF[:, 1:2], ALU.add)
                if h0 == 0:
                    with self.nc.allow_non_contiguous_dma(reason="tiny"):
                        self.st(d["o_mm"][pidx, l, :].rearrange("(p o) -> p o", o=1), mcol[:, 2:3], [self.t_out])
                self.ts(DVE, mrep[:], self.C(C_ONE)[0:8, :], mcol[:, 2:3], ALU.mult)
                self.mm(ps_b[:], [(mrep[:], self.C(C_ID)[0:8, 0:8])])
                self.act(em0[:, 8:16], ps_b[:], AF.Exp, scale=-1.0)
            if is_s:
                with self.nc.allow_non_contiguous_dma(reason="state n vectors"):
                    for dd in range(2):
                        self.ld(S[:, dd * 4:dd * 4 + 4, 0:128], d["smC"][l, dd].rearrange("h k e -> k h e"))
                        self.ld(S[:, dd * 4:dd * 4 + 4, 128:129], d["smn"][l, dd].rearrange("h (k o) -> k h o", o=1))
                self.ld(em0[:, 0:8], d["smm"][l, :].partition_broadcast(128))
                self.act(em0[:, 0:8], em0[:, 0:8], AF.Exp)
                for j in range(8):
                    self.ts(DVE, S[:, j, :], S[:, j, :], em0[:, j:j + 1], ALU.mult)
            else:
                self.memset(DVE, S[:], 0.0)
            self.cp(ACT, Sb[0][:], S[:])
            it = 0
            for dd in range(2):
                order = list(range(nblk)) if dd == 0 else list(range(nblk - 1, -1, -1))
                mask = self.C(C_LE) if dd == 0 else self.C(C_GE)
                for bi, b in enumerate(order):
                    sl = slice(b * 128, (b + 1) * 128)
                    pa = ps_a[bi % 2]
                    for h in range(2):
                        self.mm(pa[:, h, :], [(fm[:, 2 + h, sl], fm[:, h, sl])])
                    sb_cur = Sb[bi % 2]
                    sb_nxt = Sb[(bi + 1) % 2]
                    H = []
                    for h in range(2):
                        j = dd * 4 + h0 + h
                        H.append((h, j, P[(it + h) % 3], kw[(it + h) % 3], dn[(it + h) % 4], ps_o[h], ps_s[h]))
                    it += 2
                    for (h, j, Pm, kwt, dnt, po, pS) in H:
                        self.stt(DVE, Pm[:], pa[:, h, :], ew[:, b, j:j + 1], mask, ALU.mult, ALU.mult)
                        self.ts(POOL, kwt[:], tm[:, b, h * 128:(h + 1) * 128], wn[:, b, j:j + 1], ALU.mult)
                    for (h, j, Pm, kwt, dnt, po, pS) in H:
                        self.mm(po[:, 0:129], [(Pm[:], vaug[:, b, h, :]), (fm[:, h, sl], sb_cur[:, j, :])])
                        self.mm(pS[:, 0:129], [(kwt[:], vaug[:, b, h, :])])
                    for (h, j, Pm, kwt, dnt, po, pS) in H:
                        self.act(dnt[:, 0:1], po[:, 128:129], AF.Abs)
                        self.stt(DVE, S[:, j, :], S[:, j, :], ebl[:, b, j:j + 1], pS[:, 0:129], ALU.mult, ALU.add)
                    for (h, j, Pm, kwt, dnt, po, pS) in H:
                        self.tt(DVE, dnt[:, 0:1], dnt[:, 0:1], enb[:, b, j:j + 1], ALU.max)
                        self.recip(dnt[:, 1:2], dnt[:, 0:1])
                        self.cp(ACT, sb_nxt[:, j, :], S[:, j, :])
                    for (h, j, Pm, kwt, dnt, po, pS) in H:
                        a_ = acc[:, b, h * 128:(h + 1) * 128]
                        if dd == 0:
                            self.act(a_, po[:, 0:128], AF.Identity, scale=dnt[:, 1:2])
                        else:
                            self.stt(DVE, a_, po[:, 0:128], dnt[:, 1:2], a_, ALU.mult, ALU.add)
            if not is_s:
                with self.nc.allow_non_contiguous_dma(reason="state n vectors"):
                    for dd in range(2):
                        for h in range(2):
                            j = dd * 4 + h0 + h
                            self.ts(DVE, S[:, j, :], S[:, j, :], em0[:, 8 + j:9 + j], ALU.mult)
                            self.st(d["o_mC"][pidx, l, dd, h0 + h], S[:, j, 0:128], [self.t_out])
                            self.st(d["o_mn"][pidx, l, dd, h0 + h].rearrange("(k o) -> k o", o=1), S[:, j, 128:129], [self.t_out])
        self.fw.barrier()
        with contextlib.ExitStack() as st3:
            _finish_branch(self, st3, 1, h0, l, t0, Tn, acc, T_MO, "mlstm_norm_w", "mB")


def _ret_old(self, l, t0, Tn, pidx, h0):
    d = self.d
    is_s = pidx < 0
    nblk = Tn // 128
    with contextlib.ExitStack() as st:
        acc = self.sb(st, "r_acc", [128, nblk, 256], F32)
        with contextlib.ExitStack() as st2:
            fm, tm = _scan_loads(self, st2, "r", t0, Tn, [(14 + h0, 2), (18 + h0, 2)], [(T_RK + h0 * 128, 256), (T_RV + h0 * 128, 256)])
            lg = self.sb(st2, "r_lg", [128, 8], F32)
            M = self.sb(st2, "r_M", [128, 8, 128], F32)
            xi = self.sb(st2, "r_xi", [128, 8, 128], F32)
            ze = self.sb(st2, "r_ze", [128, 8], F32)
            gc = self.sb(st2, "r_gc", [128, 8], F32)
            S = self.sb(st2, "r_S", [128, 8, 128], F32)
            Sb = [self.sb(st2, "r_Sb%d" % i, [128, 8, 128], BF16) for i in range(2)]
            P = [self.sb(st2, "r_P%d" % i, [128, 2, 128], BF16) for i in range(2)]
            qx = [self.sb(st2, "r_qx%d" % i, [128, 2, 128], BF16) for i in range(2)]
            kz = [self.sb(st2, "r_kz%d" % i, [128, 128], BF16) for i in range(3)]
            bk = self.psbank(st2, "r_psa")
            ps_a = [self.sub(bk, i * 256, (i + 1) * 256, 2) for i in range(2)]
            bk = self.psbank(st2, "r_pso")
            ps_o = [self.sub(bk, i * 256, (i + 1) * 256) for i in range(2)]
            bk = self.psbank(st2, "r_pss")
            ps_s = [self.sub(bk, i * 128, (i + 1) * 128) for i in range(3)]
            self.ld(lg[:], d["ret_decay"][l, :].partition_broadcast(128))
            self.act(lg[:], lg[:], AF.Exp, scale=-1.0)
            self.act(lg[:], lg[:], AF.Ln, bias=self.epsc[:, 1:2])
            self.ts(DVE, lg[:], lg[:], -1.0, ALU.mult)
            for dd in range(2):
                for h in range(2):
                    j = dd * 4 + h0 + h
                    self.act(M[:, j, :], self.C(C_DF if dd == 0 else C_DB), AF.Exp, scale=lg[:, j:j + 1])
                    self.act(xi[:, j, :], self.C(C_PF if dd == 0 else C_PB), AF.Exp, scale=lg[:, j:j + 1])
                    self.act(ze[:, j:j + 1], self.C(C_Z)[:, dd:dd + 1], AF.Exp, scale=lg[:, j:j + 1])
            self.act(gc[:], lg[:], AF.Exp, scale=128.0)
            if is_s:
                for dd in range(2):
                    self.ld(S[:, dd * 4:dd * 4 + 4, :], d["srS"][l, dd].rearrange("h k e -> k h e"))
            else:
                self.memset(DVE, S[:], 0.0)
            self.cp(ACT, Sb[0][:], S[:])
            it = 0
            for dd in range(2):
                order = list(range(nblk)) if dd == 0 else list(range(nblk - 1, -1, -1))
                j0 = dd * 4 + h0
                for bi, b in enumerate(order):
                    sl = slice(b * 128, (b + 1) * 128)
                    pa = ps_a[bi % 2]
                    for h in range(2):
                        self.mm(pa[:, h, :], [(fm[:, 2 + h, sl], fm[:, h, sl])])
                    Pm = P[bi % 2]
                    self.tt(DVE, Pm[:], pa[:], M[:, j0:j0 + 2, :], ALU.mult)
                    q_ = qx[bi % 2]
                    self.tt(POOL, q_[:], fm[:, 0:2, sl], xi[:, j0:j0 + 2, :], ALU.mult)
                    sb_cur = Sb[bi % 2]; sb_nxt = Sb[(bi + 1) % 2]
                    po = ps_o[bi % 2]
                    for h in range(2):
                        j = j0 + h
                        self.mm(po[:, h * 128:(h + 1) * 128], [(Pm[:, h, :], tm[:, b, 256 + h * 128:256 + (h + 1) * 128]), (q_[:, h, :], sb_cur[:, j, :])])
                    if dd == 0:
                        self.cp(ACT, acc[:, b, :], po[:])
                    else:
                        self.tt(DVE, acc[:, b, :], acc[:, b, :], po[:], ALU.add)
                    for h in range(2):
                        j = j0 + h
                        kzt = kz[it % 3]
                        self.ts(POOL, kzt[:], tm[:, b, h * 128:(h + 1) * 128], ze[:, j:j + 1], ALU.mult)
                        pS = ps_s[it % 3]
                        self.mm(pS[:], [(kzt[:], tm[:, b, 256 + h * 128:256 + (h + 1) * 128])])
                        self.stt(DVE, S[:, j, :], S[:, j, :], gc[:, j:j + 1], pS[:], ALU.mult, ALU.add)
                        self.cp(ACT, sb_nxt[:, j, :], S[:, j, :])
                        it += 1
            if not is_s:
                for dd in range(2):
                    for h in range(2):
                        self.st(d["o_rS"][pidx, l, dd, h0 + h], S[:, dd * 4 + h0 + h, :], [self.t_out])
        self.fw.barrier()
        with contextlib.ExitStack() as st3:
            _finish_branch(self, st3, 2, h0, l, t0, Tn, acc, T_RG, "ret_norm_w", "rB")


def _gla_old(self, l, t0, Tn, pidx, h0):
    d = self.d
    is_s = pidx < 0
    nblk = Tn // 128
    with contextlib.ExitStack() as st:
        acc = self.sb(st, "g_acc", [128, nblk, 256], F32)
        with contextlib.ExitStack() as st2:
            fm, tm = _scan_loads(self, st2, "g", t0, Tn, [(22 + h0, 2), (26 + h0, 2)], [(T_GK + h0 * 64, 128), (T_GV + h0 * 128, 256)], rows=64)
            lr = self.sb(st2, "g_lr", [16, Tn], F32)
            w2 = self.sb(st2, "g_w2", [16, 2, 256], F32)
            bg = self.sb(st2, "g_bg", [1, 2, 256], F32)
            lap = [self.sb(st2, "g_lap%d" % i, [128, 128], F32) for i in range(2)]
            eq = [self.sb(st2, "g_eq%d" % i, [64, 128], F32) for i in range(3)]
            ek = [self.sb(st2, "g_ek%d" % i, [64, 128], F32) for i in range(3)]
            qp = [self.sb(st2, "g_qp%d" % i, [64, 128], BF16) for i in range(3)]
            kp = [self.sb(st2, "g_kp%d" % i, [64, 128], BF16) for i in range(3)]
            ekk = [self.sb(st2, "g_ekk%d" % i, [128, 64], F32) for i in range(3)]
            kpp = [self.sb(st2, "g_kpp%d" % i, [128, 64], BF16) for i in range(3)]
            P = [self.sb(st2, "g_P%d" % i, [128, 128], BF16) for i in range(3)]
            S = self.sb(st2, "g_S", [64, 8, 128], F32)
            Sb = [self.sb(st2, "g_Sb%d" % i, [64, 8, 128], BF16) for i in range(2)]
            bk0 = self.psbank(st2, "g_ps0")
            ps_l = self.sub(bk0, 0, 128)
            ps_c = [self.sub(bk0, 128, 256), self.sub(bk0, 256, 384)]
            ps_r = [self.sub(bk0, 384, 448), self.sub(bk0, 448, 512)]
            bk1 = self.psbank(st2, "g_ps1")
            ps_a = [self.sub(bk1, 0, 128), self.sub(bk1, 128, 256)]
            ps_s = [T(bk1[0:64, 256:384]), T(bk1[0:64, 384:512])]
            bk = self.psbank(st2, "g_pso")
            ps_o = [self.sub(bk, i * 256, (i + 1) * 256) for i in range(2)]
            with self.nc.allow_non_contiguous_dma(reason="small params"):
                self.ld(w2[:], d["gla_w2"][l].rearrange("a r c -> r a c"))
            self.ld(bg[:], d["gla_b"][l:l + 1, :, :])
            if is_s:
                for dd in range(2):
                    self.ld(S[:, dd * 4:dd * 4 + 4, :], d["sgS"][l, dd].rearrange("h k e -> k h e"))
            else:
                self.memset(DVE, S[:], 0.0)
            self.cp(ACT, Sb[0][:], S[:])
            it = 0
            for dd in range(2):
                self.ld(lr[:], d["glr"][dd, :, t0:t0 + Tn], _ztiles(self, t0, Tn))
                order = list(range(nblk)) if dd == 0 else list(range(nblk - 1, -1, -1))
                tri = self.C(C_LE) if dd == 0 else self.C(C_GE)
                trs = self.C(C_GT) if dd == 0 else self.C(C_LT)
                c0 = h0 * 64
                for bi, b in enumerate(order):
                    sl = slice(b * 128, (b + 1) * 128)
                    la = lap[bi % 2]
                    self.mm(ps_l[:], [(lr[:, sl], w2[:, dd, c0:c0 + 128]), (self.C(C_ONE)[0:1, :], bg[:, dd, c0:c0 + 128])])
                    self.act(la[:], ps_l[:], AF.Exp, scale=-1.0)
                    self.act(la[:], la[:], AF.Ln, bias=self.epsc[:, 1:2])
                    sb_cur = Sb[bi % 2]; sb_nxt = Sb[(bi + 1) % 2]
                    po = ps_o[bi % 2]
                    for h in range(2):
                        j = dd * 4 + h0 + h
                        lah = la[:, h * 64:(h + 1) * 64]
                        pc = ps_c[it % 2]
                        self.mm(pc[0:64, :], [(lah, tri)])
                        e1 = eq[it % 3]; e2 = ek[it % 3]
                        self.act(e1[:], pc[0:64, :], AF.Exp, scale=-1.0 / 16)
                        self.act(e2[:], pc[0:64, :], AF.Exp, scale=1.0 / 16)
                        q_ = qp[it % 3]; k_ = kp[it % 3]
                        self.tt(DVE, q_[:], fm[0:64, h, sl], e1[:], ALU.mult)
                        self.tt(POOL, k_[:], fm[0:64, 2 + h, sl], e2[:], ALU.mult)
                        pa = ps_a[it % 2]
                        self.mm(pa[:], [(k_[:], q_[:])])
                        Pm = P[it % 3]
                        self.tt(DVE, Pm[:], pa[:], tri, ALU.mult)
                        self.mm(po[:, h * 128:(h + 1) * 128], [(Pm[:], tm[:, b, 128 + h * 128:128 + (h + 1) * 128]), (q_[:], sb_cur[:, j, :])])
                        pr = ps_r[it % 2]
                        self.mm(pr[:], [(trs, lah)])
                        e3 = ekk[it % 3]
                        self.act(e3[:], pr[:], AF.Exp, scale=-1.0 / 16)
                        k2 = kpp[it % 3]
                        self.tt(DVE, k2[:], tm[:, b, h * 64:(h + 1) * 64], e3[:], ALU.mult)
                        pS = ps_s[it % 2]
                        self.mm(pS[:], [(k2[:], tm[:, b, 128 + h * 128:128 + (h + 1) * 128])])
                        lastc = e1[:, 127:128] if dd == 0 else e1[:, 0:1]
                        self.stt(DVE, S[:, j, :], S[:, j, :], lastc, pS[:], ALU.mult, ALU.add)
                        self.cp(ACT, sb_nxt[:, j, :], S[:, j, :])
                        it += 1
                    if dd == 0:
                        self.cp(ACT, acc[:, b, :], po[:])
                    else:
                        self.tt(DVE, acc[:, b, :], acc[:, b, :], po[:], ALU.add)
            if not is_s:
                for dd in range(2):
                    for h in range(2):
                        self.st(d["o_gS"][pidx, l, dd, h0 + h], S[:, dd * 4 + h0 + h, :], [self.t_out])
        self.fw.barrier()
        with contextlib.ExitStack() as st3:
            _finish_branch(self, st3, 3, h0, l, t0, Tn, acc, T_GG, "gla_norm_w", "gB")


def make_consts(TS):
    c = np.zeros((128, NCST, 128), np.float32)
    i = np.arange(128)
    s, t = i[:, None], i[None, :]
    c[:, C_ID] = (s == t)
    c[:, C_LE] = (s <= t)
    c[:, C_GE] = (s >= t)
    c[:, C_GT] = (s > t)
    c[:, C_LT] = (s < t)
    c[:, C_DF] = np.where(t >= s, t - s, 1e6)
    c[:, C_DB] = np.where(s >= t, s - t, 1e6)
    c[:, C_ONE] = 1.0
    c[:, C_PF] = (t + 1) * np.ones((128, 1))
    c[:, C_PB] = (128 - t) * np.ones((128, 1))
    c[:, C_Z, 0] = 127 - i
    c[:, C_Z, 1] = i
    c[:, C_Z, 2] = (i < 4)
    c[:, C_Z, 3] = (i >= 4)
    c[:, C_MP] = np.where(t >= s, 0.0, -30000.0)
    c[:, C_MN] = np.where(t <= s, 0.0, -30000.0)
    cst = c.reshape(128, NCST * 128)
    tt = np.arange(TS)
    row = (tt // 64).astype(np.float32)
    col = (tt % 64).astype(np.float32)
    inv = (10000.0 ** (-np.arange(32, dtype=np.float32) / 32)).astype(np.float32)
    C = np.zeros((128, TS), np.float32)
    Sg = np.zeros((128, TS), np.float32)
    for p in range(128):
        pos = row if p < 64 else col
        ang = (pos * inv[p % 32]).astype(np.float32)
        C[p] = np.cos(ang)
        sn = np.sin(ang)
        Sg[p] = -sn if (p % 64) < 32 else sn
    perm = np.zeros((128, 128), np.float32)
    for m in range(128):
        k = m + 32 if (m % 64) < 32 else m - 32
        perm[k, m] = 1.0
    return cst, C, Sg, perm


_CACHE = {}


def kernel(**inp):
    return _run(inp, 8, L)


def _run(inp, NC, nlayers, debug=False):
    inp = {k: np.asarray(v) for k, v in inp.items()}
    TS = inp["x_sample"].shape[1]
    TP = inp["x_prompt"].shape[1]
    NP = inp["x_prompt"].shape[0] // NC
    key = (TS, TP, NP, nlayers, debug)
    if key not in _CACHE:
        _CACHE[key] = K(TS, TP, NP, nlayers, debug).build()
    nc = _CACHE[key]
    cst, rc, rs, perm = make_consts(TS)
    f = lambda a: np.ascontiguousarray(a, dtype=np.float32)
    shared = {k: f(inp[k]) for k in ["w_ada", "b_ada", "norm1_w", "norm2_w", "w_in", "attn_sink", "mlstm_norm_w", "ret_norm_w",
                                      "gla_w2", "gla_b", "gla_norm_w", "w_br", "w_mgate", "w_out", "ffn_w_gu", "ffn_w_down", "final_norm_w"]}
    shared["mlstm_if_b"] = f(inp["mlstm_if_b"].reshape(L, 16))
    shared["ret_decay"] = f(inp["ret_decay"].reshape(L, 8))
    shared.update(cst=cst, ropeC=rc, ropeS=rs, perm=perm)
    in_maps = []
    for c in range(NC):
        m = dict(shared)
        m["x"] = f(np.concatenate([inp["x_sample"][c]] + [inp["x_prompt"][c * NP + i] for i in range(NP)], axis=0))
        m["cak"] = f(inp["cache_attn_k"][c].reshape(L, 256, 256))
        m["cav"] = f(inp["cache_attn_v"][c].reshape(L, 256, 256))
        m["smC"] = f(inp["state_mlstm_C"][c]); m["smn"] = f(inp["state_mlstm_n"][c]); m["smm"] = f(inp["state_mlstm_m"][c].reshape(L, 8))
        m["srS"] = f(inp["state_ret_S"][c]); m["sgS"] = f(inp["state_gla_S"][c])
        m["cond"] = f(np.stack([inp["c"][c], inp["c_ctx"]], axis=0))
        in_maps.append(m)
    res = run_bass_kernel_spmd(nc, in_maps, core_ids=list(range(NC)))
    R = res.results
    if debug:
        global _DBG
        _DBG = R
    y_s = np.stack([R[c]["y"][:TS] for c in range(NC)], axis=0)
    y_p = np.stack([R[c]["y"][TS + i * TP:TS + (i + 1) * TP] for c in range(NC) for i in range(NP)], axis=0)
    cat = lambda k: np.concatenate([R[c][k] for c in range(NC)], axis=0)
    B = NC * NP
    return (y_p.astype(np.float32), y_s.astype(np.float32),
            cat("o_ak").reshape(B, L, TP, 2, 128), cat("o_av").reshape(B, L, TP, 2, 128),
            cat("o_mC"), cat("o_mn"), cat("o_mm").reshape(B, L, 2, 4), cat("o_rS"), cat("o_gS"))
```

```python
import contextlib
import numpy as np
import concourse.bass as bass
import concourse.mybir as mybir
from concourse.bass_utils import run_bass_kernel_spmd

F32 = mybir.dt.float32
BF16 = mybir.dt.bfloat16
ALU = mybir.AluOpType
AF = mybir.ActivationFunctionType
AX = mybir.AxisListType
PE, DVE, ACT, POOL, SP = "tensor", "vector", "scalar", "gpsimd", "sync"
ENGINES = [PE, DVE, ACT, POOL, SP]

D = 2048
KD = 16
L = 2
FF = 5632
KF = 44
DIN = 6704
EPS = 1e-6
NF = 30
NTM = 4608
O_AQ, O_AK, O_AV, O_MQ, O_MK, O_MV, O_MO, O_MG = 0, 512, 768, 1024, 1536, 2048, 2560, 3072
O_RQ, O_RK, O_RV, O_RG, O_GQ, O_GK, O_GV, O_GG, O_GLR = 3088, 3600, 4112, 4624, 5136, 5392, 5648, 6160, 6672
T_AV, T_MK, T_MV, T_MO, T_RK, T_RV, T_RG, T_GK, T_GV, T_GG = 0, 256, 768, 1280, 1792, 2304, 2816, 3328, 3584, 4096
C_ID, C_LE, C_GE, C_GT, C_LT, C_DF, C_DB, C_ONE, C_PF, C_PB, C_Z, C_MP, C_MN = range(13)
NCST = 13
import os
STREAMS = os.environ.get('KSTREAMS', 'none')


class V:
    __slots__ = ("t", "ap")

    def __init__(self, t, ap):
        self.t = t
        self.ap = ap

    def __getitem__(self, k):
        return V(self.t, self.ap[k])


class T:
    _n = 0

    def __init__(self, ap=None, name="", partial=False):
        self.ap = ap
        self.name = name
        self.partial = partial
        self.writers = []
        self.readers = []
        T._n += 1
        self.id = T._n

    def __getitem__(self, k):
        return V(self, self.ap[k])

    def v(self, ap):
        return V(self, ap)


class Op:
    __slots__ = ("eng", "fn", "deps", "needed", "val", "dma_key", "is_dma")

    def __init__(self, eng, fn, is_dma=False, dma_key=None):
        self.eng = eng
        self.fn = fn
        self.deps = []
        self.needed = False
        self.val = None
        self.is_dma = is_dma
        self.dma_key = dma_key


class FW:
    def __init__(self, nc):
        self.nc = nc
        self.ops = {e: [] for e in ENGINES}
        self.pending_dma = []
        self.nops = 0
        self.slot_of = {}
        self.dma_count = {}

    def _record(self, op, reads, writes):
        deps = []
        for t in reads:
            deps.extend(t.writers)
        for t in writes:
            if t.partial:
                deps.extend(t.readers)
            else:
                deps.extend(t.writers)
                deps.extend(t.readers)
        seen = set()
        for d in deps:
            if d is op or id(d) in seen:
                continue
            seen.add(id(d))
            if (not d.is_dma) and d.eng == op.eng and not op.is_dma and op.eng == PE:
                continue
            op.deps.append(d)
            d.needed = True
        for t in reads:
            t.readers.append(op)
            if len(t.readers) > 64:
                t.readers = t.readers[-64:] if False else t.readers
        for t in writes:
            if t.partial:
                if t.readers:
                    t.writers = [op]
                    t.readers = []
                else:
                    t.writers.append(op)
            else:
                t.writers = [op]
                t.readers = []
        self.ops[op.eng].append(op)
        self.nops += 1
        return op

    def op(self, eng, fn, reads=(), writes=()):
        return self._record(Op(eng, fn), list(reads), list(writes))

    def dma(self, eng, out_ap, in_ap, reads=(), writes=(), key=None):
        if key.id not in self.slot_of:
            self.slot_of[key.id] = len(self.slot_of)
        slot = self.slot_of[key.id]
        o = Op(eng, lambda e: e.dma_start(out=out_ap, in_=in_ap, allow_slow_non_contiguous=True), is_dma=True, dma_key=slot)
        self.dma_count[slot] = self.dma_count.get(slot, 0) + 16
        o.val = self.dma_count[slot]
        o.needed = True
        self.pending_dma.append(o)
        return self._record(o, list(reads), list(writes))

    def barrier(self):
        b = Op(SP, lambda e: e.nop())
        for e in ENGINES:
            if e != SP and self.ops[e]:
                d = self.ops[e][-1]
                d.needed = True
                b.deps.append(d)
        for d in self.pending_dma:
            b.deps.append(d)
        self.pending_dma = []
        self.slot_of = {}
        b.needed = True
        self.ops[SP].append(b)
        for e in ENGINES:
            if e != SP:
                o = Op(e, lambda en: en.nop())
                o.deps.append(b)
                self.ops[e].append(o)

    def emit(self):
        nc = self.nc
        dma_counts = {}
        for e in ENGINES:
            cnt = 0
            for o in self.ops[e]:
                if o.is_dma:
                    dma_counts[o.dma_key] = 1
                elif o.needed:
                    cnt += 1
                    o.val = cnt
        sems = {}
        stack = contextlib.ExitStack()
        with stack:
            for e in ENGINES:
                sems[("eng", e)] = stack.enter_context(nc.semaphore("c_" + e))
            for k in dma_counts:
                sems[("dma", k)] = stack.enter_context(nc.semaphore("d_%d" % k))
            self.n_sems = len(sems)
            final = {}
            for e in ENGINES:
                for o in self.ops[e]:
                    if o.val is not None:
                        k = ("dma", o.dma_key) if o.is_dma else ("eng", o.eng)
                        final[k] = max(final.get(k, 0), o.val)
            block = stack.enter_context(nc.Block())

            def run(engname, eng):
                waited = {}
                for o in self.ops[engname]:
                    need = {}
                    for d in o.deps:
                        k = ("dma", d.dma_key) if d.is_dma else ("eng", d.eng)
                        if d.val > need.get(k, 0):
                            need[k] = d.val
                    for k, v in need.items():
                        if waited.get(k, 0) < v:
                            eng.wait_ge(sems[k], v)
                            waited[k] = v
                    inst = o.fn(eng)
                    if o.is_dma:
                        inst.then_inc(sems[("dma", o.dma_key)], 16)
                    elif o.needed:
                        inst.then_inc(sems[("eng", engname)], 1)
                if engname == SP:
                    for k, v in final.items():
                        if waited.get(k, 0) < v:
                            eng.wait_ge(sems[k], v)

            @block.tensor
            def _(eng):
                run(PE, eng)

            @block.vector
            def _(eng):
                run(DVE, eng)

            @block.scalar
            def _(eng):
                run(ACT, eng)

            @block.gpsimd
            def _(eng):
                run(POOL, eng)

            @block.sync
            def _(eng):
                run(SP, eng)


def _ts(xs):
    return [x.t for x in xs if isinstance(x, V)]


def _a(x):
    return x.ap if isinstance(x, V) else x


class K:
    def __init__(self, TS, TP, NP, nlayers=L, debug=False):
        self.debug = debug
        self.TS, self.TP, self.NP = TS, TP, NP
        self.NT = TS + TP * NP
        assert TS % 512 == 0 and TP * NP == 512 and TP % 128 == 0
        self.NTT = self.NT // 512
        self.nl = nlayers
        self.nc = bass.Bass("TRN2", target_bir_lowering=False)
        self.fw = FW(self.nc)
        self.seqs = [(0, TS, -1)] + [(TS + i * TP, TP, i) for i in range(NP)]

    def tt(self, eng, out, a, b, op):
        self.fw.op(eng, lambda e: e.tensor_tensor(_a(out), _a(a), _a(b), op), _ts([a, b]), _ts([out]))

    def ts(self, eng, out, a, s1, op0, s2=None, op1=None, accum=None):
        def fn(e):
            kw = {}
            if accum is not None:
                kw["accum_out"] = _a(accum)
            if op1 is None:
                return e.tensor_scalar(_a(out), _a(a), _a(s1), None, op0, **kw)
            return e.tensor_scalar(_a(out), _a(a), _a(s1), _a(s2), op0, op1, **kw)
        self.fw.op(eng, fn, _ts([a, s1, s2]), _ts([out, accum]))

    def stt(self, eng, out, a, s, b, op0, op1):
        self.fw.op(eng, lambda e: e.scalar_tensor_tensor(_a(out), _a(a), _a(s), _a(b), op0, op1), _ts([a, s, b]), _ts([out]))

    def act(self, out, a, func, bias=None, scale=None, accum=None):
        def fn(e):
            kw = {}
            if bias is not None:
                kw["bias"] = _a(bias)
            if scale is not None:
                kw["scale"] = _a(scale)
            if accum is not None:
                kw["accum_out"] = _a(accum)
            return e.activation(out=_a(out), in_=_a(a), func=func, **kw)
        self.fw.op(ACT, fn, _ts([a, bias, scale]), _ts([out, accum]))

    def cp(self, eng, out, a):
        if eng == ACT:
            self.fw.op(ACT, lambda e: e.copy(_a(out), _a(a)), _ts([a]), _ts([out]))
        else:
            self.fw.op(eng, lambda e: e.tensor_copy(_a(out), _a(a)), _ts([a]), _ts([out]))

    def memset(self, eng, out, val):
        self.fw.op(eng, lambda e: e.memset(_a(out), val), [], _ts([out]))

    def red(self, out, a, op):
        self.fw.op(DVE, lambda e: e.tensor_reduce(_a(out), _a(a), AX.X, op), _ts([a]), _ts([out]))

    def mm(self, out, pairs, extra_reads=()):
        def fn(e):
            n = len(pairs)
            inst = None
            for i, (l, r) in enumerate(pairs):
                inst = e.matmul(_a(out), _a(l), _a(r), start=(i == 0), stop=(i == n - 1))
            return inst
        rd = []
        for l, r in pairs:
            rd += _ts([l, r])
        self.fw.op(PE, fn, rd + list(extra_reads), _ts([out]))

    def tr(self, out, a, ident):
        self.fw.op(PE, lambda e: e.transpose(_a(out), _a(a), _a(ident)), _ts([a, ident]), _ts([out]))

    def ld(self, out, src_ap, src_ts=(), eng=SP):
        self.fw.dma(eng, _a(out), src_ap, reads=list(src_ts), writes=[out.t], key=out.t)

    def st(self, dst_ap, a, dst_ts=(), eng=SP):
        self.fw.dma(eng, dst_ap, _a(a), reads=[a.t], writes=list(dst_ts), key=a.t)

    def uniq(self, name):
        self._u = getattr(self, "_u", 0) + 1
        return "%s_u%d" % (name, self._u)

    def sb(self, st, name, shape, dt=F32):
        return T(st.enter_context(self.nc.sbuf_tensor(self.uniq(name), shape, dt)), name)

    def ps(self, st, name, shape, dt=F32):
        return T(st.enter_context(self.nc.psum_tensor(self.uniq(name), shape, dt)), name)

    def psbank(self, st, name, dt=F32):
        return st.enter_context(self.nc.psum_tensor(self.uniq(name), [128, 512 if dt == F32 else 1024], dt))

    def sub(self, bank, a, b, n3=None):
        ap = bank[:, a:b]
        if n3:
            ap = ap.rearrange("p (a b) -> p a b", a=n3)
        return T(ap)

    def recip(self, out, a):
        self.fw.op(DVE, lambda e: e.reciprocal(_a(out), _a(a)), _ts([a]), _ts([out]))

    def dram(self, name, shape, dt, kind="Internal"):
        if self.debug and kind == "Internal":
            kind = "ExternalOutput"
        return self.nc.dram_tensor(name, list(shape), dt, kind=kind).ap()

    def build(self):
        nc, fw = self.nc, self.fw
        NT, NTT, TS, TP, NP, nl = self.NT, self.NTT, self.TS, self.TP, self.NP, self.nl
        I = lambda n, s, dt=F32: self.dram(n, s, dt, "ExternalInput")
        O = lambda n, s, dt=F32: self.dram(n, s, dt, "ExternalOutput")
        d = self.d = {}
        d["x"] = I("x", [NT, D])
        d["cak"] = I("cak", [L, 256, 256]); d["cav"] = I("cav", [L, 256, 256])
        d["smC"] = I("smC", [L, 2, 4, 128, 128]); d["smn"] = I("smn", [L, 2, 4, 128]); d["smm"] = I("smm", [L, 8])
        d["srS"] = I("srS", [L, 2, 4, 128, 128]); d["sgS"] = I("sgS", [L, 2, 4, 64, 128])
        d["cond"] = I("cond", [2, D])
        d["w_ada"] = I("w_ada", [L, D, 6 * D]); d["b_ada"] = I("b_ada", [L, 6 * D])
        d["norm1_w"] = I("norm1_w", [L, D]); d["norm2_w"] = I("norm2_w", [L, D])
        d["w_in"] = I("w_in", [L, D, DIN]); d["attn_sink"] = I("attn_sink", [L, 4])
        d["mlstm_if_b"] = I("mlstm_if_b", [L, 16]); d["mlstm_norm_w"] = I("mlstm_norm_w", [L, 512])
        d["ret_decay"] = I("ret_decay", [L, 8]); d["ret_norm_w"] = I("ret_norm_w", [L, 512])
        d["gla_w2"] = I("gla_w2", [L, 2, 16, 256]); d["gla_b"] = I("gla_b", [L, 2, 256]); d["gla_norm_w"] = I("gla_norm_w", [L, 512])
        d["w_br"] = I("w_br", [L, 4, 512, D]); d["w_mgate"] = I("w_mgate", [L, 4, D, D]); d["w_out"] = I("w_out", [L, D, D])
        d["ffn_w_gu"] = I("ffn_w_gu", [L, D, 2 * FF]); d["ffn_w_down"] = I("ffn_w_down", [L, FF, D])
        d["final_norm_w"] = I("final_norm_w", [D])
        d["cst"] = I("cst", [128, NCST * 128]); d["ropeC"] = I("ropeC", [128, TS]); d["ropeS"] = I("ropeS", [128, TS])
        d["perm"] = I("perm", [128, 128])
        d["y"] = O("y", [NT, D])
        d["o_ak"] = O("o_ak", [NP, L, TP, 256]); d["o_av"] = O("o_av", [NP, L, TP, 256])
        d["o_mC"] = O("o_mC", [NP, L, 2, 4, 128, 128]); d["o_mn"] = O("o_mn", [NP, L, 2, 4, 128]); d["o_mm"] = O("o_mm", [NP, L, 8])
        d["o_rS"] = O("o_rS", [NP, L, 2, 4, 128, 128]); d["o_gS"] = O("o_gS", [NP, L, 2, 4, 64, 128])
        d["xres"] = self.dram("xres", [NT, D], F32)
        d["hT"] = self.dram("hT", [NTT, 128, KD, 512], BF16)
        d["zfm"] = self.dram("zfm", [NTT, 128, NF, 512], BF16)
        d["ztm"] = self.dram("ztm", [NT, NTM], BF16)
        d["mg"] = self.dram("mg", [NT, 16], F32)
        d["glr"] = self.dram("glr", [2, 16, NT], F32)
        d["ysT"] = self.dram("ysT", [NTT, 128, 16, 512], BF16)
        d["mT"] = self.dram("mT", [NTT, 128, KD, 512], BF16)
        d["aT"] = self.dram("aT", [NTT, 128, KF, 512], BF16)
        d["gb"] = self.dram("gb", [L, 2, 2, D], F32)
        if self.debug:
            d["modv_o"] = self.dram("modv_o", [128, L * 2 * 4 * KD], F32)
            d["dbg_acc"] = self.dram("dbg_acc", [128, (TS // 128) * 256], F32)
            d["dbg_fm"] = self.dram("dbg_fm", [128, 4 * TS], BF16)
            d["dbg_tm"] = self.dram("dbg_tm", [128, (TS // 128) * 512], BF16)
            d["dbg_g"] = self.dram("dbg_g", [128, 5 * (TS // 128) * 8], F32)
        self.t_x = [T(name="x%d" % i, partial=True) for i in range(NT // 128)]
        self.t_hT = [T(name="hT%d" % i, partial=True) for i in range(NTT)]
        self.t_z = [T(name="z%d" % i, partial=True) for i in range(NTT)]
        self.t_ys = [T(name="ys%d" % i, partial=True) for i in range(NTT)]
        self.t_mT = [T(name="mT%d" % i, partial=True) for i in range(NTT)]
        self.t_aT = [T(name="aT%d" % i, partial=True) for i in range(NTT)]
        self.t_gb = T(name="gb", partial=True)
        self.t_out = T(name="outs", partial=True)

        with contextlib.ExitStack() as gst:
            self.cst = self.sb(gst, "cst", [128, NCST * 128], F32)
            self.cstb = self.sb(gst, "cstb", [128, NCST * 128], BF16)
            self.modv = self.sb(gst, "modv", [128, L * 2 * 4 * KD], F32)
            self.epsc = self.sb(gst, "epsc", [128, 2], F32)
            self.memset(DVE, self.epsc[:, 0:1], EPS)
            self.memset(DVE, self.epsc[:, 1:2], 1.0)
            self.ld(self.cst[:], d["cst"][:, :])
            self.ld(self.cstb[:], d["cst"][:, :], eng=POOL)
            fw.barrier()
            self.phase_ada()
            for l in range(nl):
                self.phase_norm(l, 0)
                self.phase_win(l)
                self.phase_mix(l)
                self.phase_merge(l)
                self.phase_wout(l)
                self.phase_gu(l)
                self.phase_down(l)
            self.phase_final()
            fw.emit()
        return nc

    def C(self, blk, bf=False):
        t = self.cstb if bf else self.cst
        return t[:, blk * 128:(blk + 1) * 128]

    def mv(self, l, cond, which):
        o = ((l * 2 + cond) * 4 + which) * KD
        return o

    def cond_of_tile(self, tt):
        return 0 if tt * 512 < self.TS else 1

    def phase_ada(self):
        d, fw, nl = self.d, self.fw, self.nl
        with contextlib.ExitStack() as st:
            condT = self.sb(st, "condT", [128, 2, KD], F32)
            scT = self.sb(st, "scT", [128, KD, 2], BF16)
            rep = self.sb(st, "rep", [128, 2, KD, 128], BF16)
            ones = self.sb(st, "ones_a", [128, 128], BF16)
            wg = [self.sb(st, "wg%d" % i, [128, KD, 512], BF16) for i in range(2)]
            bfm = self.sb(st, "bfm", [128, L, 48], F32)
            nw = self.sb(st, "nw", [128, L, 2, KD], F32)
            brow = [self.sb(st, "brow%d" % i, [128, 512], F32) for i in range(2)]
            gst_ = [self.sb(st, "gst%d" % i, [128, 512], F32) for i in range(2)]
            psA = [self.ps(st, "psA%d" % i, [128, 512]) for i in range(2)]
            psB = [self.ps(st, "psB%d" % i, [128, 8]) for i in range(2)]
            bfm2 = self.sb(st, "bfm2", [128, L, 48], F32)
            for c in range(2):
                self.ld(condT[:, c, :], d["cond"][c].rearrange("(k p) -> p k", p=128))
            for l_ in range(L):
                bv = d["b_ada"][l_].rearrange("(j p) -> p j", p=128)
                self.ld(bfm[:, l_, :], bv[:, 0:48])
                self.ld(bfm2[:, l_, :], bv[:, 48:96])
                self.ld(nw[:, l_, 0, :], d["norm1_w"][l_].rearrange("(k p) -> p k", p=128))
                self.ld(nw[:, l_, 1, :], d["norm2_w"][l_].rearrange("(k p) -> p k", p=128))
            scF = self.sb(st, "scF", [128, KD, 2], F32)
            self.act(scF[:], condT.v(condT.ap[:].rearrange("p c k -> p k c")), AF.Silu)
            self.cp(DVE, scT[:], scF[:])
            self.memset(DVE, ones[:], 1.0)
            for c in range(2):
                for k in range(KD):
                    self.ts(DVE, rep[:, c, k, :], ones[:], scF[:, k, c:c + 1], ALU.mult)
            gi = 0
            for l in range(nl):
                wv = d["w_ada"][l].rearrange("(k p) n -> p k n", p=128)
                for g in range(24):
                    w = wg[gi % 2]
                    self.ld(w[:], wv[:, :, g * 512:(g + 1) * 512], eng=POOL)
                    seg = g // 4
                    if seg in (2, 5):
                        for c in range(2):
                            p = psA[c]
                            self.mm(p[:], [(rep[:, c, k, :], w[:, k, :]) for k in range(KD)])
                            br = brow[c]
                            self.ld(br[:], d["b_ada"][l, g * 512:(g + 1) * 512].partition_broadcast(128))
                            gs = gst_[c]
                            self.tt(DVE, gs[:], p[:], br[:], ALU.add)
                            col = (g % 4) * 512
                            self.st(d["gb"][l, 0 if seg == 2 else 1, c:c + 1, col:col + 512], gs[0:1, :], [self.t_gb])
                    else:
                        which = {0: 0, 1: 1, 3: 2, 4: 3}[seg]
                        for j in range(4):
                            kc = (g % 4) * 4 + j
                            p = psB[j % 2]
                            self.mm(p[:, 0:2], [(w[:, k, j * 128:(j + 1) * 128], scT[:, k, :]) for k in range(KD)])
                            bsrc = bfm if seg < 3 else bfm2
                            bj = (seg * 16 + kc) if seg < 3 else ((seg - 3) * 16 + kc)
                            for c in range(2):
                                o = self.mv(l, c, which) + kc
                                dst = self.modv[:, o:o + 1]
                                if which in (0, 2):
                                    self.tt(DVE, dst, p[:, c:c + 1], bsrc[:, l, bj:bj + 1], ALU.add)
                                else:
                                    self.stt(DVE, dst, p[:, c:c + 1], 1.0, bsrc[:, l, bj:bj + 1], ALU.add, ALU.add)
                                    self.tt(DVE, dst, dst, nw[:, l, 0 if which == 1 else 1, kc:kc + 1], ALU.mult)
                    gi += 1
            if self.debug:
                self.st(d["modv_o"][:, :], self.modv[:], [self.t_out])
        fw.barrier()

    def norm_block(self, st_, xb, cond, l, which, hst, tb, bufs, i):
        junk, ss, xn, pt = bufs
        self.act(junk[:], xb, AF.Square, accum=ss[:, 0:1])
        self.act(ss[:, 1:2], ss[:, 0:1], AF.Sqrt, bias=self.epsc[:, 0:1], scale=1.0 / D)
        self.recip(ss[:, 2:3], ss[:, 1:2])
        self.ts(DVE, xn[:], xb, ss[:, 2:3], ALU.mult)
        for q in range(4):
            p = pt[(i * 4 + q) % len(pt)]
            for j in range(4):
                kc = q * 4 + j
                self.tr(p[:, j, :], xn[:, kc * 128:(kc + 1) * 128], self.C(C_ID, True))
            for j in range(4):
                kc = q * 4 + j
                osc = self.mv(l, cond, 1 + 2 * which) + kc
                osh = self.mv(l, cond, 0 + 2 * which) + kc
                dst = hst[:, kc, tb * 128:(tb + 1) * 128]
                if j % 2 == 0:
                    self.act(dst, p[:, j, :], AF.Identity, bias=self.modv[:, osh:osh + 1], scale=self.modv[:, osc:osc + 1])
                else:
                    self.ts(DVE, dst, p[:, j, :], self.modv[:, osc:osc + 1], ALU.mult, self.modv[:, osh:osh + 1], ALU.add)

    def phase_norm(self, l, which):
        d, fw = self.d, self.fw
        xsrc = d["x"] if (l == 0 and which == 0) else d["xres"]
        with contextlib.ExitStack() as st:
            xb = [self.sb(st, "xb%d" % i, [128, D], F32) for i in range(3)]
            junk = self.sb(st, "junk", [128, D], BF16)
            ss = [self.sb(st, "ss%d" % i, [128, 4], F32) for i in range(2)]
            xn = [self.sb(st, "xn%d" % i, [128, D], BF16) for i in range(2)]
            pt = [self.ps(st, "pt%d" % i, [128, 4, 128], BF16) for i in range(4)]
            hst = [self.sb(st, "hst%d" % i, [128, KD, 512], BF16) for i in range(2)]
            nb = self.NT // 128
            self.ld(xb[0][:], xsrc[0:128, :], [self.t_x[0]])
            for i in range(nb):
                if i + 1 < nb:
                    self.ld(xb[(i + 1) % 3][:], xsrc[(i + 1) * 128:(i + 2) * 128, :], [self.t_x[i + 1]])
                tt, tb = i // 4, i % 4
                h = hst[tt % 2]
                self.norm_block(st, xb[i % 3][:], self.cond_of_tile(tt), l, which, h, tb, (junk, ss[i % 2], xn[i % 2], pt), i)
                if tb == 3:
                    self.st(d["hT"][tt], h[:], [self.t_hT[tt]])
        fw.barrier()

    def wload(self, wt, src3):
        self.ld(wt, src3, eng=POOL)

    def phase_win(self, l):
        d, fw, NTT = self.d, self.fw, self.NTT
        wv = d["w_in"][l].rearrange("(k p) n -> p k n", p=128)
        S = 128 ** -0.5
        S64 = 64 ** -0.5
        groups = [
            (0, 1024, [(O_AQ + i * 128, 128, i, 1.0, "rope") for i in range(4)] + [(O_AK + i * 128, 128, 4 + i, 1.0, "rope") for i in range(2)],
             [(O_AV, 256, T_AV, AF.Identity, 1.0)], True),
            (1024, 2048, [(O_MQ + i * 128, 128, 6 + i, 1.0, "") for i in range(4)] + [(O_MK + i * 128, 128, 10 + i, S, "") for i in range(4)],
             [(O_MK, 512, T_MK, AF.Identity, S)], False),
            (2048, 3072, [], [(O_MV, 512, T_MV, AF.Identity, 1.0), (O_MO, 512, T_MO, AF.Sigmoid, 1.0)], False),
            (3072, 4112, [(O_RQ + i * 128, 128, 14 + i, 1.0, "") for i in range(4)] + [(O_RK + i * 128, 128, 18 + i, S, "") for i in range(4)],
             [(O_MG, 16, -1, AF.Identity, 1.0), (O_RK, 512, T_RK, AF.Identity, S)], False),
            (4112, 5136, [], [(O_RV, 512, T_RV, AF.Identity, 1.0), (O_RG, 512, T_RG, AF.Silu, 1.0)], False),
            (5136, 6160, [(O_GQ + i * 64, 64, 22 + i, S64, "") for i in range(4)] + [(O_GK + i * 64, 64, 26 + i, 1.0, "") for i in range(4)],
             [(O_GK, 256, T_GK, AF.Identity, 1.0), (O_GV, 512, T_GV, AF.Identity, 1.0)], False),
            (6160, 6704, [(O_GLR, 16, -2, 1.0, "glr"), (O_GLR + 16, 16, -3, 1.0, "glr")],
             [(O_GG, 512, T_GG, AF.Silu, 1.0)], False),
        ]
        with contextlib.ExitStack() as st:
            wb = [self.sb(st, "wb%d" % i, [128, KD, 1040], BF16) for i in range(2)]
            hb = [self.sb(st, "hb%d" % i, [128, KD, 512], BF16) for i in range(2)]
            zst = [self.sb(st, "zst%d" % i, [128, 8, 512], BF16) for i in range(2)]
            tst = [self.sb(st, "tst%d" % i, [128, 4, 1024], BF16) for i in range(2)]
            mgst = [self.sb(st, "mgst%d" % i, [128, 4, 16], F32) for i in range(2)]
            glst = [self.sb(st, "glst%d" % i, [16, 2, 512], F32) for i in range(2)]
            kvst = [self.sb(st, "kvst%d" % i, [128, 512], F32) for i in range(2)]
            rc = [self.sb(st, "rc%d" % i, [128, 512], F32) for i in range(2)]
            rs = [self.sb(st, "rs%d" % i, [128, 512], F32) for i in range(2)]
            qf = [self.sb(st, "qf%d" % i, [128, 512], F32) for i in range(2)]
            t1 = [self.sb(st, "t1%d" % i, [128, 512], F32) for i in range(2)]
            permf = self.sb(st, "permf", [128, 128], F32)
            pp = [self.ps(st, "pp%d" % i, [128, 512]) for i in range(6)]
            pq = [self.ps(st, "pq%d" % i, [128, 512]) for i in range(2)]
            self.ld(permf[:], d["perm"][:, :])
            self.wload(wb[0][:, :, 0:1024], wv[:, :, 0:1024])
            pi = 0
            it = 0
            for gi, (c0, c1, fmj, tmj, isattn) in enumerate(groups):
                w = wb[gi % 2]
                if gi + 1 < len(groups):
                    n0, n1 = groups[gi + 1][0], groups[gi + 1][1]
                    self.wload(wb[(gi + 1) % 2][:, :, 0:n1 - n0], wv[:, :, n0:n1])
                self.ld(hb[it % 2][:], d["hT"][0], [self.t_hT[0]])
                for tt in range(NTT):
                    h = hb[it % 2]
                    if tt + 1 < NTT:
                        self.ld(hb[(it + 1) % 2][:], d["hT"][tt + 1], [self.t_hT[tt + 1]])
                    is_s = self.cond_of_tile(tt) == 0
                    zs = zst[it % 2]
                    ts_ = tst[it % 2]
                    if isattn and is_s:
                        self.ld(rc[it % 2][:], d["ropeC"][:, tt * 512:(tt + 1) * 512])
                        self.ld(rs[it % 2][:], d["ropeS"][:, tt * 512:(tt + 1) * 512])
                    for ji, (col, rows, fmc, scale, kind) in enumerate(fmj):
                        p = pp[pi % 6]; pi += 1
                        cc = col - c0
                        self.mm(p[0:rows, :], [(w[:, k, cc:cc + rows], h[:, k, :]) for k in range(KD)])
                        if kind == "glr":
                            self.cp(DVE, glst[it % 2][:, -2 - fmc, :], p[0:16, :])
                        elif kind == "rope" and is_s:
                            q = qf[ji % 2]
                            self.cp(ACT, q[:], p[:])
                            p2 = pq[ji % 2]
                            self.mm(p2[:], [(permf[:], q[:])])
                            tq = t1[ji % 2]
                            self.tt(DVE, tq[:], q[:], rc[it % 2][:], ALU.mult)
                            self.tt(DVE, q[:], p2[:], rs[it % 2][:], ALU.mult)
                            self.tt(DVE, zs[:, ji, :], tq[:], q[:], ALU.add)
                        else:
                            self.ts(DVE, zs[0:rows, ji, :], p[0:rows, :], scale, ALU.mult)
                    if fmj and fmj[0][4] != "glr":
                        f0 = fmj[0][2]
                        self.st(d["zfm"][tt][:, f0:f0 + len(fmj), :], zs[:, 0:len(fmj), :], [self.t_z[tt]])
                    if fmj and fmj[0][4] == "glr":
                        self.st(d["glr"][:, :, tt * 512:(tt + 1) * 512].rearrange("a r t -> r a t"), glst[it % 2][:], [self.t_z[tt]])
                    off = 0
                    for (col, ncols, tmc, func, scale) in tmj:
                        cc = col - c0
                        for tb in range(4):
                            p = pp[pi % 6]; pi += 1
                            self.mm(p[:, 0:ncols], [(h[:, k, tb * 128:(tb + 1) * 128], w[:, k, cc:cc + ncols]) for k in range(KD)])
                            if tmc == -1:
                                self.cp(ACT, mgst[it % 2][:, tb, :], p[:, 0:16])
                            else:
                                self.act(ts_[:, tb, off:off + ncols], p[:, 0:ncols], func, scale=scale)
                        if tmc == -1:
                            self.st(d["mg"][tt * 512:(tt + 1) * 512, :].rearrange("(b p) c -> p b c", p=128), mgst[it % 2][:], [self.t_z[tt]])
                        else:
                            self.st(d["ztm"][tt * 512:(tt + 1) * 512, tmc:tmc + ncols].rearrange("(b p) c -> p b c", p=128),
                                    ts_[:, :, off:off + ncols], [self.t_z[tt]])
                            off += ncols
                    if isattn and not is_s:
                        cc = O_AK - c0
                        for tb in range(4):
                            p = pp[pi % 6]; pi += 1
                            self.mm(p[:], [(h[:, k, tb * 128:(tb + 1) * 128], w[:, k, cc:cc + 512]) for k in range(KD)])
                            kv = kvst[tb % 2]
                            self.cp(ACT, kv[:], p[:])
                            pr, r0 = (tb * 128) // self.TP, (tb * 128) % self.TP
                            self.st(d["o_ak"][pr, l, r0:r0 + 128, :], kv[:, 0:256], [self.t_out])
                            self.st(d["o_av"][pr, l, r0:r0 + 128, :], kv[:, 256:512], [self.t_out])
                    it += 1
        fw.barrier()

    def phase_merge(self, l):
        d, fw, NTT = self.d, self.fw, self.NTT
        with contextlib.ExitStack() as st:
            wg = [self.sb(st, "wmg%d" % i, [128, 4, KD, 256], BF16) for i in range(2)]
            wr = [self.sb(st, "wbr%d" % i, [128, 4, 4, 256], BF16) for i in range(2)]
            hb = [self.sb(st, "hb%d" % i, [128, KD, 512], BF16) for i in range(2)]
            yb = [self.sb(st, "yb%d" % i, [128, 16, 512], BF16) for i in range(2)]
            gsb = [self.sb(st, "gsb%d" % i, [128, 512], F32) for i in range(3)]
            acc = [self.sb(st, "macc%d" % i, [128, 512], F32) for i in range(2)]
            mst = [self.sb(st, "mst%d" % i, [128, 2, 512], BF16) for i in range(2)]
            pg = [self.ps(st, "pg%d" % i, [128, 512]) for i in range(4)]
            pb = [self.ps(st, "pb%d" % i, [128, 512]) for i in range(4)]

            def wl(g):
                c0 = g * 256
                for b in range(4):
                    self.wload(wg[g % 2][:, b, :, :], d["w_mgate"][l, b].rearrange("(k p) n -> p k n", p=128)[:, :, c0:c0 + 256])
                self.wload(wr[g % 2][:], d["w_br"][l].rearrange("b (k p) n -> p b k n", p=128)[:, :, :, c0:c0 + 256])
            wl(0)
            it = 0
            gi = 0
            for g in range(8):
                if g + 1 < 8:
                    wl(g + 1)
                self.ld(hb[it % 2][:], d["hT"][0], [self.t_hT[0]])
                self.ld(yb[it % 2][:], d["ysT"][0], [self.t_ys[0]])
                for tt in range(NTT):
                    h, y = hb[it % 2], yb[it % 2]
                    if tt + 1 < NTT:
                        self.ld(hb[(it + 1) % 2][:], d["hT"][tt + 1], [self.t_hT[tt + 1]])
                        self.ld(yb[(it + 1) % 2][:], d["ysT"][tt + 1], [self.t_ys[tt + 1]])
                    ms = mst[it % 2]
                    for j in range(2):
                        a = acc[j]
                        for b in range(4):
                            p1 = pg[gi % 4]; p2 = pb[gi % 4]
                            self.mm(p1[:], [(wg[g % 2][:, b, k, j * 128:(j + 1) * 128], h[:, k, :]) for k in range(KD)])
                            self.mm(p2[:], [(wr[g % 2][:, b, k, j * 128:(j + 1) * 128], y[:, b * 4 + k, :]) for k in range(4)])
                            gs = gsb[gi % 3]
                            self.act(gs[:], p1[:], AF.Sigmoid)
                            if b == 0:
                                self.tt(DVE, a[:], gs[:], p2[:], ALU.mult)
                            elif b < 3:
                                self.tt(DVE, gs[:], gs[:], p2[:], ALU.mult)
                                self.tt(POOL, a[:], a[:], gs[:], ALU.add)
                            else:
                                self.tt(DVE, gs[:], gs[:], p2[:], ALU.mult)
                                self.tt(POOL, ms[:, j, :], a[:], gs[:], ALU.add)
                            gi += 1
                    self.st(d["mT"][tt][:, g * 2:g * 2 + 2, :], ms[:], [self.t_mT[tt]])
                    it += 1
        fw.barrier()

    def phase_resid(self, l, wsrc, nk, asrc, t_a, gsel, name):
        d, fw, NTT = self.d, self.fw, self.NTT
        xsrc = d["x"] if (l == 0 and gsel == 0) else d["xres"]
        wv = wsrc.rearrange("(k p) n -> p k n", p=128)
        with contextlib.ExitStack() as st:
            wb = [self.sb(st, "rw%d" % i, [128, nk, 512], BF16) for i in range(2)]
            ab = [self.sb(st, "ra%d" % i, [128, nk, 256], BF16) for i in range(2)]
            xb = [self.sb(st, "rx%d" % i, [128, 2, 512], F32) for i in range(3)]
            gbt = self.sb(st, "rg", [128, 2, D], F32)
            tmp = [self.sb(st, "rt%d" % i, [128, 512], F32) for i in range(2)]
            pp = [self.ps(st, "rp%d" % i, [128, 512]) for i in range(4)]
            for c in range(2):
                self.ld(gbt[:, c, :], d["gb"][l, gsel, c, :].partition_broadcast(128), [self.t_gb])
            self.wload(wb[0][:], wv[:, :, 0:512])
            nh = NTT * 2
            it = 0
            pi = 0
            for g in range(4):
                w = wb[g % 2]
                if g + 1 < 4:
                    self.wload(wb[(g + 1) % 2][:], wv[:, :, (g + 1) * 512:(g + 2) * 512])
                self.ld(ab[it % 2][:], asrc[0][:, :, 0:256], [t_a[0]])
                for hh in range(nh):
                    tt, half = hh // 2, hh % 2
                    a = ab[it % 2]
                    if hh + 1 < nh:
                        self.ld(ab[(it + 1) % 2][:], asrc[(hh + 1) // 2][:, :, ((hh + 1) % 2) * 256:((hh + 1) % 2) * 256 + 256], [t_a[(hh + 1) // 2]])
                    x = xb[it % 3]
                    r0 = hh * 256
                    xin = xsrc
                    self.ld(x[:], xin[r0:r0 + 256, g * 512:(g + 1) * 512].rearrange("(b p) c -> p b c", p=128), [self.t_x[hh * 2], self.t_x[hh * 2 + 1]])
                    cond = self.cond_of_tile(tt)
                    for tb in range(2):
                        p = pp[pi % 4]; pi += 1
                        self.mm(p[:], [(a[:, k, tb * 128:(tb + 1) * 128], w[:, k, :]) for k in range(nk)])
                        tm_ = tmp[pi % 2]
                        self.tt(DVE, tm_[:], p[:], gbt[:, cond, g * 512:(g + 1) * 512], ALU.mult)
                        self.tt(POOL, x[:, tb, :], x[:, tb, :], tm_[:], ALU.add)
                    self.st(d["xres"][r0:r0 + 256, g * 512:(g + 1) * 512].rearrange("(b p) c -> p b c", p=128), x[:], [self.t_x[hh * 2], self.t_x[hh * 2 + 1]])
                    it += 1
        fw.barrier()

    def phase_wout(self, l):
        d, fw, NTT = self.d, self.fw, self.NTT
        xsrc = d["x"] if l == 0 else d["xres"]
        wv = d["w_out"][l].rearrange("(k p) n -> p k n", p=128)
        with contextlib.ExitStack() as st:
            wb = self.sb(st, "ow", [128, KD, D], BF16)
            ab = [self.sb(st, "oa%d" % i, [128, KD, 512], BF16) for i in range(2)]
            xb = [self.sb(st, "ox%d" % i, [128, D], F32) for i in range(3)]
            gbt = self.sb(st, "og", [128, 2, D], F32)
            tmp = [self.sb(st, "ot%d" % i, [128, 512], F32) for i in range(2)]
            pp = [self.ps(st, "op%d" % i, [128, 512]) for i in range(4)]
            njunk = self.sb(st, "ojunk", [128, D], BF16)
            nss = [self.sb(st, "oss%d" % i, [128, 4], F32) for i in range(2)]
            nxn = [self.sb(st, "oxn%d" % i, [128, D], BF16) for i in range(2)]
            npt = [self.ps(st, "opt%d" % i, [128, 4, 128], BF16) for i in range(4)]
            nhst = [self.sb(st, "ohst%d" % i, [128, KD, 512], BF16) for i in range(2)]
            for c in range(2):
                self.ld(gbt[:, c, :], d["gb"][l, 0, c, :].partition_broadcast(128), [self.t_gb])
            for g in range(4):
                self.wload(wb[:, :, g * 512:(g + 1) * 512], wv[:, :, g * 512:(g + 1) * 512])
            nb = self.NT // 128
            self.ld(ab[0][:], d["mT"][0], [self.t_mT[0]])
            self.ld(xb[0][:], xsrc[0:128, :], [self.t_x[0]])
            pi = 0
            for i in range(nb):
                tt, tb = i // 4, i % 4
                if tb == 0 and tt + 1 < NTT:
                    self.ld(ab[(tt + 1) % 2][:], d["mT"][tt + 1], [self.t_mT[tt + 1]])
                if i + 1 < nb:
                    self.ld(xb[(i + 1) % 3][:], xsrc[(i + 1) * 128:(i + 2) * 128, :], [self.t_x[i + 1]])
                a, x = ab[tt % 2], xb[i % 3]
                cond = self.cond_of_tile(tt)
                for g in range(4):
                    p = pp[pi % 4]; pi += 1
                    self.mm(p[:], [(a[:, k, tb * 128:(tb + 1) * 128], wb[:, k, g * 512:(g + 1) * 512]) for k in range(KD)])
                    tm_ = tmp[pi % 2]
                    self.tt(DVE, tm_[:], p[:], gbt[:, cond, g * 512:(g + 1) * 512], ALU.mult)
                    self.tt(POOL, x[:, g * 512:(g + 1) * 512], x[:, g * 512:(g + 1) * 512], tm_[:], ALU.add)
                self.st(d["xres"][i * 128:(i + 1) * 128, :], x[:], [self.t_x[i]])
                hh_ = nhst[tt % 2]
                self.norm_block(st, x[:], cond, l, 1, hh_, tb, (njunk, nss[i % 2], nxn[i % 2], npt), i)
                if tb == 3:
                    self.st(d["hT"][tt], hh_[:], [self.t_hT[tt]])
        fw.barrier()

    def phase_down(self, l):
        self.phase_resid(l, self.d["ffn_w_down"][l], KF, self.d["aT"], self.t_aT, 1, "down")

    def phase_gu(self, l):
        d, fw, NTT = self.d, self.fw, self.NTT
        wv = d["ffn_w_gu"][l].rearrange("(k p) n -> p k n", p=128)
        with contextlib.ExitStack() as st:
            wb = [self.sb(st, "gw%d" % i, [128, KD, 1024], BF16) for i in range(2)]
            hb = [self.sb(st, "hb%d" % i, [128, KD, 512], BF16) for i in range(2)]
            sg = [self.sb(st, "sg%d" % i, [128, 512], F32) for i in range(3)]
            ast = [self.sb(st, "ast%d" % i, [128, 4, 512], BF16) for i in range(2)]
            pg = [self.ps(st, "pg%d" % i, [128, 512]) for i in range(4)]
            pu = [self.ps(st, "pu%d" % i, [128, 512]) for i in range(4)]

            def wl(g):
                self.wload(wb[g % 2][:, :, 0:512], wv[:, :, g * 512:(g + 1) * 512])
                self.wload(wb[g % 2][:, :, 512:1024], wv[:, :, FF + g * 512:FF + (g + 1) * 512])
            wl(0)
            it = 0
            pi = 0
            for g in range(11):
                w = wb[g % 2]
                if g + 1 < 11:
                    wl(g + 1)
                self.ld(hb[it % 2][:], d["hT"][0], [self.t_hT[0]])
                for tt in range(NTT):
                    h = hb[it % 2]
                    if tt + 1 < NTT:
                        self.ld(hb[(it + 1) % 2][:], d["hT"][tt + 1], [self.t_hT[tt + 1]])
                    a = ast[it % 2]
                    for j in range(4):
                        p1 = pg[pi % 4]; p2 = pu[pi % 4]; s = sg[pi % 3]; pi += 1
                        self.mm(p1[:], [(w[:, k, j * 128:(j + 1) * 128], h[:, k, :]) for k in range(KD)])
                        self.mm(p2[:], [(w[:, k, 512 + j * 128:512 + (j + 1) * 128], h[:, k, :]) for k in range(KD)])
                        self.act(s[:], p1[:], AF.Silu)
                        self.tt(DVE, a[:, j, :], s[:], p2[:], ALU.mult)
                    self.st(d["aT"][tt][:, g * 4:g * 4 + 4, :], a[:], [self.t_aT[tt]])
                    it += 1
        fw.barrier()

    def phase_final(self):
        d, fw = self.d, self.fw
        with contextlib.ExitStack() as st:
            xb = [self.sb(st, "fx%d" % i, [128, D], F32) for i in range(3)]
            junk = self.sb(st, "fjunk", [128, D], BF16)
            ss = [self.sb(st, "fss%d" % i, [128, 4], F32) for i in range(2)]
            fw_ = self.sb(st, "fnw", [128, D], F32)
            self.ld(fw_[:], d["final_norm_w"].partition_broadcast(128))
            nb = self.NT // 128
            self.ld(xb[0][:], d["xres"][0:128, :], [self.t_x[0]])
            for i in range(nb):
                if i + 1 < nb:
                    self.ld(xb[(i + 1) % 3][:], d["xres"][(i + 1) * 128:(i + 2) * 128, :], [self.t_x[i + 1]])
                x, s = xb[i % 3], ss[i % 2]
                self.act(junk[:], x[:], AF.Square, accum=s[:, 0:1])
                self.act(s[:, 1:2], s[:, 0:1], AF.Sqrt, bias=self.epsc[:, 0:1], scale=1.0 / D)
                self.recip(s[:, 2:3], s[:, 1:2])
                self.stt(DVE, x[:], x[:], s[:, 2:3], fw_[:], ALU.mult, ALU.mult)
                self.st(d["y"][i * 128:(i + 1) * 128, :], x[:], [self.t_out])
        fw.barrier()

    def phase_mix(self, l):
        _mix_impl(self, l)


def _mix_impl(self, l):
    fw = self.fw
    for (t0, Tn, pidx) in self.seqs:
        _attn(self, l, t0, Tn, pidx)
        fw.barrier()
        for h0 in (0, 2):
            (_mlstm if "m" in STREAMS else _mlstm_old)(self, l, t0, Tn, pidx, h0)
            fw.barrier()
            (_ret if "r" in STREAMS else _ret_old)(self, l, t0, Tn, pidx, h0)
            fw.barrier()
            (_gla if "g" in STREAMS else _gla_old)(self, l, t0, Tn, pidx, h0)
            fw.barrier()


def _tile_of(self, t0, blk):
    g = t0 // 128 + blk
    return g // 4, g % 4


def _ztiles(self, t0, Tn):
    return [self.t_z[tt] for tt in range(t0 // 512, (t0 + Tn - 1) // 512 + 1)]


def _to_ysT(self, y, br, h0, nh, tt, tb, t0, Tn, b, yst, ptr):
    d = self.d
    p = ptr[b % 2]
    for h in range(nh):
        self.tr(p[:, h, :], y[:, h * 128:(h + 1) * 128], self.C(C_ID, True))
    ys = yst[tt % 2]
    self.cp(ACT, ys[:, :, tb * 128:(tb + 1) * 128], p[:, 0:nh, :])
    last = (b == Tn // 128 - 1)
    if tb == 3 or last:
        tb0 = (t0 // 128) % 4 if (tt == (t0 // 128) // 4) else 0
        c0 = br * 4 + h0
        self.st(d["ysT"][tt][:, c0:c0 + nh, tb0 * 128:(tb + 1) * 128], ys[:, :, tb0 * 128:(tb + 1) * 128], [self.t_ys[tt]])


def _finish_branch(self, st, br, h0, l, t0, Tn, acc, gate_col, nw_name, nm, accB=None):
    d = self.d
    nblk = Tn // 128
    nwb = self.sb(st, nm + "nwb", [128, 256], F32)
    self.ld(nwb[:], d[nw_name][l, h0 * 128:h0 * 128 + 256].partition_broadcast(128))
    gt = [self.sb(st, nm + "gt%d" % i, [128, 256], BF16) for i in range(2)]
    ssq = [self.sb(st, nm + "ssq%d" % i, [128, 8], F32) for i in range(2)]
    junk = self.sb(st, nm + "fj", [128, 128], BF16)
    yb = [self.sb(st, nm + "yb%d" % i, [128, 256], BF16) for i in range(2)]
    yst = [self.sb(st, nm + "yst%d" % i, [128, 2, 512], BF16) for i in range(2)]
    bank = self.psbank(st, nm + "ptr", BF16)
    ptr = [self.sub(bank, i * 256, (i + 1) * 256, 2) for i in range(2)]
    def stage_a(b):
        tt, tb = _tile_of(self, t0, b)
        g = gt[b % 2]
        r0 = t0 + b * 128
        gc = gate_col + h0 * 128
        self.ld(g[:], d["ztm"][r0:r0 + 128, gc:gc + 256], [self.t_z[tt]])
        s = ssq[b % 2]
        for h in range(2):
            self.act(junk[:], acc[:, b, h * 128:(h + 1) * 128], AF.Square, accum=s[:, h:h + 1])
        self.act(s[:, 2:4], s[:, 0:2], AF.Sqrt, bias=self.epsc[:, 0:1], scale=1.0 / 128)
        self.recip(s[:, 4:6], s[:, 2:4])
        for h in range(2):
            self.ts(DVE, acc[:, b, h * 128:(h + 1) * 128], acc[:, b, h * 128:(h + 1) * 128], s[:, 4 + h:5 + h], ALU.mult)
        self.tt(POOL, acc[:, b, :], acc[:, b, :], nwb[:], ALU.mult)
        self.tt(DVE, yb[b % 2][:], acc[:, b, :], g[:], ALU.mult)

    stage_a(0)
    for b in range(nblk):
        if b + 1 < nblk:
            stage_a(b + 1)
        tt, tb = _tile_of(self, t0, b)
        _to_ysT(self, yb[b % 2], br, h0, 2, tt, tb, t0, Tn, b, yst, ptr)


def _scan_loads(self, st, nm, t0, Tn, fmchunks, tmcols, rows=128):
    d = self.d
    nblk = Tn // 128
    ntm = sum(n for _, n in tmcols)
    fm = self.sb(st, nm + "_fm", [128, sum(n for _, n in fmchunks), Tn], BF16)
    tm = self.sb(st, nm + "_tm", [128, nblk, ntm], BF16)
    for b4 in range(0, Tn, 512):
        n = min(512, Tn - b4)
        tt = (t0 + b4) // 512
        o = (t0 + b4) % 512
        fo = 0
        for (c0, cn) in fmchunks:
            self.ld(fm[0:rows, fo:fo + cn, b4:b4 + n], d["zfm"][tt][0:rows, c0:c0 + cn, o:o + n], [self.t_z[tt]])
            fo += cn
        nb4 = n // 128
        off = 0
        for (c0, cn) in tmcols:
            self.ld(tm[:, b4 // 128:b4 // 128 + nb4, off:off + cn],
                    d["ztm"][t0 + b4:t0 + b4 + n, c0:c0 + cn].rearrange("(b p) c -> p b c", p=128), [self.t_z[tt]])
            off += cn
    return fm, tm


def _attn(self, l, t0, Tn, pidx):
    d = self.d
    is_s = pidx < 0
    nblk = Tn // 128
    SC = 128 ** -0.5
    with contextlib.ExitStack() as st:
        qk = self.sb(st, "a_qk", [128, 6, Tn], BF16)
        v = self.sb(st, "a_v", [128, nblk, 256], BF16)
        snk = self.sb(st, "a_snk", [128, 8], F32)
        kc = self.sb(st, "a_kc", [128, 2, 256], BF16)
        vc = self.sb(st, "a_vc", [128, 2, 256], BF16)
        ktm = self.sb(st, "a_ktm", [128, 2, 256], BF16)
        ssb = [self.sb(st, "a_s%d" % i, [128, 384], F32) for i in range(2)]
        pb = [self.sb(st, "a_p%d" % i, [128, 640], BF16) for i in range(2)]
        pT = [self.sb(st, "a_pT%d" % i, [128, 5, 128], BF16) for i in range(2)]
        sm = [self.sb(st, "a_sm%d" % i, [128, 8], F32) for i in range(4)]
        y = [self.sb(st, "a_y%d" % i, [128, 512], BF16) for i in range(2)]
        yst = [self.sb(st, "a_yst%d" % i, [128, 4, 512], BF16) for i in range(2)]
        ps_s = [T(self.psbank(st, "a_pss%d" % i)) for i in range(2)]
        bk = self.psbank(st, "a_psc")
        ps_c = [self.sub(bk, i * 256, (i + 1) * 256) for i in range(2)]
        ps_t = self.sub(self.psbank(st, "a_pst", BF16), 0, 640, 5)
        bk = self.psbank(st, "a_pso")
        ps_o = [self.sub(bk, i * 128, (i + 1) * 128) for i in range(2)]
        bk = self.psbank(st, "a_ptr", BF16)
        ptr = [self.sub(bk, i * 512, (i + 1) * 512, 4) for i in range(2)]
        for b4 in range(0, Tn, 512):
            n = min(512, Tn - b4)
            tt = (t0 + b4) // 512
            o = (t0 + b4) % 512
            self.ld(qk[:, :, b4:b4 + n], d["zfm"][tt][:, 0:6, o:o + n], [self.t_z[tt]])
        self.ld(v[:], d["ztm"][t0:t0 + Tn, T_AV:T_AV + 256].rearrange("(b p) c -> p b c", p=128), _ztiles(self, t0, Tn))
        self.ld(snk[:, 0:4], d["attn_sink"][l, :].partition_broadcast(128))
        self.ts(DVE, snk[:, 4:8], snk[:, 0:4], -1.0, ALU.mult)
        if is_s:
            self.ld(ktm[:], d["cak"][l].rearrange("(b p) c -> p b c", p=128), eng=POOL)
            self.ld(vc[:], d["cav"][l].rearrange("(b p) c -> p b c", p=128), eng=POOL)
            for kb in range(2):
                for kv in range(2):
                    self.tr(ps_t[:, kv, :], ktm[:, kb, kv * 128:(kv + 1) * 128], self.C(C_ID, True))
                self.cp(DVE, kc[:, :, kb * 128:(kb + 1) * 128], ps_t[:, 0:2, :])
        it = 0
        for b in range(nblk):
            tt, tb = _tile_of(self, t0, b)
            yb = y[b % 2]
            for h in range(4):
                kv = h // 2
                q = qk[:, h, b * 128:(b + 1) * 128]
                smt = sm[it % 4]
                p = pb[it % 2]
                pss = ps_s[it % 2]
                if is_s:
                    k0 = max(0, b - 1)
                    k1 = min(nblk, b + 2)
                    nk = (k1 - k0) * 128
                    self.mm(pss[:, 0:nk], [(q, qk[:, 4 + kv, k0 * 128:k1 * 128])])
                    psc = ps_c[it % 2]
                    self.mm(psc[:], [(q, kc[:, kv, :])])
                    s = ssb[it % 2]
                    off = 0
                    if b > 0:
                        self.tt(DVE, s[:, 0:128], pss[:, 0:128], self.C(C_MP), ALU.add)
                        off = 128
                    self.cp(DVE, s[:, off:off + 128], pss[:, off:off + 128])
                    if b + 1 < nblk:
                        self.tt(DVE, s[:, off + 128:off + 256], pss[:, off + 128:off + 256], self.C(C_MN), ALU.add)
                    self.red(smt[:, 0:1], s[:, 0:nk], ALU.max)
                    self.red(smt[:, 1:2], psc[:], ALU.max)
                    self.tt(DVE, smt[:, 0:1], smt[:, 0:1], smt[:, 1:2], ALU.max)
                    self.ts(DVE, smt[:, 2:3], smt[:, 0:1], -SC, ALU.mult, snk[:, 4 + h:5 + h], ALU.min)
                    self.act(p[:, 0:nk], s[:, 0:nk], AF.Exp, bias=smt[:, 2:3], scale=SC, accum=smt[:, 3:4])
                    self.act(p[:, nk:nk + 256], psc[:], AF.Exp, bias=smt[:, 2:3], scale=SC, accum=smt[:, 4:5])
                    self.act(smt[:, 5:6], snk[:, h:h + 1], AF.Exp, bias=smt[:, 2:3])
                    self.tt(DVE, smt[:, 3:4], smt[:, 3:4], smt[:, 4:5], ALU.add)
                    self.tt(DVE, smt[:, 3:4], smt[:, 3:4], smt[:, 5:6], ALU.add)
                    ntot = nk + 256
                    vlist = [v[:, kb, kv * 128:(kv + 1) * 128] for kb in range(k0, k1)] + [vc[:, kb, kv * 128:(kv + 1) * 128] for kb in range(2)]
                else:
                    self.mm(pss[:, 0:Tn], [(q, qk[:, 4 + kv, :])])
                    self.red(smt[:, 0:1], pss[:, 0:Tn], ALU.max)
                    self.ts(DVE, smt[:, 2:3], smt[:, 0:1], -SC, ALU.mult, snk[:, 4 + h:5 + h], ALU.min)
                    self.act(p[:, 0:Tn], pss[:, 0:Tn], AF.Exp, bias=smt[:, 2:3], scale=SC, accum=smt[:, 3:4])
                    self.act(smt[:, 5:6], snk[:, h:h + 1], AF.Exp, bias=smt[:, 2:3])
                    self.tt(DVE, smt[:, 3:4], smt[:, 3:4], smt[:, 5:6], ALU.add)
                    ntot = Tn
                    vlist = [v[:, kb, kv * 128:(kv + 1) * 128] for kb in range(nblk)]
                self.recip(smt[:, 6:7], smt[:, 3:4])
                nkb = ntot // 128
                for kb in range(nkb):
                    self.tr(ps_t[:, kb, :], p[:, kb * 128:(kb + 1) * 128], self.C(C_ID, True))
                ptt = pT[it % 2]
                self.cp(ACT, ptt[:, 0:nkb, :], ps_t[:, 0:nkb, :])
                pso = ps_o[it % 2]
                self.mm(pso[:], [(ptt[:, kb, :], vlist[kb]) for kb in range(nkb)])
                self.ts(DVE, yb[:, h * 128:(h + 1) * 128], pso[:], smt[:, 6:7], ALU.mult)
                it += 1
            _to_ysT(self, yb, 0, 0, 4, tt, tb, t0, Tn, b, yst, ptr)


def _run_streams(gens):
    gens = list(gens)
    if os.environ.get("KSEQ"):
        for g in gens:
            for _ in g:
                pass
        return
    while gens:
        for g in list(gens):
            try:
                next(g)
            except StopIteration:
                gens.remove(g)


def _mlstm(self, l, t0, Tn, pidx, h0):
    d = self.d
    is_s = pidx < 0
    nblk = Tn // 128
    with contextlib.ExitStack() as st:
        acc = self.sb(st, "m_acc", [128, nblk, 256], F32)
        accB = self.sb(st, "m_accB", [128, nblk, 256], F32)
        with contextlib.ExitStack() as st2:
            fm, tm = _scan_loads(self, st2, "m", t0, Tn, [(6 + h0, 2), (10 + h0, 2)], [(T_MK + h0 * 128, 256), (T_MV + h0 * 128, 256)])
            G = self.sb(st2, "m_G", [128, nblk, 16], F32)
            gb = self.sb(st2, "m_gb", [128, 16], F32)
            lf = self.sb(st2, "m_lf", [128, nblk, 8], F32)
            ig = self.sb(st2, "m_ig", [128, nblk, 8], F32)
            bb = self.sb(st2, "m_b", [128, nblk, 8], F32)
            bl = self.sb(st2, "m_bl", [128, nblk, 8], F32)
            ew = self.sb(st2, "m_ew", [128, nblk, 8], F32)
            wn = self.sb(st2, "m_wn", [128, nblk, 8], F32)
            enb = self.sb(st2, "m_enb", [128, nblk, 8], F32)
            ebl = self.sb(st2, "m_ebl", [128, nblk, 8], F32)
            tmpg = self.sb(st2, "m_tmpg", [128, nblk, 8], F32)
            vaug = self.sb(st2, "m_vaug", [128, nblk, 2, 129], BF16)
            em0 = self.sb(st2, "m_em0", [128, 16], F32)
            bk0 = self.psbank(st2, "m_ps0")
            ps_g = self.sub(bk0, 0, 16)
            self.ld(G[:], d["mg"][t0:t0 + Tn, :].rearrange("(b p) c -> p b c", p=128), _ztiles(self, t0, Tn))
            self.ld(gb[:], d["mlstm_if_b"][l, :].partition_broadcast(128))
            for b in range(nblk):
                self.tt(DVE, G[:, b, :], G[:, b, :], gb[:], ALU.add)
                for dd in range(2):
                    self.cp(DVE, ig[:, b, dd * 4:dd * 4 + 4], G[:, b, dd * 8:dd * 8 + 4])
                    self.act(lf[:, b, dd * 4:dd * 4 + 4], G[:, b, dd * 8 + 4:dd * 8 + 8], AF.Exp, scale=-1.0)
            self.act(lf[:], lf[:], AF.Ln, bias=self.epsc[:, 1:2])
            self.ts(DVE, lf[:], lf[:], -1.0, ALU.mult)
            for b in range(nblk):
                self.mm(ps_g[:, 0:4], [(self.C(C_LE), lf[:, b, 0:4])])
                self.mm(ps_g[:, 4:8], [(self.C(C_GE), lf[:, b, 4:8])])
                self.mm(ps_g[:, 8:16], [(self.C(C_ONE), lf[:, b, :])])
                self.cp(DVE, bb[:, b, :], ps_g[:, 0:8])
                self.cp(DVE, bl[:, b, :], ps_g[:, 8:16])
            self.tt(DVE, tmpg[:], ig[:], bb[:], ALU.subtract)
            self.act(ew[:], tmpg[:], AF.Exp)
            self.tt(DVE, tmpg[:], tmpg[:], bl[:], ALU.add)
            self.act(wn[:], tmpg[:], AF.Exp)
            self.act(enb[:], bb[:], AF.Exp, scale=-1.0)
            self.act(ebl[:], bl[:], AF.Exp)
            self.memset(POOL, vaug[:], 1.0)
            for b in range(nblk):
                self.cp(POOL, vaug[:, b, :, 0:128], tm.v(tm.ap[:, b, 256:512].rearrange("p (h e) -> p h e", h=2)))
            if not is_s:
                assert nblk == 2
                mcol = self.sb(st2, "m_mcol", [8, 4], F32)
                gT = self.sb(st2, "m_gT", [8, 2], F32)
                blc = self.sb(st2, "m_blc", [8, 2], F32)
                mF = self.sb(st2, "m_mF", [8, 2], F32)
                mrep = self.sb(st2, "m_mrep", [8, 128], F32)
                ps_t = T(bk0[0:8, 16:144])
                ps_c = T(bk0[0:8, 144:146])
                ps_b = self.sub(bk0, 160, 168)
                for b in range(nblk):
                    self.tr(ps_t[:], tmpg[:, b, :], self.C(C_ID))
                    self.red(gT[:, b:b + 1], ps_t[:], ALU.max)
                    self.mm(ps_c[:, b:b + 1], [(lf[:, b, :], self.C(C_ONE)[:, 0:1])])
                self.cp(DVE, blc[:], ps_c[:])
                for (col, b0, b1) in ((0, 0, 1), (1, 1, 0)):
                    self.tt(DVE, mcol[:, 0:1], blc[:, b0:b0 + 1], gT[:, b0:b0 + 1], ALU.max)
                    self.tt(DVE, mcol[:, 1:2], mcol[:, 0:1], blc[:, b1:b1 + 1], ALU.add)
                    self.tt(DVE, mF[:, col:col + 1], mcol[:, 1:2], gT[:, b1:b1 + 1], ALU.max)
                self.tt(DVE, mF[:, 0:1], mF[:, 0:1], self.C(C_Z)[0:8, 2:3], ALU.mult)
                self.tt(DVE, mF[:, 1:2], mF[:, 1:2], self.C(C_Z)[0:8, 3:4], ALU.mult)
                self.tt(DVE, mcol[:, 2:3], mF[:, 0:1], mF[:, 1:2], ALU.add)
                if h0 == 0:
                    self.st(d["o_mm"][pidx, l, :].rearrange("(p o) -> p o", o=1), mcol[:, 2:3], [self.t_out])
                self.ts(DVE, mrep[:], self.C(C_ONE)[0:8, :], mcol[:, 2:3], ALU.mult)
                self.mm(ps_b[:], [(mrep[:], self.C(C_ID)[0:8, 0:8])])
                self.act(em0[:, 8:16], ps_b[:], AF.Exp, scale=-1.0)
            else:
                self.ld(em0[:, 0:8], d["smm"][l, :].partition_broadcast(128))
                self.act(em0[:, 0:8], em0[:, 0:8], AF.Exp)

            def stream(dd, h):
                j = dd * 4 + h0 + h
                S = self.sb(st2, "m_S%d" % j, [128, 129], F32)
                Sb = [self.sb(st2, "m_Sb%d_%d" % (j, i), [128, 129], BF16) for i in range(2)]
                P = [self.sb(st2, "m_P%d_%d" % (j, i), [128, 128], BF16) for i in range(2)]
                kw = [self.sb(st2, "m_kw%d_%d" % (j, i), [128, 128], BF16) for i in range(2)]
                dn = [self.sb(st2, "m_dn%d_%d" % (j, i), [128, 4], F32) for i in range(2)]
                bk = self.psbank(st2, "m_psS%d" % j)
                pa, po, pS = self.sub(bk, 0, 128), self.sub(bk, 128, 257), self.sub(bk, 320, 449)
                A = acc if dd == 0 else accB
                if is_s:
                    self.ld(S[:, 0:128], d["smC"][l, dd, h0 + h])
                    self.ld(S[:, 128:129], d["smn"][l, dd, h0 + h].rearrange("(k o) -> k o", o=1))
                    self.ts(DVE, S[:], S[:], em0[:, j:j + 1], ALU.mult)
                else:
                    self.memset(DVE, S[:], 0.0)
                self.cp(ACT, Sb[0][:], S[:])
                yield
                order = list(range(nblk)) if dd == 0 else list(range(nblk - 1, -1, -1))
                mask = self.C(C_LE) if dd == 0 else self.C(C_GE)
                for bi, b in enumerate(order):
                    sl = slice(b * 128, (b + 1) * 128)
                    Pm, kwt, dnt = P[bi % 2], kw[bi % 2], dn[bi % 2]
                    sb_cur, sb_nxt = Sb[bi % 2], Sb[(bi + 1) % 2]
                    self.mm(pa[:], [(fm[:, 2 + h, sl], fm[:, h, sl])])
                    self.ts(POOL, kwt[:], tm[:, b, h * 128:(h + 1) * 128], wn[:, b, j:j + 1], ALU.mult)
                    yield
                    self.stt(DVE, Pm[:], pa[:], ew[:, b, j:j + 1], mask, ALU.mult, ALU.mult)
                    self.mm(pS[:], [(kwt[:], vaug[:, b, h, :])])
                    yield
                    self.mm(po[:], [(Pm[:], vaug[:, b, h, :]), (fm[:, h, sl], sb_cur[:])])
                    self.stt(DVE, S[:], S[:], ebl[:, b, j:j + 1], pS[:], ALU.mult, ALU.add)
                    yield
                    self.act(dnt[:, 0:1], po[:, 128:129], AF.Abs)
                    self.cp(ACT, sb_nxt[:], S[:])
                    yield
                    self.tt(DVE, dnt[:, 0:1], dnt[:, 0:1], enb[:, b, j:j + 1], ALU.max)
                    self.recip(dnt[:, 1:2], dnt[:, 0:1])
                    yield
                    self.act(A[:, b, h * 128:(h + 1) * 128], po[:, 0:128], AF.Identity, scale=dnt[:, 1:2])
                    yield
                if not is_s:
                    self.ts(DVE, S[:], S[:], em0[:, 8 + j:9 + j], ALU.mult)
                    self.st(d["o_mC"][pidx, l, dd, h0 + h], S[:, 0:128], [self.t_out])
                    self.st(d["o_mn"][pidx, l, dd, h0 + h].rearrange("(k o) -> k o", o=1), S[:, 128:129], [self.t_out])
            _run_streams([stream(dd, h) for dd in range(2) for h in range(2)])
        self.fw.barrier()
        with contextlib.ExitStack() as st3:
            _finish_branch(self, st3, 1, h0, l, t0, Tn, acc, T_MO, "mlstm_norm_w", "mB", accB)


def _ret(self, l, t0, Tn, pidx, h0):
    d = self.d
    is_s = pidx < 0
    nblk = Tn // 128
    with contextlib.ExitStack() as st:
        acc = self.sb(st, "r_acc", [128, nblk, 256], F32)
        accB = self.sb(st, "r_accB", [128, nblk, 256], F32)
        with contextlib.ExitStack() as st2:
            fm, tm = _scan_loads(self, st2, "r", t0, Tn, [(14 + h0, 2), (18 + h0, 2)], [(T_RK + h0 * 128, 256), (T_RV + h0 * 128, 256)])
            lg = self.sb(st2, "r_lg", [128, 8], F32)
            gc = self.sb(st2, "r_gc", [128, 8], F32)
            self.ld(lg[:], d["ret_decay"][l, :].partition_broadcast(128))
            self.act(lg[:], lg[:], AF.Exp, scale=-1.0)
            self.act(lg[:], lg[:], AF.Ln, bias=self.epsc[:, 1:2])
            self.ts(DVE, lg[:], lg[:], -1.0, ALU.mult)
            self.act(gc[:], lg[:], AF.Exp, scale=128.0)

            def stream(dd, h):
                j = dd * 4 + h0 + h
                M = self.sb(st2, "r_M%d" % j, [128, 128], F32)
                xi = self.sb(st2, "r_xi%d" % j, [128, 128], F32)
                ze = self.sb(st2, "r_ze%d" % j, [128, 1], F32)
                S = self.sb(st2, "r_S%d" % j, [128, 128], F32)
                Sb = [self.sb(st2, "r_Sb%d_%d" % (j, i), [128, 128], BF16) for i in range(2)]
                P = [self.sb(st2, "r_P%d_%d" % (j, i), [128, 128], BF16) for i in range(2)]
                qx = [self.sb(st2, "r_qx%d_%d" % (j, i), [128, 128], BF16) for i in range(2)]
                kz = [self.sb(st2, "r_kz%d_%d" % (j, i), [128, 128], BF16) for i in range(2)]
                bk = self.psbank(st2, "r_psS%d" % j)
                pa, po, pS = self.sub(bk, 0, 128), self.sub(bk, 128, 256), self.sub(bk, 256, 384)
                A = acc if dd == 0 else accB
                self.act(M[:], self.C(C_DF if dd == 0 else C_DB), AF.Exp, scale=lg[:, j:j + 1])
                self.act(xi[:], self.C(C_PF if dd == 0 else C_PB), AF.Exp, scale=lg[:, j:j + 1])
                self.act(ze[:], self.C(C_Z)[:, dd:dd + 1], AF.Exp, scale=lg[:, j:j + 1])
                if is_s:
                    self.ld(S[:], d["srS"][l, dd, h0 + h])
                else:
                    self.memset(DVE, S[:], 0.0)
                self.cp(ACT, Sb[0][:], S[:])
                yield
                order = list(range(nblk)) if dd == 0 else list(range(nblk - 1, -1, -1))
                for bi, b in enumerate(order):
                    sl = slice(b * 128, (b + 1) * 128)
                    Pm, q_, kzt = P[bi % 2], qx[bi % 2], kz[bi % 2]
                    sb_cur, sb_nxt = Sb[bi % 2], Sb[(bi + 1) % 2]
                    self.mm(pa[:], [(fm[:, 2 + h, sl], fm[:, h, sl])])
                    self.ts(POOL, kzt[:], tm[:, b, h * 128:(h + 1) * 128], ze[:, 0:1], ALU.mult)
                    self.tt(POOL, q_[:], fm[:, h, sl], xi[:], ALU.mult)
                    yield
                    self.tt(DVE, Pm[:], pa[:], M[:], ALU.mult)
                    self.mm(pS[:], [(kzt[:], tm[:, b, 256 + h * 128:256 + (h + 1) * 128])])
                    yield
                    self.mm(po[:], [(Pm[:], tm[:, b, 256 + h * 128:256 + (h + 1) * 128]), (q_[:], sb_cur[:])])
                    self.stt(DVE, S[:], S[:], gc[:, j:j + 1], pS[:], ALU.mult, ALU.add)
                    yield
                    self.cp(ACT, A[:, b, h * 128:(h + 1) * 128], po[:])
                    self.cp(ACT, sb_nxt[:], S[:])
                    yield
                if not is_s:
                    self.st(d["o_rS"][pidx, l, dd, h0 + h], S[:], [self.t_out])
            _run_streams([stream(dd, h) for dd in range(2) for h in range(2)])
        self.fw.barrier()
        with contextlib.ExitStack() as st3:
            _finish_branch(self, st3, 2, h0, l, t0, Tn, acc, T_RG, "ret_norm_w", "rB", accB)


def _gla(self, l, t0, Tn, pidx, h0):
    d = self.d
    is_s = pidx < 0
    nblk = Tn // 128
    with contextlib.ExitStack() as st:
        acc = self.sb(st, "g_acc", [128, nblk, 256], F32)
        accB = self.sb(st, "g_accB", [128, nblk, 256], F32)
        with contextlib.ExitStack() as st2:
            fm, tm = _scan_loads(self, st2, "g", t0, Tn, [(22 + h0, 2), (26 + h0, 2)], [(T_GK + h0 * 64, 128), (T_GV + h0 * 128, 256)], rows=64)
            lr = self.sb(st2, "g_lr", [16, Tn], F32)
            w2 = self.sb(st2, "g_w2", [16, 2, 256], F32)
            bg = self.sb(st2, "g_bg", [1, 2, 256], F32)
            la = self.sb(st2, "g_la", [128, 2, nblk, 128], F32)
            bkl = self.psbank(st2, "g_psl")
            ps_l = [self.sub(bkl, i * 128, (i + 1) * 128) for i in range(4)]
            bkc = self.psbank(st2, "g_psc")
            self.ld(w2[:], d["gla_w2"][l].rearrange("a r c -> r a c"))
            self.ld(bg[:], d["gla_b"][l:l + 1, :, :])
            c0 = h0 * 64
            for dd in range(2):
                self.ld(lr[:], d["glr"][dd, :, t0:t0 + Tn], _ztiles(self, t0, Tn))
                for b in range(nblk):
                    sl = slice(b * 128, (b + 1) * 128)
                    p = ps_l[b % 4]
                    self.mm(p[:], [(lr[:, sl], w2[:, dd, c0:c0 + 128]), (self.C(C_ONE)[0:1, :], bg[:, dd, c0:c0 + 128])])
                    self.act(la[:, dd, b, :], p[:], AF.Exp, scale=-1.0)
            self.act(la[:], la[:], AF.Ln, bias=self.epsc[:, 1:2])

            def stream(dd, h, si):
                j = dd * 4 + h0 + h
                S = self.sb(st2, "g_S%d" % j, [64, 128], F32)
                Sb = [self.sb(st2, "g_Sb%d_%d" % (j, i), [64, 128], BF16) for i in range(2)]
                eq = [self.sb(st2, "g_eq%d_%d" % (j, i), [64, 128], F32) for i in range(2)]
                ek = [self.sb(st2, "g_ek%d_%d" % (j, i), [64, 128], F32) for i in range(2)]
                qp = [self.sb(st2, "g_qp%d_%d" % (j, i), [64, 128], BF16) for i in range(2)]
                kp = [self.sb(st2, "g_kp%d_%d" % (j, i), [64, 128], BF16) for i in range(2)]
                ekk = [self.sb(st2, "g_ekk%d_%d" % (j, i), [128, 64], F32) for i in range(2)]
                kpp = [self.sb(st2, "g_kpp%d_%d" % (j, i), [128, 64], BF16) for i in range(2)]
                P = [self.sb(st2, "g_P%d_%d" % (j, i), [128, 128], BF16) for i in range(2)]
                bk = self.psbank(st2, "g_psS%d" % j)
                pr, pa, po = self.sub(bk, 0, 64), self.sub(bk, 64, 192), self.sub(bk, 192, 320)
                pS = T(bk[0:64, 320:448])
                pc = T(bkc[0:64, si * 128:(si + 1) * 128])
                A = acc if dd == 0 else accB
                if is_s:
                    self.ld(S[:], d["sgS"][l, dd, h0 + h])
                else:
                    self.memset(DVE, S[:], 0.0)
                self.cp(ACT, Sb[0][:], S[:])
                yield
                order = list(range(nblk)) if dd == 0 else list(range(nblk - 1, -1, -1))
                tri = self.C(C_LE) if dd == 0 else self.C(C_GE)
                trs = self.C(C_GT) if dd == 0 else self.C(C_LT)
                for bi, b in enumerate(order):
                    sl = slice(b * 128, (b + 1) * 128)
                    i2 = bi % 2
                    sb_cur, sb_nxt = Sb[i2], Sb[(bi + 1) % 2]
                    lah = la[:, dd, b, h * 64:(h + 1) * 64]
                    self.mm(pc[:], [(lah, tri)])
                    self.mm(pr[:], [(trs, lah)])
                    yield
                    self.act(eq[i2][:], pc[:], AF.Exp, scale=-1.0 / 16)
                    self.act(ek[i2][:], pc[:], AF.Exp, scale=1.0 / 16)
                    self.act(ekk[i2][:], pr[:], AF.Exp, scale=-1.0 / 16)
                    yield
                    self.tt(DVE, qp[i2][:], fm[0:64, h, sl], eq[i2][:], ALU.mult)
                    self.tt(POOL, kp[i2][:], fm[0:64, 2 + h, sl], ek[i2][:], ALU.mult)
                    self.tt(DVE, kpp[i2][:], tm[:, b, h * 64:(h + 1) * 64], ekk[i2][:], ALU.mult)
                    yield
                    self.mm(pa[:], [(kp[i2][:], qp[i2][:])])
                    self.mm(pS[:], [(kpp[i2][:], tm[:, b, 128 + h * 128:128 + (h + 1) * 128])])
                    yield
                    self.tt(DVE, P[i2][:], pa[:], tri, ALU.mult)
                    lastc = eq[i2][:, 127:128] if dd == 0 else eq[i2][:, 0:1]
                    self.stt(DVE, S[:], S[:], lastc, pS[:], ALU.mult, ALU.add)
                    yield
                    self.mm(po[:], [(P[i2][:], tm[:, b, 128 + h * 128:128 + (h + 1) * 128]), (qp[i2][:], sb_cur[:])])
                    self.cp(ACT, sb_nxt[:], S[:])
                    yield
                    self.cp(ACT, A[:, b, h * 128:(h + 1) * 128], po[:])
                    yield
                if not is_s:
                    self.st(d["o_gS"][pidx, l, dd, h0 + h], S[:], [self.t_out])
            _run_streams([stream(dd, h, dd * 2 + h) for dd in range(2) for h in range(2)])
        self.fw.barrier()
        with contextlib.ExitStack() as st3:
            _finish_branch(self, st3, 3, h0, l, t0, Tn, acc, T_GG, "gla_norm_w", "gB", accB)


def _mlstm_old(self, l, t0, Tn, pidx, h0):
    d = self.d
    is_s = pidx < 0
    nblk = Tn // 128
    with contextlib.ExitStack() as st:
        acc = self.sb(st, "m_acc", [128, nblk, 256], F32)
        with contextlib.ExitStack() as st2:
            fm, tm = _scan_loads(self, st2, "m", t0, Tn, [(6 + h0, 2), (10 + h0, 2)], [(T_MK + h0 * 128, 256), (T_MV + h0 * 128, 256)])
            G = self.sb(st2, "m_G", [128, nblk, 16], F32)
            gb = self.sb(st2, "m_gb", [128, 16], F32)
            lf = self.sb(st2, "m_lf", [128, nblk, 8], F32)
            ig = self.sb(st2, "m_ig", [128, nblk, 8], F32)
            bb = self.sb(st2, "m_b", [128, nblk, 8], F32)
            bl = self.sb(st2, "m_bl", [128, nblk, 8], F32)
            ew = self.sb(st2, "m_ew", [128, nblk, 8], F32)
            wn = self.sb(st2, "m_wn", [128, nblk, 8], F32)
            enb = self.sb(st2, "m_enb", [128, nblk, 8], F32)
            ebl = self.sb(st2, "m_ebl", [128, nblk, 8], F32)
            tmpg = self.sb(st2, "m_tmpg", [128, nblk, 8], F32)
            vaug = self.sb(st2, "m_vaug", [128, nblk, 2, 129], BF16)
            S = self.sb(st2, "m_S", [128, 8, 129], F32)
            Sb = [self.sb(st2, "m_Sb%d" % i, [128, 8, 129], BF16) for i in range(2)]
            em0 = self.sb(st2, "m_em0", [128, 16], F32)
            P = [self.sb(st2, "m_P%d" % i, [128, 128], BF16) for i in range(3)]
            kw = [self.sb(st2, "m_kw%d" % i, [128, 128], BF16) for i in range(3)]
            dn = [self.sb(st2, "m_dn%d" % i, [128, 4], F32) for i in range(4)]
            bk0 = self.psbank(st2, "m_ps0")
            ps_g = self.sub(bk0, 0, 16)
            bk = self.psbank(st2, "m_psa")
            ps_a = [self.sub(bk, i * 256, (i + 1) * 256, 2) for i in range(2)]
            bk1, bk2 = self.psbank(st2, "m_pso0"), self.psbank(st2, "m_pso1")
            ps_o = [self.sub(bk1, 0, 256), self.sub(bk1, 256, 512), self.sub(bk2, 0, 256), self.sub(bk2, 256, 512)]
            bk = self.psbank(st2, "m_pss")
            ps_s = [self.sub(bk, 0, 256), self.sub(bk, 256, 512)]
            self.ld(G[:], d["mg"][t0:t0 + Tn, :].rearrange("(b p) c -> p b c", p=128), _ztiles(self, t0, Tn))
            self.ld(gb[:], d["mlstm_if_b"][l, :].partition_broadcast(128))
            for b in range(nblk):
                self.tt(DVE, G[:, b, :], G[:, b, :], gb[:], ALU.add)
                for dd in range(2):
                    self.cp(DVE, ig[:, b, dd * 4:dd * 4 + 4], G[:, b, dd * 8:dd * 8 + 4])
                    self.act(lf[:, b, dd * 4:dd * 4 + 4], G[:, b, dd * 8 + 4:dd * 8 + 8], AF.Exp, scale=-1.0)
            self.act(lf[:], lf[:], AF.Ln, bias=self.epsc[:, 1:2])
            self.ts(DVE, lf[:], lf[:], -1.0, ALU.mult)
            for b in range(nblk):
                self.mm(ps_g[:, 0:4], [(self.C(C_LE), lf[:, b, 0:4])])
                self.mm(ps_g[:, 4:8], [(self.C(C_GE), lf[:, b, 4:8])])
                self.mm(ps_g[:, 8:16], [(self.C(C_ONE), lf[:, b, :])])
                self.cp(DVE, bb[:, b, :], ps_g[:, 0:8])
                self.cp(DVE, bl[:, b, :], ps_g[:, 8:16])
            self.tt(DVE, tmpg[:], ig[:], bb[:], ALU.subtract)
            self.act(ew[:], tmpg[:], AF.Exp)
            self.tt(DVE, tmpg[:], tmpg[:], bl[:], ALU.add)
            self.act(wn[:], tmpg[:], AF.Exp)
            self.act(enb[:], bb[:], AF.Exp, scale=-1.0)
            self.act(ebl[:], bl[:], AF.Exp)
            self.memset(POOL, vaug[:], 1.0)
            for b in range(nblk):
                self.cp(POOL, vaug[:, b, :, 0:128], tm.v(tm.ap[:, b, 256:512].rearrange("p (h e) -> p h e", h=2)))
            if not is_s:
                assert nblk == 2
                mcol = self.sb(st2, "m_mcol", [8, 4], F32)
                gT = self.sb(st2, "m_gT", [8, 2], F32)
                blc = self.sb(st2, "m_blc", [8, 2], F32)
                mF = self.sb(st2, "m_mF", [8, 2], F32)
                mrep = self.sb(st2, "m_mrep", [8, 128], F32)
                ps_t = T(bk0[0:8, 16:144])
                ps_c = T(bk0[0:8, 144:146])
                ps_b = self.sub(bk0, 160, 168)
                for b in range(nblk):
                    self.tr(ps_t[:], tmpg[:, b, :], self.C(C_ID))
                    self.red(gT[:, b:b + 1], ps_t[:], ALU.max)
                    self.mm(ps_c[:, b:b + 1], [(lf[:, b, :], self.C(C_ONE)[:, 0:1])])
                self.cp(DVE, blc[:], ps_c[:])
                for (col, b0, b1) in ((0, 0, 1), (1, 1, 0)):
                    self.tt(DVE, mcol[:, 0:1], blc[:, b0:b0 + 1], gT[:, b0:b0 + 1], ALU.max)
                    self.tt(DVE, mcol[:, 1:2], mcol[:, 0:1], blc[:, b1:b1 + 1], ALU.add)
                    self.tt(DVE, mF[:, col:col + 1], mcol[:, 1:2], gT[:, b1:b1 + 1], ALU.max)
                self.tt(DVE, mF[:, 0:1], mF[:, 0:1], self.C(C_Z)[0:8, 2:3], ALU.mult)
                self.tt(DVE, mF[:, 1:2], mF[:, 1:2], self.C(C_Z)[0:8, 3:4], ALU.mult)
                self.tt(DVE, mcol[:, 2:3], mF[:, 0:1], mF[:, 1:2], ALU.add)
                if h0 == 0:
                    with self.nc.allow_non_contiguous_dma(reason="tiny"):
                        self.st(d["o_mm"][pidx, l, :].rearrange("(p o) -> p o", o=1), mcol[:, 2:3], [self.t_out])
                self.ts(DVE, mrep[:], self.C(C_ONE)[0:8, :], mcol[:, 2:3], ALU.mult)
                self.mm(ps_b[:], [(mrep[:], self.C(C_ID)[0:8, 0:8])])
                self.act(em0[:, 8:16], ps_b[:], AF.Exp, scale=-1.0)
            if is_s:
                with self.nc.allow_non_contiguous_dma(reason="state n vectors"):
                    for dd in range(2):
                        self.ld(S[:, dd * 4:dd * 4 + 4, 0:128], d["smC"][l, dd].rearrange("h k e -> k h e"))
                        self.ld(S[:, dd * 4:dd * 4 + 4, 128:129], d["smn"][l, dd].rearrange("h (k o) -> k h o", o=1))
                self.ld(em0[:, 0:8], d["smm"][l, :].partition_broadcast(128))
                self.act(em0[:, 0:8], em0[:, 0:8], AF.Exp)
                for j in range(8):
                    self.ts(DVE, S[:, j, :], S[:, j, :], em0[:, j:j + 1], ALU.mult)
            else:
                self.memset(DVE, S[:], 0.0)
            self.cp(ACT, Sb[0][:], S[:])
            it = 0
            for dd in range(2):
                order = list(range(nblk)) if dd == 0 else list(range(nblk - 1, -1, -1))
                mask = self.C(C_LE) if dd == 0 else self.C(C_GE)
                for bi, b in enumerate(order):
                    sl = slice(b * 128, (b + 1) * 128)
                    pa = ps_a[bi % 2]
                    for h in range(2):
                        self.mm(pa[:, h, :], [(fm[:, 2 + h, sl], fm[:, h, sl])])
                    sb_cur = Sb[bi % 2]
                    sb_nxt = Sb[(bi + 1) % 2]
                    for h in range(2):
                        j = dd * 4 + h0 + h
                        Pm = P[it % 3]; kwt = kw[it % 3]; dnt = dn[it % 4]
                        self.stt(DVE, Pm[:], pa[:, h, :], ew[:, b, j:j + 1], mask, ALU.mult, ALU.mult)
                        po = ps_o[it % 4]
                        self.mm(po[:, 0:129], [(Pm[:], vaug[:, b, h, :]), (fm[:, h, sl], sb_cur[:, j, :])])
                        self.act(dnt[:, 0:1], po[:, 128:129], AF.Abs)
                        self.tt(DVE, dnt[:, 0:1], dnt[:, 0:1], enb[:, b, j:j + 1], ALU.max)
                        self.recip(dnt[:, 1:2], dnt[:, 0:1])
                        a_ = acc[:, b, h * 128:(h + 1) * 128]
                        if dd == 0:
                            self.act(a_, po[:, 0:128], AF.Identity, scale=dnt[:, 1:2])
                        else:
                            self.stt(DVE, a_, po[:, 0:128], dnt[:, 1:2], a_, ALU.mult, ALU.add)
                        self.ts(POOL, kwt[:], tm[:, b, h * 128:(h + 1) * 128], wn[:, b, j:j + 1], ALU.mult)
                        pS = ps_s[it % 2]
                        self.mm(pS[:, 0:129], [(kwt[:], vaug[:, b, h, :])])
                        self.stt(DVE, S[:, j, :], S[:, j, :], ebl[:, b, j:j + 1], pS[:, 0:129], ALU.mult, ALU.add)
                        self.cp(ACT, sb_nxt[:, j, :], S[:, j, :])
                        it += 1
            if not is_s:
                with self.nc.allow_non_contiguous_dma(reason="state n vectors"):
                    for dd in range(2):
                        for h in range(2):
                            j = dd * 4 + h0 + h
                            self.ts(DVE, S[:, j, :], S[:, j, :], em0[:, 8 + j:9 + j], ALU.mult)
                            self.st(d["o_mC"][pidx, l, dd, h0 + h], S[:, j, 0:128], [self.t_out])
                            self.st(d["o_mn"][pidx, l, dd, h0 + h].rearrange("(k o) -> k o", o=1), S[:, j, 128:129], [self.t_out])
        self.fw.barrier()
        with contextlib.ExitStack() as st3:
            _finish_branch(self, st3, 1, h0, l, t0, Tn, acc, T_MO, "mlstm_norm_w", "mB")


def _ret_old(self, l, t0, Tn, pidx, h0):
    d = self.d
    is_s = pidx < 0
    nblk = Tn // 128
    with contextlib.ExitStack() as st:
        acc = self.sb(st, "r_acc", [128, nblk, 256], F32)
        with contextlib.ExitStack() as st2:
            fm, tm = _scan_loads(self, st2, "r", t0, Tn, [(14 + h0, 2), (18 + h0, 2)], [(T_RK + h0 * 128, 256), (T_RV + h0 * 128, 256)])
            lg = self.sb(st2, "r_lg", [128, 8], F32)
            M = self.sb(st2, "r_M", [128, 8, 128], F32)
            xi = self.sb(st2, "r_xi", [128, 8, 128], F32)
            ze = self.sb(st2, "r_ze", [128, 8], F32)
            gc = self.sb(st2, "r_gc", [128, 8], F32)
            S = self.sb(st2, "r_S", [128, 8, 128], F32)
            Sb = [self.sb(st2, "r_Sb%d" % i, [128, 8, 128], BF16) for i in range(2)]
            P = [self.sb(st2, "r_P%d" % i, [128, 2, 128], BF16) for i in range(2)]
            qx = [self.sb(st2, "r_qx%d" % i, [128, 2, 128], BF16) for i in range(2)]
            kz = [self.sb(st2, "r_kz%d" % i, [128, 128], BF16) for i in range(3)]
            bk = self.psbank(st2, "r_psa")
            ps_a = [self.sub(bk, i * 256, (i + 1) * 256, 2) for i in range(2)]
            bk = self.psbank(st2, "r_pso")
            ps_o = [self.sub(bk, i * 256, (i + 1) * 256) for i in range(2)]
            bk = self.psbank(st2, "r_pss")
            ps_s = [self.sub(bk, i * 128, (i + 1) * 128) for i in range(3)]
            self.ld(lg[:], d["ret_decay"][l, :].partition_broadcast(128))
            self.act(lg[:], lg[:], AF.Exp, scale=-1.0)
            self.act(lg[:], lg[:], AF.Ln, bias=self.epsc[:, 1:2])
            self.ts(DVE, lg[:], lg[:], -1.0, ALU.mult)
            for dd in range(2):
                for h in range(2):
                    j = dd * 4 + h0 + h
                    self.act(M[:, j, :], self.C(C_DF if dd == 0 else C_DB), AF.Exp, scale=lg[:, j:j + 1])
                    self.act(xi[:, j, :], self.C(C_PF if dd == 0 else C_PB), AF.Exp, scale=lg[:, j:j + 1])
                    self.act(ze[:, j:j + 1], self.C(C_Z)[:, dd:dd + 1], AF.Exp, scale=lg[:, j:j + 1])
            self.act(gc[:], lg[:], AF.Exp, scale=128.0)
            if is_s:
                for dd in range(2):
                    self.ld(S[:, dd * 4:dd * 4 + 4, :], d["srS"][l, dd].rearrange("h k e -> k h e"))
            else:
                self.memset(DVE, S[:], 0.0)
            self.cp(ACT, Sb[0][:], S[:])
            it = 0
            for dd in range(2):
                order = list(range(nblk)) if dd == 0 else list(range(nblk - 1, -1, -1))
                j0 = dd * 4 + h0
                for bi, b in enumerate(order):
                    sl = slice(b * 128, (b + 1) * 128)
                    pa = ps_a[bi % 2]
                    for h in range(2):
                        self.mm(pa[:, h, :], [(fm[:, 2 + h, sl], fm[:, h, sl])])
                    Pm = P[bi % 2]
                    self.tt(DVE, Pm[:], pa[:], M[:, j0:j0 + 2, :], ALU.mult)
                    q_ = qx[bi % 2]
                    self.tt(POOL, q_[:], fm[:, 0:2, sl], xi[:, j0:j0 + 2, :], ALU.mult)
                    sb_cur = Sb[bi % 2]; sb_nxt = Sb[(bi + 1) % 2]
                    po = ps_o[bi % 2]
                    for h in range(2):
                        j = j0 + h
                        self.mm(po[:, h * 128:(h + 1) * 128], [(Pm[:, h, :], tm[:, b, 256 + h * 128:256 + (h + 1) * 128]), (q_[:, h, :], sb_cur[:, j, :])])
                    if dd == 0:
                        self.cp(ACT, acc[:, b, :], po[:])
                    else:
                        self.tt(DVE, acc[:, b, :], acc[:, b, :], po[:], ALU.add)
                    for h in range(2):
                        j = j0 + h
                        kzt = kz[it % 3]
                        self.ts(POOL, kzt[:], tm[:, b, h * 128:(h + 1) * 128], ze[:, j:j + 1], ALU.mult)
                        pS = ps_s[it % 3]
                        self.mm(pS[:], [(kzt[:], tm[:, b, 256 + h * 128:256 + (h + 1) * 128])])
                        self.stt(DVE, S[:, j, :], S[:, j, :], gc[:, j:j + 1], pS[:], ALU.mult, ALU.add)
                        self.cp(ACT, sb_nxt[:, j, :], S[:, j, :])
                        it += 1
            if not is_s:
                for dd in range(2):
                    for h in range(2):
                        self.st(d["o_rS"][pidx, l, dd, h0 + h], S[:, dd * 4 + h0 + h, :], [self.t_out])
        self.fw.barrier()
        with contextlib.ExitStack() as st3:
            _finish_branch(self, st3, 2, h0, l, t0, Tn, acc, T_RG, "ret_norm_w", "rB")


def _gla_old(self, l, t0, Tn, pidx, h0):
    d = self.d
    is_s = pidx < 0
    nblk = Tn // 128
    with contextlib.ExitStack() as st:
        acc = self.sb(st, "g_acc", [128, nblk, 256], F32)
        with contextlib.ExitStack() as st2:
            fm, tm = _scan_loads(self, st2, "g", t0, Tn, [(22 + h0, 2), (26 + h0, 2)], [(T_GK + h0 * 64, 128), (T_GV + h0 * 128, 256)], rows=64)
            lr = self.sb(st2, "g_lr", [16, Tn], F32)
            w2 = self.sb(st2, "g_w2", [16, 2, 256], F32)
            bg = self.sb(st2, "g_bg", [1, 2, 256], F32)
            lap = [self.sb(st2, "g_lap%d" % i, [128, 128], F32) for i in range(2)]
            eq = [self.sb(st2, "g_eq%d" % i, [64, 128], F32) for i in range(3)]
            ek = [self.sb(st2, "g_ek%d" % i, [64, 128], F32) for i in range(3)]
            qp = [self.sb(st2, "g_qp%d" % i, [64, 128], BF16) for i in range(3)]
            kp = [self.sb(st2, "g_kp%d" % i, [64, 128], BF16) for i in range(3)]
            ekk = [self.sb(st2, "g_ekk%d" % i, [128, 64], F32) for i in range(3)]
            kpp = [self.sb(st2, "g_kpp%d" % i, [128, 64], BF16) for i in range(3)]
            P = [self.sb(st2, "g_P%d" % i, [128, 128], BF16) for i in range(3)]
            S = self.sb(st2, "g_S", [64, 8, 128], F32)
            Sb = [self.sb(st2, "g_Sb%d" % i, [64, 8, 128], BF16) for i in range(2)]
            bk0 = self.psbank(st2, "g_ps0")
            ps_l = self.sub(bk0, 0, 128)
            ps_c = [self.sub(bk0, 128, 256), self.sub(bk0, 256, 384)]
            ps_r = [self.sub(bk0, 384, 448), self.sub(bk0, 448, 512)]
            bk1 = self.psbank(st2, "g_ps1")
            ps_a = [self.sub(bk1, 0, 128), self.sub(bk1, 128, 256)]
            ps_s = [T(bk1[0:64, 256:384]), T(bk1[0:64, 384:512])]
            bk = self.psbank(st2, "g_pso")
            ps_o = [self.sub(bk, i * 256, (i + 1) * 256) for i in range(2)]
            with self.nc.allow_non_contiguous_dma(reason="small params"):
                self.ld(w2[:], d["gla_w2"][l].rearrange("a r c -> r a c"))
            self.ld(bg[:], d["gla_b"][l:l + 1, :, :])
            if is_s:
                for dd in range(2):
                    self.ld(S[:, dd * 4:dd * 4 + 4, :], d["sgS"][l, dd].rearrange("h k e -> k h e"))
            else:
                self.memset(DVE, S[:], 0.0)
            self.cp(ACT, Sb[0][:], S[:])
            it = 0
            for dd in range(2):
                self.ld(lr[:], d["glr"][dd, :, t0:t0 + Tn], _ztiles(self, t0, Tn))
                order = list(range(nblk)) if dd == 0 else list(range(nblk - 1, -1, -1))
                tri = self.C(C_LE) if dd == 0 else self.C(C_GE)
                trs = self.C(C_GT) if dd == 0 else self.C(C_LT)
                c0 = h0 * 64
                for bi, b in enumerate(order):
                    sl = slice(b * 128, (b + 1) * 128)
                    la = lap[bi % 2]
                    self.mm(ps_l[:], [(lr[:, sl], w2[:, dd, c0:c0 + 128]), (self.C(C_ONE)[0:1, :], bg[:, dd, c0:c0 + 128])])
                    self.act(la[:], ps_l[:], AF.Exp, scale=-1.0)
                    self.act(la[:], la[:], AF.Ln, bias=self.epsc[:, 1:2])
                    sb_cur = Sb[bi % 2]; sb_nxt = Sb[(bi + 1) % 2]
                    po = ps_o[bi % 2]
                    for h in range(2):
                        j = dd * 4 + h0 + h
                        lah = la[:, h * 64:(h + 1) * 64]
                        pc = ps_c[it % 2]
                        self.mm(pc[0:64, :], [(lah, tri)])
                        e1 = eq[it % 3]; e2 = ek[it % 3]
                        self.act(e1[:], pc[0:64, :], AF.Exp, scale=-1.0 / 16)
                        self.act(e2[:], pc[0:64, :], AF.Exp, scale=1.0 / 16)
                        q_ = qp[it % 3]; k_ = kp[it % 3]
                        self.tt(DVE, q_[:], fm[0:64, h, sl], e1[:], ALU.mult)
                        self.tt(POOL, k_[:], fm[0:64, 2 + h, sl], e2[:], ALU.mult)
                        pa = ps_a[it % 2]
                        self.mm(pa[:], [(k_[:], q_[:])])
                        Pm = P[it % 3]
                        self.tt(DVE, Pm[:], pa[:], tri, ALU.mult)
                        self.mm(po[:, h * 128:(h + 1) * 128], [(Pm[:], tm[:, b, 128 + h * 128:128 + (h + 1) * 128]), (q_[:], sb_cur[:, j, :])])
                        pr = ps_r[it % 2]
                        self.mm(pr[:], [(trs, lah)])
                        e3 = ekk[it % 3]
                        self.act(e3[:], pr[:], AF.Exp, scale=-1.0 / 16)
                        k2 = kpp[it % 3]
                        self.tt(DVE, k2[:], tm[:, b, h * 64:(h + 1) * 64], e3[:], ALU.mult)
                        pS = ps_s[it % 2]
                        self.mm(pS[:], [(k2[:], tm[:, b, 128 + h * 128:128 + (h + 1) * 128])])
                        lastc = e1[:, 127:128] if dd == 0 else e1[:, 0:1]
                        self.stt(DVE, S[:, j, :], S[:, j, :], lastc, pS[:], ALU.mult, ALU.add)
                        self.cp(ACT, sb_nxt[:, j, :], S[:, j, :])
                        it += 1
                    if dd == 0:
                        self.cp(ACT, acc[:, b, :], po[:])
                    else:
                        self.tt(DVE, acc[:, b, :], acc[:, b, :], po[:], ALU.add)
            if not is_s:
                for dd in range(2):
                    for h in range(2):
                        self.st(d["o_gS"][pidx, l, dd, h0 + h], S[:, dd * 4 + h0 + h, :], [self.t_out])
        self.fw.barrier()
        with contextlib.ExitStack() as st3:
            _finish_branch(self, st3, 3, h0, l, t0, Tn, acc, T_GG, "gla_norm_w", "gB")


def make_consts(TS):
    c = np.zeros((128, NCST, 128), np.float32)
    i = np.arange(128)
    s, t = i[:, None], i[None, :]
    c[:, C_ID] = (s == t)
    c[:, C_LE] = (s <= t)
    c[:, C_GE] = (s >= t)
    c[:, C_GT] = (s > t)
    c[:, C_LT] = (s < t)
    c[:, C_DF] = np.where(t >= s, t - s, 1e6)
    c[:, C_DB] = np.where(s >= t, s - t, 1e6)
    c[:, C_ONE] = 1.0
    c[:, C_PF] = (t + 1) * np.ones((128, 1))
    c[:, C_PB] = (128 - t) * np.ones((128, 1))
    c[:, C_Z, 0] = 127 - i
    c[:, C_Z, 1] = i
    c[:, C_Z, 2] = (i < 4)
    c[:, C_Z, 3] = (i >= 4)
    c[:, C_MP] = np.where(t >= s, 0.0, -30000.0)
    c[:, C_MN] = np.where(t <= s, 0.0, -30000.0)
    cst = c.reshape(128, NCST * 128)
    tt = np.arange(TS)
    row = (tt // 64).astype(np.float32)
    col = (tt % 64).astype(np.float32)
    inv = (10000.0 ** (-np.arange(32, dtype=np.float32) / 32)).astype(np.float32)
    C = np.zeros((128, TS), np.float32)
    Sg = np.zeros((128, TS), np.float32)
    for p in range(128):
        pos = row if p < 64 else col
        ang = (pos * inv[p % 32]).astype(np.float32)
        C[p] = np.cos(ang)
        sn = np.sin(ang)
        Sg[p] = -sn if (p % 64) < 32 else sn
    perm = np.zeros((128, 128), np.float32)
    for m in range(128):
        k = m + 32 if (m % 64) < 32 else m - 32
        perm[k, m] = 1.0
    return cst, C, Sg, perm


_CACHE = {}


def kernel(**inp):
    return _run(inp, 8, L)


def _run(inp, NC, nlayers, debug=False):
    inp = {k: np.asarray(v) for k, v in inp.items()}
    TS = inp["x_sample"].shape[1]
    TP = inp["x_prompt"].shape[1]
    NP = inp["x_prompt"].shape[0] // NC
    key = (TS, TP, NP, nlayers, debug)
    if key not in _CACHE:
        _CACHE[key] = K(TS, TP, NP, nlayers, debug).build()
    nc = _CACHE[key]
    cst, rc, rs, perm = make_consts(TS)
    f = lambda a: np.ascontiguousarray(a, dtype=np.float32)
    shared = {k: f(inp[k]) for k in ["w_ada", "b_ada", "norm1_w", "norm2_w", "w_in", "attn_sink", "mlstm_norm_w", "ret_norm_w",
                                      "gla_w2", "gla_b", "gla_norm_w", "w_br", "w_mgate", "w_out", "ffn_w_gu", "ffn_w_down", "final_norm_w"]}
    shared["mlstm_if_b"] = f(inp["mlstm_if_b"].reshape(L, 16))
    shared["ret_decay"] = f(inp["ret_decay"].reshape(L, 8))
    shared.update(cst=cst, ropeC=rc, ropeS=rs, perm=perm)
    in_maps = []
    for c in range(NC):
        m = dict(shared)
        m["x"] = f(np.concatenate([inp["x_sample"][c]] + [inp["x_prompt"][c * NP + i] for i in range(NP)], axis=0))
        m["cak"] = f(inp["cache_attn_k"][c].reshape(L, 256, 256))
        m["cav"] = f(inp["cache_attn_v"][c].reshape(L, 256, 256))
        m["smC"] = f(inp["state_mlstm_C"][c]); m["smn"] = f(inp["state_mlstm_n"][c]); m["smm"] = f(inp["state_mlstm_m"][c].reshape(L, 8))
        m["srS"] = f(inp["state_ret_S"][c]); m["sgS"] = f(inp["state_gla_S"][c])
        m["cond"] = f(np.stack([inp["c"][c], inp["c_ctx"]], axis=0))
        in_maps.append(m)
    res = run_bass_kernel_spmd(nc, in_maps, core_ids=list(range(NC)))
    R = res.results
    if debug:
        global _DBG
        _DBG = R
    y_s = np.stack([R[c]["y"][:TS] for c in range(NC)], axis=0)
    y_p = np.stack([R[c]["y"][TS + i * TP:TS + (i + 1) * TP] for c in range(NC) for i in range(NP)], axis=0)
    cat = lambda k: np.concatenate([R[c][k] for c in range(NC)], axis=0)
    B = NC * NP
    return (y_p.astype(np.float32), y_s.astype(np.float32),
            cat("o_ak").reshape(B, L, TP, 2, 128), cat("o_av").reshape(B, L, TP, 2, 128),
            cat("o_mC"), cat("o_mn"), cat("o_mm").reshape(B, L, 2, 4), cat("o_rS"), cat("o_gS"))
```

```python
import contextlib
import numpy as np
import concourse.bass as bass
import concourse.mybir as mybir
from concourse.bass_utils import run_bass_kernel_spmd

F32 = mybir.dt.float32
BF16 = mybir.dt.bfloat16
ALU = mybir.AluOpType
AF = mybir.ActivationFunctionType
AX = mybir.AxisListType
PE, DVE, ACT, POOL, SP = "tensor", "vector", "scalar", "gpsimd", "sync"
ENGINES = [PE, DVE, ACT, POOL, SP]

D = 2048
KD = 16
L = 2
FF = 5632
KF = 44
DIN = 6704
EPS = 1e-6
NF = 30
NTM = 4608
O_AQ, O_AK, O_AV, O_MQ, O_MK, O_MV, O_MO, O_MG = 0, 512, 768, 1024, 1536, 2048, 2560, 3072
O_RQ, O_RK, O_RV, O_RG, O_GQ, O_GK, O_GV, O_GG, O_GLR = 3088, 3600, 4112, 4624, 5136, 5392, 5648, 6160, 6672
T_AV, T_MK, T_MV, T_MO, T_RK, T_RV, T_RG, T_GK, T_GV, T_GG = 0, 256, 768, 1280, 1792, 2304, 2816, 3328, 3584, 4096
C_ID, C_LE, C_GE, C_GT, C_LT, C_DF, C_DB, C_ONE, C_PF, C_PB, C_Z, C_MP, C_MN = range(13)
NCST = 13
import os
STREAMS = os.environ.get('KSTREAMS', 'none')


class V:
    __slots__ = ("t", "ap")

    def __init__(self, t, ap):
        self.t = t
        self.ap = ap

    def __getitem__(self, k):
        return V(self.t, self.ap[k])


class T:
    _n = 0

    def __init__(self, ap=None, name="", partial=False):
        self.ap = ap
        self.name = name
        self.partial = partial
        self.writers = []
        self.readers = []
        T._n += 1
        self.id = T._n

    def __getitem__(self, k):
        return V(self, self.ap[k])

    def v(self, ap):
        return V(self, ap)


class Op:
    __slots__ = ("eng", "fn", "deps", "needed", "val", "dma_key", "is_dma")

    def __init__(self, eng, fn, is_dma=False, dma_key=None):
        self.eng = eng
        self.fn = fn
        self.deps = []
        self.needed = False
        self.val = None
        self.is_dma = is_dma
        self.dma_key = dma_key


class FW:
    def __init__(self, nc):
        self.nc = nc
        self.ops = {e: [] for e in ENGINES}
        self.pending_dma = []
        self.nops = 0
        self.slot_of = {}
        self.dma_count = {}

    def _record(self, op, reads, writes):
        deps = []
        for t in reads:
            deps.extend(t.writers)
        for t in writes:
            if t.partial:
                deps.extend(t.readers)
            else:
                deps.extend(t.writers)
                deps.extend(t.readers)
        seen = set()
        for d in deps:
            if d is op or id(d) in seen:
                continue
            seen.add(id(d))
            if (not d.is_dma) and d.eng == op.eng and not op.is_dma and op.eng == PE:
                continue
            op.deps.append(d)
            d.needed = True
        for t in reads:
            t.readers.append(op)
            if len(t.readers) > 64:
                t.readers = t.readers[-64:] if False else t.readers
        for t in writes:
            if t.partial:
                if t.readers:
                    t.writers = [op]
                    t.readers = []
                else:
                    t.writers.append(op)
            else:
                t.writers = [op]
                t.readers = []
        self.ops[op.eng].append(op)
        self.nops += 1
        return op

    def op(self, eng, fn, reads=(), writes=()):
        return self._record(Op(eng, fn), list(reads), list(writes))

    def dma(self, eng, out_ap, in_ap, reads=(), writes=(), key=None):
        if key.id not in self.slot_of:
            self.slot_of[key.id] = len(self.slot_of)
        slot = self.slot_of[key.id]
        o = Op(eng, lambda e: e.dma_start(out=out_ap, in_=in_ap, allow_slow_non_contiguous=True), is_dma=True, dma_key=slot)
        self.dma_count[slot] = self.dma_count.get(slot, 0) + 16
        o.val = self.dma_count[slot]
        o.needed = True
        self.pending_dma.append(o)
        return self._record(o, list(reads), list(writes))

    def barrier(self):
        b = Op(SP, lambda e: e.nop())
        for e in ENGINES:
            if e != SP and self.ops[e]:
                d = self.ops[e][-1]
                d.needed = True
                b.deps.append(d)
        for d in self.pending_dma:
            b.deps.append(d)
        self.pending_dma = []
        self.slot_of = {}
        b.needed = True
        self.ops[SP].append(b)
        for e in ENGINES:
            if e != SP:
                o = Op(e, lambda en: en.nop())
                o.deps.append(b)
                self.ops[e].append(o)

    def emit(self):
        nc = self.nc
        dma_counts = {}
        for e in ENGINES:
            cnt = 0
            for o in self.ops[e]:
                if o.is_dma:
                    dma_counts[o.dma_key] = 1
                elif o.needed:
                    cnt += 1
                    o.val = cnt
        sems = {}
        stack = contextlib.ExitStack()
        with stack:
            for e in ENGINES:
                sems[("eng", e)] = stack.enter_context(nc.semaphore("c_" + e))
            for k in dma_counts:
                sems[("dma", k)] = stack.enter_context(nc.semaphore("d_%d" % k))
            self.n_sems = len(sems)
            final = {}
            for e in ENGINES:
                for o in self.ops[e]:
                    if o.val is not None:
                        k = ("dma", o.dma_key) if o.is_dma else ("eng", o.eng)
                        final[k] = max(final.get(k, 0), o.val)
            block = stack.enter_context(nc.Block())

            def run(engname, eng):
                waited = {}
                for o in self.ops[engname]:
                    need = {}
                    for d in o.deps:
                        k = ("dma", d.dma_key) if d.is_dma else ("eng", d.eng)
                        if d.val > need.get(k, 0):
                            need[k] = d.val
                    for k, v in need.items():
                        if waited.get(k, 0) < v:
                            eng.wait_ge(sems[k], v)
                            waited[k] = v
                    inst = o.fn(eng)
                    if o.is_dma:
                        inst.then_inc(sems[("dma", o.dma_key)], 16)
                    elif o.needed:
                        inst.then_inc(sems[("eng", engname)], 1)
                if engname == SP:
                    for k, v in final.items():
                        if waited.get(k, 0) < v:
                            eng.wait_ge(sems[k], v)

            @block.tensor
            def _(eng):
                run(PE, eng)

            @block.vector
            def _(eng):
                run(DVE, eng)

            @block.scalar
            def _(eng):
                run(ACT, eng)

            @block.gpsimd
            def _(eng):
                run(POOL, eng)

            @block.sync
            def _(eng):
                run(SP, eng)


def _ts(xs):
    return [x.t for x in xs if isinstance(x, V)]


def _a(x):
    return x.ap if isinstance(x, V) else x


class K:
    def __init__(self, TS, TP, NP, nlayers=L, debug=False):
        self.debug = debug
        self.TS, self.TP, self.NP = TS, TP, NP
        self.NT = TS + TP * NP
        assert TS % 512 == 0 and TP * NP == 512 and TP % 128 == 0
        self.NTT = self.NT // 512
        self.nl = nlayers
        self.nc = bass.Bass("TRN2", target_bir_lowering=False)
        self.fw = FW(self.nc)
        self.seqs = [(0, TS, -1)] + [(TS + i * TP, TP, i) for i in range(NP)]

    def tt(self, eng, out, a, b, op):
        self.fw.op(eng, lambda e: e.tensor_tensor(_a(out), _a(a), _a(b), op), _ts([a, b]), _ts([out]))

    def ts(self, eng, out, a, s1, op0, s2=None, op1=None, accum=None):
        def fn(e):
            kw = {}
            if accum is not None:
                kw["accum_out"] = _a(accum)
            if op1 is None:
                return e.tensor_scalar(_a(out), _a(a), _a(s1), None, op0, **kw)
            return e.tensor_scalar(_a(out), _a(a), _a(s1), _a(s2), op0, op1, **kw)
        self.fw.op(eng, fn, _ts([a, s1, s2]), _ts([out, accum]))

    def stt(self, eng, out, a, s, b, op0, op1):
        self.fw.op(eng, lambda e: e.scalar_tensor_tensor(_a(out), _a(a), _a(s), _a(b), op0, op1), _ts([a, s, b]), _ts([out]))

    def act(self, out, a, func, bias=None, scale=None, accum=None):
        def fn(e):
            kw = {}
            if bias is not None:
                kw["bias"] = _a(bias)
            if scale is not None:
                kw["scale"] = _a(scale)
            if accum is not None:
                kw["accum_out"] = _a(accum)
            return e.activation(out=_a(out), in_=_a(a), func=func, **kw)
        self.fw.op(ACT, fn, _ts([a, bias, scale]), _ts([out, accum]))

    def cp(self, eng, out, a):
        if eng == ACT:
            self.fw.op(ACT, lambda e: e.copy(_a(out), _a(a)), _ts([a]), _ts([out]))
        else:
            self.fw.op(eng, lambda e: e.tensor_copy(_a(out), _a(a)), _ts([a]), _ts([out]))

    def memset(self, eng, out, val):
        self.fw.op(eng, lambda e: e.memset(_a(out), val), [], _ts([out]))

    def red(self, out, a, op):
        self.fw.op(DVE, lambda e: e.tensor_reduce(_a(out), _a(a), AX.X, op), _ts([a]), _ts([out]))

    def mm(self, out, pairs, extra_reads=()):
        def fn(e):
            n = len(pairs)
            inst = None
            for i, (l, r) in enumerate(pairs):
                inst = e.matmul(_a(out), _a(l), _a(r), start=(i == 0), stop=(i == n - 1))
            return inst
        rd = []
        for l, r in pairs:
            rd += _ts([l, r])
        self.fw.op(PE, fn, rd + list(extra_reads), _ts([out]))

    def tr(self, out, a, ident):
        self.fw.op(PE, lambda e: e.transpose(_a(out), _a(a), _a(ident)), _ts([a, ident]), _ts([out]))

    def ld(self, out, src_ap, src_ts=(), eng=SP):
        self.fw.dma(eng, _a(out), src_ap, reads=list(src_ts), writes=[out.t], key=out.t)

    def st(self, dst_ap, a, dst_ts=(), eng=SP):
        self.fw.dma(eng, dst_ap, _a(a), reads=[a.t], writes=list(dst_ts), key=a.t)

    def uniq(self, name):
        self._u = getattr(self, "_u", 0) + 1
        return "%s_u%d" % (name, self._u)

    def sb(self, st, name, shape, dt=F32):
        return T(st.enter_context(self.nc.sbuf_tensor(self.uniq(name), shape, dt)), name)

    def ps(self, st, name, shape, dt=F32):
        return T(st.enter_context(self.nc.psum_tensor(self.uniq(name), shape, dt)), name)

    def psbank(self, st, name, dt=F32):
        return st.enter_context(self.nc.psum_tensor(self.uniq(name), [128, 512 if dt == F32 else 1024], dt))

    def sub(self, bank, a, b, n3=None):
        ap = bank[:, a:b]
        if n3:
            ap = ap.rearrange("p (a b) -> p a b", a=n3)
        return T(ap)

    def recip(self, out, a):
        self.fw.op(DVE, lambda e: e.reciprocal(_a(out), _a(a)), _ts([a]), _ts([out]))

    def dram(self, name, shape, dt, kind="Internal"):
        if self.debug and kind == "Internal":
            kind = "ExternalOutput"
        return self.nc.dram_tensor(name, list(shape), dt, kind=kind).ap()

    def build(self):
        nc, fw = self.nc, self.fw
        NT, NTT, TS, TP, NP, nl = self.NT, self.NTT, self.TS, self.TP, self.NP, self.nl
        I = lambda n, s, dt=F32: self.dram(n, s, dt, "ExternalInput")
        O = lambda n, s, dt=F32: self.dram(n, s, dt, "ExternalOutput")
        d = self.d = {}
        d["x"] = I("x", [NT, D])
        d["cak"] = I("cak", [L, 256, 256]); d["cav"] = I("cav", [L, 256, 256])
        d["smC"] = I("smC", [L, 2, 4, 128, 128]); d["smn"] = I("smn", [L, 2, 4, 128]); d["smm"] = I("smm", [L, 8])
        d["srS"] = I("srS", [L, 2, 4, 128, 128]); d["sgS"] = I("sgS", [L, 2, 4, 64, 128])
        d["cond"] = I("cond", [2, D])
        d["w_ada"] = I("w_ada", [L, D, 6 * D]); d["b_ada"] = I("b_ada", [L, 6 * D])
        d["norm1_w"] = I("norm1_w", [L, D]); d["norm2_w"] = I("norm2_w", [L, D])
        d["w_in"] = I("w_in", [L, D, DIN]); d["attn_sink"] = I("attn_sink", [L, 4])
        d["mlstm_if_b"] = I("mlstm_if_b", [L, 16]); d["mlstm_norm_w"] = I("mlstm_norm_w", [L, 512])
        d["ret_decay"] = I("ret_decay", [L, 8]); d["ret_norm_w"] = I("ret_norm_w", [L, 512])
        d["gla_w2"] = I("gla_w2", [L, 2, 16, 256]); d["gla_b"] = I("gla_b", [L, 2, 256]); d["gla_norm_w"] = I("gla_norm_w", [L, 512])
        d["w_br"] = I("w_br", [L, 4, 512, D]); d["w_mgate"] = I("w_mgate", [L, 4, D, D]); d["w_out"] = I("w_out", [L, D, D])
        d["ffn_w_gu"] = I("ffn_w_gu", [L, D, 2 * FF]); d["ffn_w_down"] = I("ffn_w_down", [L, FF, D])
        d["final_norm_w"] = I("final_norm_w", [D])
        d["cst"] = I("cst", [128, NCST * 128]); d["ropeC"] = I("ropeC", [128, TS]); d["ropeS"] = I("ropeS", [128, TS])
        d["perm"] = I("perm", [128, 128])
        d["y"] = O("y", [NT, D])
        d["o_ak"] = O("o_ak", [NP, L, TP, 256]); d["o_av"] = O("o_av", [NP, L, TP, 256])
        d["o_mC"] = O("o_mC", [NP, L, 2, 4, 128, 128]); d["o_mn"] = O("o_mn", [NP, L, 2, 4, 128]); d["o_mm"] = O("o_mm", [NP, L, 8])
        d["o_rS"] = O("o_rS", [NP, L, 2, 4, 128, 128]); d["o_gS"] = O("o_gS", [NP, L, 2, 4, 64, 128])
        d["xres"] = self.dram("xres", [NT, D], F32)
        d["hT"] = self.dram("hT", [NTT, 128, KD, 512], BF16)
        d["zfm"] = self.dram("zfm", [NTT, 128, NF, 512], BF16)
        d["ztm"] = self.dram("ztm", [NT, NTM], BF16)
        d["mg"] = self.dram("mg", [NT, 16], F32)
        d["glr"] = self.dram("glr", [2, 16, NT], F32)
        d["ysT"] = self.dram("ysT", [NTT, 128, 16, 512], BF16)
        d["mT"] = self.dram("mT", [NTT, 128, KD, 512], BF16)
        d["aT"] = self.dram("aT", [NTT, 128, KF, 512], BF16)
        d["gb"] = self.dram("gb", [L, 2, 2, D], F32)
        if self.debug:
            d["modv_o"] = self.dram("modv_o", [128, L * 2 * 4 * KD], F32)
            d["dbg_acc"] = self.dram("dbg_acc", [128, (TS // 128) * 256], F32)
            d["dbg_fm"] = self.dram("dbg_fm", [128, 4 * TS], BF16)
            d["dbg_tm"] = self.dram("dbg_tm", [128, (TS // 128) * 512], BF16)
            d["dbg_g"] = self.dram("dbg_g", [128, 5 * (TS // 128) * 8], F32)
        self.t_x = [T(name="x%d" % i, partial=True) for i in range(NT // 128)]
        self.t_hT = [T(name="hT%d" % i, partial=True) for i in range(NTT)]
        self.t_z = [T(name="z%d" % i, partial=True) for i in range(NTT)]
        self.t_ys = [T(name="ys%d" % i, partial=True) for i in range(NTT)]
        self.t_mT = [T(name="mT%d" % i, partial=True) for i in range(NTT)]
        self.t_aT = [T(name="aT%d" % i, partial=True) for i in range(NTT)]
        self.t_gb = T(name="gb", partial=True)
        self.t_out = T(name="outs", partial=True)

        with contextlib.ExitStack() as gst:
            self.cst = self.sb(gst, "cst", [128, NCST * 128], F32)
            self.cstb = self.sb(gst, "cstb", [128, NCST * 128], BF16)
            self.modv = self.sb(gst, "modv", [128, L * 2 * 4 * KD], F32)
            self.epsc = self.sb(gst, "epsc", [128, 2], F32)
            self.memset(DVE, self.epsc[:, 0:1], EPS)
            self.memset(DVE, self.epsc[:, 1:2], 1.0)
            self.ld(self.cst[:], d["cst"][:, :])
            self.ld(self.cstb[:], d["cst"][:, :], eng=POOL)
            fw.barrier()
            self.phase_ada()
            for l in range(nl):
                self.phase_norm(l, 0)
                self.phase_win(l)
                self.phase_mix(l)
                self.phase_merge(l)
                self.phase_wout(l)
                self.phase_gu(l)
                self.phase_down(l)
            self.phase_final()
            fw.emit()
        return nc

    def C(self, blk, bf=False):
        t = self.cstb if bf else self.cst
        return t[:, blk * 128:(blk + 1) * 128]

    def mv(self, l, cond, which):
        o = ((l * 2 + cond) * 4 + which) * KD
        return o

    def cond_of_tile(self, tt):
        return 0 if tt * 512 < self.TS else 1

    def phase_ada(self):
        d, fw, nl = self.d, self.fw, self.nl
        with contextlib.ExitStack() as st:
            condT = self.sb(st, "condT", [128, 2, KD], F32)
            scT = self.sb(st, "scT", [128, KD, 2], BF16)
            rep = self.sb(st, "rep", [128, 2, KD, 128], BF16)
            ones = self.sb(st, "ones_a", [128, 128], BF16)
            wg = [self.sb(st, "wg%d" % i, [128, KD, 512], BF16) for i in range(2)]
            bfm = self.sb(st, "bfm", [128, L, 48], F32)
            nw = self.sb(st, "nw", [128, L, 2, KD], F32)
            brow = [self.sb(st, "brow%d" % i, [128, 512], F32) for i in range(2)]
            gst_ = [self.sb(st, "gst%d" % i, [128, 512], F32) for i in range(2)]
            psA = [self.ps(st, "psA%d" % i, [128, 512]) for i in range(2)]
            psB = [self.ps(st, "psB%d" % i, [128, 8]) for i in range(2)]
            bfm2 = self.sb(st, "bfm2", [128, L, 48], F32)
            for c in range(2):
                self.ld(condT[:, c, :], d["cond"][c].rearrange("(k p) -> p k", p=128))
            for l_ in range(L):
                bv = d["b_ada"][l_].rearrange("(j p) -> p j", p=128)
                self.ld(bfm[:, l_, :], bv[:, 0:48])
                self.ld(bfm2[:, l_, :], bv[:, 48:96])
                self.ld(nw[:, l_, 0, :], d["norm1_w"][l_].rearrange("(k p) -> p k", p=128))
                self.ld(nw[:, l_, 1, :], d["norm2_w"][l_].rearrange("(k p) -> p k", p=128))
            scF = self.sb(st, "scF", [128, KD, 2], F32)
            self.act(scF[:], condT.v(condT.ap[:].rearrange("p c k -> p k c")), AF.Silu)
            self.cp(DVE, scT[:], scF[:])
            self.memset(DVE, ones[:], 1.0)
            for c in range(2):
                for k in range(KD):
                    self.ts(DVE, rep[:, c, k, :], ones[:], scF[:, k, c:c + 1], ALU.mult)
            gi = 0
            for l in range(nl):
                wv = d["w_ada"][l].rearrange("(k p) n -> p k n", p=128)
                for g in range(24):
                    w = wg[gi % 2]
                    self.ld(w[:], wv[:, :, g * 512:(g + 1) * 512], eng=POOL)
                    seg = g // 4
                    if seg in (2, 5):
                        for c in range(2):
                            p = psA[c]
                            self.mm(p[:], [(rep[:, c, k, :], w[:, k, :]) for k in range(KD)])
                            br = brow[c]
                            self.ld(br[:], d["b_ada"][l, g * 512:(g + 1) * 512].partition_broadcast(128))
                            gs = gst_[c]
                            self.tt(DVE, gs[:], p[:], br[:], ALU.add)
                            col = (g % 4) * 512
                            self.st(d["gb"][l, 0 if seg == 2 else 1, c:c + 1, col:col + 512], gs[0:1, :], [self.t_gb])
                    else:
                        which = {0: 0, 1: 1, 3: 2, 4: 3}[seg]
                        for j in range(4):
                            kc = (g % 4) * 4 + j
                            p = psB[j % 2]
                            self.mm(p[:, 0:2], [(w[:, k, j * 128:(j + 1) * 128], scT[:, k, :]) for k in range(KD)])
                            bsrc = bfm if seg < 3 else bfm2
                            bj = (seg * 16 + kc) if seg < 3 else ((seg - 3) * 16 + kc)
                            for c in range(2):
                                o = self.mv(l, c, which) + kc
                                dst = self.modv[:, o:o + 1]
                                if which in (0, 2):
                                    self.tt(DVE, dst, p[:, c:c + 1], bsrc[:, l, bj:bj + 1], ALU.add)
                                else:
                                    self.stt(DVE, dst, p[:, c:c + 1], 1.0, bsrc[:, l, bj:bj + 1], ALU.add, ALU.add)
                                    self.tt(DVE, dst, dst, nw[:, l, 0 if which == 1 else 1, kc:kc + 1], ALU.mult)
                    gi += 1
            if self.debug:
                self.st(d["modv_o"][:, :], self.modv[:], [self.t_out])
        fw.barrier()

    def norm_block(self, st_, xb, cond, l, which, hst, tb, bufs, i):
        junk, ss, xn, pt = bufs
        self.act(junk[:], xb, AF.Square, accum=ss[:, 0:1])
        self.act(ss[:, 1:2], ss[:, 0:1], AF.Sqrt, bias=self.epsc[:, 0:1], scale=1.0 / D)
        self.recip(ss[:, 2:3], ss[:, 1:2])
        self.ts(DVE, xn[:], xb, ss[:, 2:3], ALU.mult)
        for q in range(4):
            p = pt[(i * 4 + q) % len(pt)]
            for j in range(4):
                kc = q * 4 + j
                self.tr(p[:, j, :], xn[:, kc * 128:(kc + 1) * 128], self.C(C_ID, True))
            for j in range(4):
                kc = q * 4 + j
                osc = self.mv(l, cond, 1 + 2 * which) + kc
                osh = self.mv(l, cond, 0 + 2 * which) + kc
                dst = hst[:, kc, tb * 128:(tb + 1) * 128]
                if j % 2 == 0:
                    self.act(dst, p[:, j, :], AF.Identity, bias=self.modv[:, osh:osh + 1], scale=self.modv[:, osc:osc + 1])
                else:
                    self.ts(DVE, dst, p[:, j, :], self.modv[:, osc:osc + 1], ALU.mult, self.modv[:, osh:osh + 1], ALU.add)

    def phase_norm(self, l, which):
        d, fw = self.d, self.fw
        xsrc = d["x"] if (l == 0 and which == 0) else d["xres"]
        with contextlib.ExitStack() as st:
            xb = [self.sb(st, "xb%d" % i, [128, D], F32) for i in range(3)]
            junk = self.sb(st, "junk", [128, D], BF16)
            ss = [self.sb(st, "ss%d" % i, [128, 4], F32) for i in range(2)]
            xn = [self.sb(st, "xn%d" % i, [128, D], BF16) for i in range(2)]
            pt = [self.ps(st, "pt%d" % i, [128, 4, 128], BF16) for i in range(4)]
            hst = [self.sb(st, "hst%d" % i, [128, KD, 512], BF16) for i in range(2)]
            nb = self.NT // 128
            self.ld(xb[0][:], xsrc[0:128, :], [self.t_x[0]])
            for i in range(nb):
                if i + 1 < nb:
                    self.ld(xb[(i + 1) % 3][:], xsrc[(i + 1) * 128:(i + 2) * 128, :], [self.t_x[i + 1]])
                tt, tb = i // 4, i % 4
                h = hst[tt % 2]
                self.norm_block(st, xb[i % 3][:], self.cond_of_tile(tt), l, which, h, tb, (junk, ss[i % 2], xn[i % 2], pt), i)
                if tb == 3:
                    self.st(d["hT"][tt], h[:], [self.t_hT[tt]])
        fw.barrier()

    def wload(self, wt, src3):
        self.ld(wt, src3, eng=POOL)

    def phase_win(self, l):
        d, fw, NTT = self.d, self.fw, self.NTT
        wv = d["w_in"][l].rearrange("(k p) n -> p k n", p=128)
        S = 128 ** -0.5
        S64 = 64 ** -0.5
        groups = [
            (0, 1024, [(O_AQ + i * 128, 128, i, 1.0, "rope") for i in range(4)] + [(O_AK + i * 128, 128, 4 + i, 1.0, "rope") for i in range(2)],
             [(O_AV, 256, T_AV, AF.Identity, 1.0)], True),
            (1024, 2048, [(O_MQ + i * 128, 128, 6 + i, 1.0, "") for i in range(4)] + [(O_MK + i * 128, 128, 10 + i, S, "") for i in range(4)],
             [(O_MK, 512, T_MK, AF.Identity, S)], False),
            (2048, 3072, [], [(O_MV, 512, T_MV, AF.Identity, 1.0), (O_MO, 512, T_MO, AF.Sigmoid, 1.0)], False),
            (3072, 4112, [(O_RQ + i * 128, 128, 14 + i, 1.0, "") for i in range(4)] + [(O_RK + i * 128, 128, 18 + i, S, "") for i in range(4)],
             [(O_MG, 16, -1, AF.Identity, 1.0), (O_RK, 512, T_RK, AF.Identity, S)], False),
            (4112, 5136, [], [(O_RV, 512, T_RV, AF.Identity, 1.0), (O_RG, 512, T_RG, AF.Silu, 1.0)], False),
            (5136, 6160, [(O_GQ + i * 64, 64, 22 + i, S64, "") for i in range(4)] + [(O_GK + i * 64, 64, 26 + i, 1.0, "") for i in range(4)],
             [(O_GK, 256, T_GK, AF.Identity, 1.0), (O_GV, 512, T_GV, AF.Identity, 1.0)], False),
            (6160, 6704, [(O_GLR, 16, -2, 1.0, "glr"), (O_GLR + 16, 16, -3, 1.0, "glr")],
             [(O_GG, 512, T_GG, AF.Silu, 1.0)], False),
        ]
        with contextlib.ExitStack() as st:
            wb = [self.sb(st, "wb%d" % i, [128, KD, 1040], BF16) for i in range(2)]
            hb = [self.sb(st, "hb%d" % i, [128, KD, 512], BF16) for i in range(2)]
            zst = [self.sb(st, "zst%d" % i, [128, 8, 512], BF16) for i in range(2)]
            tst = [self.sb(st, "tst%d" % i, [128, 4, 1024], BF16) for i in range(2)]
            mgst = [self.sb(st, "mgst%d" % i, [128, 4, 16], F32) for i in range(2)]
            glst = [self.sb(st, "glst%d" % i, [16, 2, 512], F32) for i in range(2)]
            kvst = [self.sb(st, "kvst%d" % i, [128, 512], F32) for i in range(2)]
            rc = [self.sb(st, "rc%d" % i, [128, 512], F32) for i in range(2)]
            rs = [self.sb(st, "rs%d" % i, [128, 512], F32) for i in range(2)]
            qf = [self.sb(st, "qf%d" % i, [128, 512], F32) for i in range(2)]
            t1 = [self.sb(st, "t1%d" % i, [128, 512], F32) for i in range(2)]
            permf = self.sb(st, "permf", [128, 128], F32)
            pp = [self.ps(st, "pp%d" % i, [128, 512]) for i in range(6)]
            pq = [self.ps(st, "pq%d" % i, [128, 512]) for i in range(2)]
            self.ld(permf[:], d["perm"][:, :])
            self.wload(wb[0][:, :, 0:1024], wv[:, :, 0:1024])
            pi = 0
            it = 0
            for gi, (c0, c1, fmj, tmj, isattn) in enumerate(groups):
                w = wb[gi % 2]
                if gi + 1 < len(groups):
                    n0, n1 = groups[gi + 1][0], groups[gi + 1][1]
                    self.wload(wb[(gi + 1) % 2][:, :, 0:n1 - n0], wv[:, :, n0:n1])
                self.ld(hb[it % 2][:], d["hT"][0], [self.t_hT[0]])
                for tt in range(NTT):
                    h = hb[it % 2]
                    if tt + 1 < NTT:
                        self.ld(hb[(it + 1) % 2][:], d["hT"][tt + 1], [self.t_hT[tt + 1]])
                    is_s = self.cond_of_tile(tt) == 0
                    zs = zst[it % 2]
                    ts_ = tst[it % 2]
                    if isattn and is_s:
                        self.ld(rc[it % 2][:], d["ropeC"][:, tt * 512:(tt + 1) * 512])
                        self.ld(rs[it % 2][:], d["ropeS"][:, tt * 512:(tt + 1) * 512])
                    for ji, (col, rows, fmc, scale, kind) in enumerate(fmj):
                        p = pp[pi % 6]; pi += 1
                        cc = col - c0
                        self.mm(p[0:rows, :], [(w[:, k, cc:cc + rows], h[:, k, :]) for k in range(KD)])
                        if kind == "glr":
                            self.cp(DVE, glst[it % 2][:, -2 - fmc, :], p[0:16, :])
                        elif kind == "rope" and is_s:
                            q = qf[ji % 2]
                            self.cp(ACT, q[:], p[:])
                            p2 = pq[ji % 2]
                            self.mm(p2[:], [(permf[:], q[:])])
                            tq = t1[ji % 2]
                            self.tt(DVE, tq[:], q[:], rc[it % 2][:], ALU.mult)
                            self.tt(DVE, q[:], p2[:], rs[it % 2][:], ALU.mult)
                            self.tt(DVE, zs[:, ji, :], tq[:], q[:], ALU.add)
                        else:
                            self.ts(DVE, zs[0:rows, ji, :], p[0:rows, :], scale, ALU.mult)
                    if fmj and fmj[0][4] != "glr":
                        f0 = fmj[0][2]
                        self.st(d["zfm"][tt][:, f0:f0 + len(fmj), :], zs[:, 0:len(fmj), :], [self.t_z[tt]])
                    if fmj and fmj[0][4] == "glr":
                        self.st(d["glr"][:, :, tt * 512:(tt + 1) * 512].rearrange("a r t -> r a t"), glst[it % 2][:], [self.t_z[tt]])
                    off = 0
                    for (col, ncols, tmc, func, scale) in tmj:
                        cc = col - c0
                        for tb in range(4):
                            p = pp[pi % 6]; pi += 1
                            self.mm(p[:, 0:ncols], [(h[:, k, tb * 128:(tb + 1) * 128], w[:, k, cc:cc + ncols]) for k in range(KD)])
                            if tmc == -1:
                                self.cp(ACT, mgst[it % 2][:, tb, :], p[:, 0:16])
                            else:
                                self.act(ts_[:, tb, off:off + ncols], p[:, 0:ncols], func, scale=scale)
                        if tmc == -1:
                            self.st(d["mg"][tt * 512:(tt + 1) * 512, :].rearrange("(b p) c -> p b c", p=128), mgst[it % 2][:], [self.t_z[tt]])
                        else:
                            self.st(d["ztm"][tt * 512:(tt + 1) * 512, tmc:tmc + ncols].rearrange("(b p) c -> p b c", p=128),
                                    ts_[:, :, off:off + ncols], [self.t_z[tt]])
                            off += ncols
                    if isattn and not is_s:
                        cc = O_AK - c0
                        for tb in range(4):
                            p = pp[pi % 6]; pi += 1
                            self.mm(p[:], [(h[:, k, tb * 128:(tb + 1) * 128], w[:, k, cc:cc + 512]) for k in range(KD)])
                            kv = kvst[tb % 2]
                            self.cp(ACT, kv[:], p[:])
                            pr, r0 = (tb * 128) // self.TP, (tb * 128) % self.TP
                            self.st(d["o_ak"][pr, l, r0:r0 + 128, :], kv[:, 0:256], [self.t_out])
                            self.st(d["o_av"][pr, l, r0:r0 + 128, :], kv[:, 256:512], [self.t_out])
                    it += 1
        fw.barrier()

    def phase_merge(self, l):
        d, fw, NTT = self.d, self.fw, self.NTT
        with contextlib.ExitStack() as st:
            wg = [self.sb(st, "wmg%d" % i, [128, 4, KD, 256], BF16) for i in range(2)]
            wr = [self.sb(st, "wbr%d" % i, [128, 4, 4, 256], BF16) for i in range(2)]
            hb = [self.sb(st, "hb%d" % i, [128, KD, 512], BF16) for i in range(2)]
            yb = [self.sb(st, "yb%d" % i, [128, 16, 512], BF16) for i in range(2)]
            gsb = [self.sb(st, "gsb%d" % i, [128, 512], F32) for i in range(3)]
            acc = [self.sb(st, "macc%d" % i, [128, 512], F32) for i in range(2)]
            mst = [self.sb(st, "mst%d" % i, [128, 2, 512], BF16) for i in range(2)]
            pg = [self.ps(st, "pg%d" % i, [128, 512]) for i in range(4)]
            pb = [self.ps(st, "pb%d" % i, [128, 512]) for i in range(4)]

            def wl(g):
                c0 = g * 256
                for b in range(4):
                    self.wload(wg[g % 2][:, b, :, :], d["w_mgate"][l, b].rearrange("(k p) n -> p k n", p=128)[:, :, c0:c0 + 256])
                self.wload(wr[g % 2][:], d["w_br"][l].rearrange("b (k p) n -> p b k n", p=128)[:, :, :, c0:c0 + 256])
            wl(0)
            it = 0
            gi = 0
            for g in range(8):
                if g + 1 < 8:
                    wl(g + 1)
                self.ld(hb[it % 2][:], d["hT"][0], [self.t_hT[0]])
                self.ld(yb[it % 2][:], d["ysT"][0], [self.t_ys[0]])
                for tt in range(NTT):
                    h, y = hb[it % 2], yb[it % 2]
                    if tt + 1 < NTT:
                        self.ld(hb[(it + 1) % 2][:], d["hT"][tt + 1], [self.t_hT[tt + 1]])
                        self.ld(yb[(it + 1) % 2][:], d["ysT"][tt + 1], [self.t_ys[tt + 1]])
                    ms = mst[it % 2]
                    for j in range(2):
                        a = acc[j]
                        for b in range(4):
                            p1 = pg[gi % 4]; p2 = pb[gi % 4]
                            self.mm(p1[:], [(wg[g % 2][:, b, k, j * 128:(j + 1) * 128], h[:, k, :]) for k in range(KD)])
                            self.mm(p2[:], [(wr[g % 2][:, b, k, j * 128:(j + 1) * 128], y[:, b * 4 + k, :]) for k in range(4)])
                            gs = gsb[gi % 3]
                            self.act(gs[:], p1[:], AF.Sigmoid)
                            if b == 0:
                                self.tt(DVE, a[:], gs[:], p2[:], ALU.mult)
                            elif b < 3:
                                self.tt(DVE, gs[:], gs[:], p2[:], ALU.mult)
                                self.tt(POOL, a[:], a[:], gs[:], ALU.add)
                            else:
                                self.tt(DVE, gs[:], gs[:], p2[:], ALU.mult)
                                self.tt(POOL, ms[:, j, :], a[:], gs[:], ALU.add)
                            gi += 1
                    self.st(d["mT"][tt][:, g * 2:g * 2 + 2, :], ms[:], [self.t_mT[tt]])
                    it += 1
        fw.barrier()

    def phase_resid(self, l, wsrc, nk, asrc, t_a, gsel, name):
        d, fw, NTT = self.d, self.fw, self.NTT
        xsrc = d["x"] if (l == 0 and gsel == 0) else d["xres"]
        wv = wsrc.rearrange("(k p) n -> p k n", p=128)
        with contextlib.ExitStack() as st:
            wb = [self.sb(st, "rw%d" % i, [128, nk, 512], BF16) for i in range(2)]
            ab = [self.sb(st, "ra%d" % i, [128, nk, 256], BF16) for i in range(2)]
            xb = [self.sb(st, "rx%d" % i, [128, 2, 512], F32) for i in range(3)]
            gbt = self.sb(st, "rg", [128, 2, D], F32)
            tmp = [self.sb(st, "rt%d" % i, [128, 512], F32) for i in range(2)]
            pp = [self.ps(st, "rp%d" % i, [128, 512]) for i in range(4)]
            for c in range(2):
                self.ld(gbt[:, c, :], d["gb"][l, gsel, c, :].partition_broadcast(128), [self.t_gb])
            self.wload(wb[0][:], wv[:, :, 0:512])
            nh = NTT * 2
            it = 0
            pi = 0
            for g in range(4):
                w = wb[g % 2]
                if g + 1 < 4:
                    self.wload(wb[(g + 1) % 2][:], wv[:, :, (g + 1) * 512:(g + 2) * 512])
                self.ld(ab[it % 2][:], asrc[0][:, :, 0:256], [t_a[0]])
                for hh in range(nh):
                    tt, half = hh // 2, hh % 2
                    a = ab[it % 2]
                    if hh + 1 < nh:
                        self.ld(ab[(it + 1) % 2][:], asrc[(hh + 1) // 2][:, :, ((hh + 1) % 2) * 256:((hh + 1) % 2) * 256 + 256], [t_a[(hh + 1) // 2]])
                    x = xb[it % 3]
                    r0 = hh * 256
                    xin = xsrc
                    self.ld(x[:], xin[r0:r0 + 256, g * 512:(g + 1) * 512].rearrange("(b p) c -> p b c", p=128), [self.t_x[hh * 2], self.t_x[hh * 2 + 1]])
                    cond = self.cond_of_tile(tt)
                    for tb in range(2):
                        p = pp[pi % 4]; pi += 1
                        self.mm(p[:], [(a[:, k, tb * 128:(tb + 1) * 128], w[:, k, :]) for k in range(nk)])
                        tm_ = tmp[pi % 2]
                        self.tt(DVE, tm_[:], p[:], gbt[:, cond, g * 512:(g + 1) * 512], ALU.mult)
                        self.tt(POOL, x[:, tb, :], x[:, tb, :], tm_[:], ALU.add)
                    self.st(d["xres"][r0:r0 + 256, g * 512:(g + 1) * 512].rearrange("(b p) c -> p b c", p=128), x[:], [self.t_x[hh * 2], self.t_x[hh * 2 + 1]])
                    it += 1
        fw.barrier()

    def phase_wout(self, l):
        d, fw, NTT = self.d, self.fw, self.NTT
        xsrc = d["x"] if l == 0 else d["xres"]
        wv = d["w_out"][l].rearrange("(k p) n -> p k n", p=128)
        with contextlib.ExitStack() as st:
            wb = self.sb(st, "ow", [128, KD, D], BF16)
            ab = [self.sb(st, "oa%d" % i, [128, KD, 512], BF16) for i in range(2)]
            xb = [self.sb(st, "ox%d" % i, [128, D], F32) for i in range(3)]
            gbt = self.sb(st, "og", [128, 2, D], F32)
            tmp = [self.sb(st, "ot%d" % i, [128, 512], F32) for i in range(2)]
            pp = [self.ps(st, "op%d" % i, [128, 512]) for i in range(4)]
            njunk = self.sb(st, "ojunk", [128, D], BF16)
            nss = [self.sb(st, "oss%d" % i, [128, 4], F32) for i in range(2)]
            nxn = [self.sb(st, "oxn%d" % i, [128, D], BF16) for i in range(2)]
            npt = [self.ps(st, "opt%d" % i, [128, 4, 128], BF16) for i in range(4)]
            nhst = [self.sb(st, "ohst%d" % i, [128, KD, 512], BF16) for i in range(2)]
            for c in range(2):
                self.ld(gbt[:, c, :], d["gb"][l, 0, c, :].partition_broadcast(128), [self.t_gb])
            for g in range(4):
                self.wload(wb[:, :, g * 512:(g + 1) * 512], wv[:, :, g * 512:(g + 1) * 512])
            nb = self.NT // 128
            self.ld(ab[0][:], d["mT"][0], [self.t_mT[0]])
            self.ld(xb[0][:], xsrc[0:128, :], [self.t_x[0]])
            pi = 0
            pend = None

            def do_norm(x_, cond_, tt_, tb_, i_):
                hh_ = nhst[tt_ % 2]
                self.norm_block(st, x_[:], cond_, l, 1, hh_, tb_, (njunk, nss[i_ % 2], nxn[i_ % 2], npt), i_)
                if tb_ == 3:
                    self.st(d["hT"][tt_], hh_[:], [self.t_hT[tt_]])

            for i in range(nb):
                tt, tb = i // 4, i % 4
                if tb == 0 and tt + 1 < NTT:
                    self.ld(ab[(tt + 1) % 2][:], d["mT"][tt + 1], [self.t_mT[tt + 1]])
                if i + 1 < nb:
                    self.ld(xb[(i + 1) % 3][:], xsrc[(i + 1) * 128:(i + 2) * 128, :], [self.t_x[i + 1]])
                a, x = ab[tt % 2], xb[i % 3]
                cond = self.cond_of_tile(tt)
                for g in range(4):
                    p = pp[pi % 4]; pi += 1
                    self.mm(p[:], [(a[:, k, tb * 128:(tb + 1) * 128], wb[:, k, g * 512:(g + 1) * 512]) for k in range(KD)])
                    tm_ = tmp[pi % 2]
                    self.tt(DVE, tm_[:], p[:], gbt[:, cond, g * 512:(g + 1) * 512], ALU.mult)
                    self.tt(POOL, x[:, g * 512:(g + 1) * 512], x[:, g * 512:(g + 1) * 512], tm_[:], ALU.add)
                self.st(d["xres"][i * 128:(i + 1) * 128, :], x[:], [self.t_x[i]])
                if pend is not None:
                    do_norm(*pend)
                pend = (x, cond, tt, tb, i)
            do_norm(*pend)
        fw.barrier()

    def phase_down(self, l):
        self.phase_resid(l, self.d["ffn_w_down"][l], KF, self.d["aT"], self.t_aT, 1, "down")

    def phase_gu(self, l):
        d, fw, NTT = self.d, self.fw, self.NTT
        wv = d["ffn_w_gu"][l].rearrange("(k p) n -> p k n", p=128)
        with contextlib.ExitStack() as st:
            wb = [self.sb(st, "gw%d" % i, [128, KD, 1024], BF16) for i in range(2)]
            hb = [self.sb(st, "hb%d" % i, [128, KD, 512], BF16) for i in range(2)]
            sg = [self.sb(st, "sg%d" % i, [128, 512], F32) for i in range(3)]
            ast = [self.sb(st, "ast%d" % i, [128, 4, 512], BF16) for i in range(2)]
            pg = [self.ps(st, "pg%d" % i, [128, 512]) for i in range(4)]
            pu = [self.ps(st, "pu%d" % i, [128, 512]) for i in range(4)]

            def wl(g):
                self.wload(wb[g % 2][:, :, 0:512], wv[:, :, g * 512:(g + 1) * 512])
                self.wload(wb[g % 2][:, :, 512:1024], wv[:, :, FF + g * 512:FF + (g + 1) * 512])
            wl(0)
            it = 0
            pi = 0
            for g in range(11):
                w = wb[g % 2]
                if g + 1 < 11:
                    wl(g + 1)
                self.ld(hb[it % 2][:], d["hT"][0], [self.t_hT[0]])
                for tt in range(NTT):
                    h = hb[it % 2]
                    if tt + 1 < NTT:
                        self.ld(hb[(it + 1) % 2][:], d["hT"][tt + 1], [self.t_hT[tt + 1]])
                    a = ast[it % 2]
                    for j in range(4):
                        p1 = pg[pi % 4]; p2 = pu[pi % 4]; s = sg[pi % 3]; pi += 1
                        self.mm(p1[:], [(w[:, k, j * 128:(j + 1) * 128], h[:, k, :]) for k in range(KD)])
                        self.mm(p2[:], [(w[:, k, 512 + j * 128:512 + (j + 1) * 128], h[:, k, :]) for k in range(KD)])
                        self.act(s[:], p1[:], AF.Silu)
                        self.tt(DVE, a[:, j, :], s[:], p2[:], ALU.mult)
                    self.st(d["aT"][tt][:, g * 4:g * 4 + 4, :], a[:], [self.t_aT[tt]])
                    it += 1
        fw.barrier()

    def phase_final(self):
        d, fw = self.d, self.fw
        with contextlib.ExitStack() as st:
            xb = [self.sb(st, "fx%d" % i, [128, D], F32) for i in range(3)]
            junk = self.sb(st, "fjunk", [128, D], BF16)
            ss = [self.sb(st, "fss%d" % i, [128, 4], F32) for i in range(2)]
            fw_ = self.sb(st, "fnw", [128, D], F32)
            self.ld(fw_[:], d["final_norm_w"].partition_broadcast(128))
            nb = self.NT // 128
            self.ld(xb[0][:], d["xres"][0:128, :], [self.t_x[0]])
            for i in range(nb):
                if i + 1 < nb:
                    self.ld(xb[(i + 1) % 3][:], d["xres"][(i + 1) * 128:(i + 2) * 128, :], [self.t_x[i + 1]])
                x, s = xb[i % 3], ss[i % 2]
                self.act(junk[:], x[:], AF.Square, accum=s[:, 0:1])
                self.act(s[:, 1:2], s[:, 0:1], AF.Sqrt, bias=self.epsc[:, 0:1], scale=1.0 / D)
                self.recip(s[:, 2:3], s[:, 1:2])
                self.stt(DVE, x[:], x[:], s[:, 2:3], fw_[:], ALU.mult, ALU.mult)
                self.st(d["y"][i * 128:(i + 1) * 128, :], x[:], [self.t_out])
        fw.barrier()

    def phase_mix(self, l):
        _mix_impl(self, l)


def _mix_impl(self, l):
    fw = self.fw
    for (t0, Tn, pidx) in self.seqs:
        _attn(self, l, t0, Tn, pidx)
        fw.barrier()
        for h0 in (0, 2):
            (_mlstm if "m" in STREAMS else _mlstm_old)(self, l, t0, Tn, pidx, h0)
            fw.barrier()
            (_ret if "r" in STREAMS else _ret_old)(self, l, t0, Tn, pidx, h0)
            fw.barrier()
            (_gla if "g" in STREAMS else _gla_old)(self, l, t0, Tn, pidx, h0)
            fw.barrier()


def _tile_of(self, t0, blk):
    g = t0 // 128 + blk
    return g // 4, g % 4


def _ztiles(self, t0, Tn):
    return [self.t_z[tt] for tt in range(t0 // 512, (t0 + Tn - 1) // 512 + 1)]


def _to_ysT(self, y, br, h0, nh, tt, tb, t0, Tn, b, yst, ptr):
    d = self.d
    p = ptr[b % 2]
    for h in range(nh):
        self.tr(p[:, h, :], y[:, h * 128:(h + 1) * 128], self.C(C_ID, True))
    ys = yst[tt % 2]
    self.cp(ACT, ys[:, :, tb * 128:(tb + 1) * 128], p[:, 0:nh, :])
    last = (b == Tn // 128 - 1)
    if tb == 3 or last:
        tb0 = (t0 // 128) % 4 if (tt == (t0 // 128) // 4) else 0
        c0 = br * 4 + h0
        self.st(d["ysT"][tt][:, c0:c0 + nh, tb0 * 128:(tb + 1) * 128], ys[:, :, tb0 * 128:(tb + 1) * 128], [self.t_ys[tt]])


def _finish_branch(self, st, br, h0, l, t0, Tn, acc, gate_col, nw_name, nm, accB=None):
    d = self.d
    nblk = Tn // 128
    nwb = self.sb(st, nm + "nwb", [128, 256], F32)
    self.ld(nwb[:], d[nw_name][l, h0 * 128:h0 * 128 + 256].partition_broadcast(128))
    gt = [self.sb(st, nm + "gt%d" % i, [128, 256], BF16) for i in range(2)]
    ssq = [self.sb(st, nm + "ssq%d" % i, [128, 8], F32) for i in range(2)]
    junk = self.sb(st, nm + "fj", [128, 128], BF16)
    yb = [self.sb(st, nm + "yb%d" % i, [128, 256], BF16) for i in range(2)]
    yst = [self.sb(st, nm + "yst%d" % i, [128, 2, 512], BF16) for i in range(2)]
    bank = self.psbank(st, nm + "ptr", BF16)
    ptr = [self.sub(bank, i * 256, (i + 1) * 256, 2) for i in range(2)]
    def stage_a(b):
        tt, tb = _tile_of(self, t0, b)
        g = gt[b % 2]
        r0 = t0 + b * 128
        gc = gate_col + h0 * 128
        self.ld(g[:], d["ztm"][r0:r0 + 128, gc:gc + 256], [self.t_z[tt]])
        s = ssq[b % 2]
        for h in range(2):
            self.act(junk[:], acc[:, b, h * 128:(h + 1) * 128], AF.Square, accum=s[:, h:h + 1])
        self.act(s[:, 2:4], s[:, 0:2], AF.Sqrt, bias=self.epsc[:, 0:1], scale=1.0 / 128)
        self.recip(s[:, 4:6], s[:, 2:4])
        for h in range(2):
            self.ts(DVE, acc[:, b, h * 128:(h + 1) * 128], acc[:, b, h * 128:(h + 1) * 128], s[:, 4 + h:5 + h], ALU.mult)
        self.tt(POOL, acc[:, b, :], acc[:, b, :], nwb[:], ALU.mult)
        self.tt(DVE, yb[b % 2][:], acc[:, b, :], g[:], ALU.mult)

    stage_a(0)
    for b in range(nblk):
        if b + 1 < nblk:
            stage_a(b + 1)
        tt, tb = _tile_of(self, t0, b)
        _to_ysT(self, yb[b % 2], br, h0, 2, tt, tb, t0, Tn, b, yst, ptr)


def _scan_loads(self, st, nm, t0, Tn, fmchunks, tmcols, rows=128):
    d = self.d
    nblk = Tn // 128
    ntm = sum(n for _, n in tmcols)
    fm = self.sb(st, nm + "_fm", [128, sum(n for _, n in fmchunks), Tn], BF16)
    tm = self.sb(st, nm + "_tm", [128, nblk, ntm], BF16)
    for b4 in range(0, Tn, 512):
        n = min(512, Tn - b4)
        tt = (t0 + b4) // 512
        o = (t0 + b4) % 512
        fo = 0
        for (c0, cn) in fmchunks:
            self.ld(fm[0:rows, fo:fo + cn, b4:b4 + n], d["zfm"][tt][0:rows, c0:c0 + cn, o:o + n], [self.t_z[tt]])
            fo += cn
        nb4 = n // 128
        off = 0
        for (c0, cn) in tmcols:
            self.ld(tm[:, b4 // 128:b4 // 128 + nb4, off:off + cn],
                    d["ztm"][t0 + b4:t0 + b4 + n, c0:c0 + cn].rearrange("(b p) c -> p b c", p=128), [self.t_z[tt]])
            off += cn
    return fm, tm


def _attn(self, l, t0, Tn, pidx):
    d = self.d
    is_s = pidx < 0
    nblk = Tn // 128
    SC = 128 ** -0.5
    with contextlib.ExitStack() as st:
        qk = self.sb(st, "a_qk", [128, 6, Tn], BF16)
        v = self.sb(st, "a_v", [128, nblk, 256], BF16)
        snk = self.sb(st, "a_snk", [128, 8], F32)
        kc = self.sb(st, "a_kc", [128, 2, 256], BF16)
        vc = self.sb(st, "a_vc", [128, 2, 256], BF16)
        ktm = self.sb(st, "a_ktm", [128, 2, 256], BF16)
        ssb = [self.sb(st, "a_s%d" % i, [128, 384], F32) for i in range(2)]
        pb = [self.sb(st, "a_p%d" % i, [128, 640], BF16) for i in range(2)]
        pT = [self.sb(st, "a_pT%d" % i, [128, 5, 128], BF16) for i in range(2)]
        sm = [self.sb(st, "a_sm%d" % i, [128, 8], F32) for i in range(4)]
        y = [self.sb(st, "a_y%d" % i, [128, 512], BF16) for i in range(2)]
        yst = [self.sb(st, "a_yst%d" % i, [128, 4, 512], BF16) for i in range(2)]
        ps_s = [T(self.psbank(st, "a_pss%d" % i)) for i in range(2)]
        bk = self.psbank(st, "a_psc")
        ps_c = [self.sub(bk, i * 256, (i + 1) * 256) for i in range(2)]
        ps_t = self.sub(self.psbank(st, "a_pst", BF16), 0, 640, 5)
        bk = self.psbank(st, "a_pso")
        ps_o = [self.sub(bk, i * 128, (i + 1) * 128) for i in range(2)]
        bk = self.psbank(st, "a_ptr", BF16)
        ptr = [self.sub(bk, i * 512, (i + 1) * 512, 4) for i in range(2)]
        for b4 in range(0, Tn, 512):
            n = min(512, Tn - b4)
            tt = (t0 + b4) // 512
            o = (t0 + b4) % 512
            self.ld(qk[:, :, b4:b4 + n], d["zfm"][tt][:, 0:6, o:o + n], [self.t_z[tt]])
        self.ld(v[:], d["ztm"][t0:t0 + Tn, T_AV:T_AV + 256].rearrange("(b p) c -> p b c", p=128), _ztiles(self, t0, Tn))
        self.ld(snk[:, 0:4], d["attn_sink"][l, :].partition_broadcast(128))
        self.ts(DVE, snk[:, 4:8], snk[:, 0:4], -1.0, ALU.mult)
        if is_s:
            self.ld(ktm[:], d["cak"][l].rearrange("(b p) c -> p b c", p=128), eng=POOL)
            self.ld(vc[:], d["cav"][l].rearrange("(b p) c -> p b c", p=128), eng=POOL)
            for kb in range(2):
                for kv in range(2):
                    self.tr(ps_t[:, kv, :], ktm[:, kb, kv * 128:(kv + 1) * 128], self.C(C_ID, True))
                self.cp(DVE, kc[:, :, kb * 128:(kb + 1) * 128], ps_t[:, 0:2, :])
        it = 0
        for b in range(nblk):
            tt, tb = _tile_of(self, t0, b)
            yb = y[b % 2]
            for h in range(4):
                kv = h // 2
                q = qk[:, h, b * 128:(b + 1) * 128]
                smt = sm[it % 4]
                p = pb[it % 2]
                pss = ps_s[it % 2]
                if is_s:
                    k0 = max(0, b - 1)
                    k1 = min(nblk, b + 2)
                    nk = (k1 - k0) * 128
                    self.mm(pss[:, 0:nk], [(q, qk[:, 4 + kv, k0 * 128:k1 * 128])])
                    psc = ps_c[it % 2]
                    self.mm(psc[:], [(q, kc[:, kv, :])])
                    s = ssb[it % 2]
                    off = 0
                    if b > 0:
                        self.tt(DVE, s[:, 0:128], pss[:, 0:128], self.C(C_MP), ALU.add)
                        off = 128
                    self.cp(DVE, s[:, off:off + 128], pss[:, off:off + 128])
                    if b + 1 < nblk:
                        self.tt(DVE, s[:, off + 128:off + 256], pss[:, off + 128:off + 256], self.C(C_MN), ALU.add)
                    self.red(smt[:, 0:1], s[:, 0:nk], ALU.max)
                    self.red(smt[:, 1:2], psc[:], ALU.max)
                    self.tt(DVE, smt[:, 0:1], smt[:, 0:1], smt[:, 1:2], ALU.max)
                    self.ts(DVE, smt[:, 2:3], smt[:, 0:1], -SC, ALU.mult, snk[:, 4 + h:5 + h], ALU.min)
                    self.act(p[:, 0:nk], s[:, 0:nk], AF.Exp, bias=smt[:, 2:3], scale=SC, accum=smt[:, 3:4])
                    self.act(p[:, nk:nk + 256], psc[:], AF.Exp, bias=smt[:, 2:3], scale=SC, accum=smt[:, 4:5])
                    self.act(smt[:, 5:6], snk[:, h:h + 1], AF.Exp, bias=smt[:, 2:3])
                    self.tt(DVE, smt[:, 3:4], smt[:, 3:4], smt[:, 4:5], ALU.add)
                    self.tt(DVE, smt[:, 3:4], smt[:, 3:4], smt[:, 5:6], ALU.add)
                    ntot = nk + 256
                    vlist = [v[:, kb, kv * 128:(kv + 1) * 128] for kb in range(k0, k1)] + [vc[:, kb, kv * 128:(kv + 1) * 128] for kb in range(2)]
                else:
                    self.mm(pss[:, 0:Tn], [(q, qk[:, 4 + kv, :])])
                    self.red(smt[:, 0:1], pss[:, 0:Tn], ALU.max)
                    self.ts(DVE, smt[:, 2:3], smt[:, 0:1], -SC, ALU.mult, snk[:, 4 + h:5 + h], ALU.min)
                    self.act(p[:, 0:Tn], pss[:, 0:Tn], AF.Exp, bias=smt[:, 2:3], scale=SC, accum=smt[:, 3:4])
                    self.act(smt[:, 5:6], snk[:, h:h + 1], AF.Exp, bias=smt[:, 2:3])
                    self.tt(DVE, smt[:, 3:4], smt[:, 3:4], smt[:, 5:6], ALU.add)
                    ntot = Tn
                    vlist = [v[:, kb, kv * 128:(kv + 1) * 128] for kb in range(nblk)]
                self.recip(smt[:, 6:7], smt[:, 3:4])
                nkb = ntot // 128
                for kb in range(nkb):
                    self.tr(ps_t[:, kb, :], p[:, kb * 128:(kb + 1) * 128], self.C(C_ID, True))
                ptt = pT[it % 2]
                self.cp(ACT, ptt[:, 0:nkb, :], ps_t[:, 0:nkb, :])
                pso = ps_o[it % 2]
                self.mm(pso[:], [(ptt[:, kb, :], vlist[kb]) for kb in range(nkb)])
                self.ts(DVE, yb[:, h * 128:(h + 1) * 128], pso[:], smt[:, 6:7], ALU.mult)
                it += 1
            _to_ysT(self, yb, 0, 0, 4, tt, tb, t0, Tn, b, yst, ptr)


def _run_streams(gens):
    gens = list(gens)
    if os.environ.get("KSEQ"):
        for g in gens:
            for _ in g:
                pass
        return
    while gens:
        for g in list(gens):
            try:
                next(g)
            except StopIteration:
                gens.remove(g)


def _mlstm(self, l, t0, Tn, pidx, h0):
    d = self.d
    is_s = pidx < 0
    nblk = Tn // 128
    with contextlib.ExitStack() as st:
        acc = self.sb(st, "m_acc", [128, nblk, 256], F32)
        accB = self.sb(st, "m_accB", [128, nblk, 256], F32)
        with contextlib.ExitStack() as st2:
            fm, tm = _scan_loads(self, st2, "m", t0, Tn, [(6 + h0, 2), (10 + h0, 2)], [(T_MK + h0 * 128, 256), (T_MV + h0 * 128, 256)])
            G = self.sb(st2, "m_G", [128, nblk, 16], F32)
            gb = self.sb(st2, "m_gb", [128, 16], F32)
            lf = self.sb(st2, "m_lf", [128, nblk, 8], F32)
            ig = self.sb(st2, "m_ig", [128, nblk, 8], F32)
            bb = self.sb(st2, "m_b", [128, nblk, 8], F32)
            bl = self.sb(st2, "m_bl", [128, nblk, 8], F32)
            ew = self.sb(st2, "m_ew", [128, nblk, 8], F32)
            wn = self.sb(st2, "m_wn", [128, nblk, 8], F32)
            enb = self.sb(st2, "m_enb", [128, nblk, 8], F32)
            ebl = self.sb(st2, "m_ebl", [128, nblk, 8], F32)
            tmpg = self.sb(st2, "m_tmpg", [128, nblk, 8], F32)
            vaug = self.sb(st2, "m_vaug", [128, nblk, 2, 129], BF16)
            em0 = self.sb(st2, "m_em0", [128, 16], F32)
            bk0 = self.psbank(st2, "m_ps0")
            ps_g = self.sub(bk0, 0, 16)
            self.ld(G[:], d["mg"][t0:t0 + Tn, :].rearrange("(b p) c -> p b c", p=128), _ztiles(self, t0, Tn))
            self.ld(gb[:], d["mlstm_if_b"][l, :].partition_broadcast(128))
            for b in range(nblk):
                self.tt(DVE, G[:, b, :], G[:, b, :], gb[:], ALU.add)
                for dd in range(2):
                    self.cp(DVE, ig[:, b, dd * 4:dd * 4 + 4], G[:, b, dd * 8:dd * 8 + 4])
                    self.act(lf[:, b, dd * 4:dd * 4 + 4], G[:, b, dd * 8 + 4:dd * 8 + 8], AF.Exp, scale=-1.0)
            self.act(lf[:], lf[:], AF.Ln, bias=self.epsc[:, 1:2])
            self.ts(DVE, lf[:], lf[:], -1.0, ALU.mult)
            for b in range(nblk):
                self.mm(ps_g[:, 0:4], [(self.C(C_LE), lf[:, b, 0:4])])
                self.mm(ps_g[:, 4:8], [(self.C(C_GE), lf[:, b, 4:8])])
                self.mm(ps_g[:, 8:16], [(self.C(C_ONE), lf[:, b, :])])
                self.cp(DVE, bb[:, b, :], ps_g[:, 0:8])
                self.cp(DVE, bl[:, b, :], ps_g[:, 8:16])
            self.tt(DVE, tmpg[:], ig[:], bb[:], ALU.subtract)
            self.act(ew[:], tmpg[:], AF.Exp)
            self.tt(DVE, tmpg[:], tmpg[:], bl[:], ALU.add)
            self.act(wn[:], tmpg[:], AF.Exp)
            self.act(enb[:], bb[:], AF.Exp, scale=-1.0)
            self.act(ebl[:], bl[:], AF.Exp)
            self.memset(POOL, vaug[:], 1.0)
            for b in range(nblk):
                self.cp(POOL, vaug[:, b, :, 0:128], tm.v(tm.ap[:, b, 256:512].rearrange("p (h e) -> p h e", h=2)))
            if not is_s:
                assert nblk == 2
                mcol = self.sb(st2, "m_mcol", [8, 4], F32)
                gT = self.sb(st2, "m_gT", [8, 2], F32)
                blc = self.sb(st2, "m_blc", [8, 2], F32)
                mF = self.sb(st2, "m_mF", [8, 2], F32)
                mrep = self.sb(st2, "m_mrep", [8, 128], F32)
                ps_t = T(bk0[0:8, 16:144])
                ps_c = T(bk0[0:8, 144:146])
                ps_b = self.sub(bk0, 160, 168)
                for b in range(nblk):
                    self.tr(ps_t[:], tmpg[:, b, :], self.C(C_ID))
                    self.red(gT[:, b:b + 1], ps_t[:], ALU.max)
                    self.mm(ps_c[:, b:b + 1], [(lf[:, b, :], self.C(C_ONE)[:, 0:1])])
                self.cp(DVE, blc[:], ps_c[:])
                for (col, b0, b1) in ((0, 0, 1), (1, 1, 0)):
                    self.tt(DVE, mcol[:, 0:1], blc[:, b0:b0 + 1], gT[:, b0:b0 + 1], ALU.max)
                    self.tt(DVE, mcol[:, 1:2], mcol[:, 0:1], blc[:, b1:b1 + 1], ALU.add)
                    self.tt(DVE, mF[:, col:col + 1], mcol[:, 1:2], gT[:, b1:b1 + 1], ALU.max)
                self.tt(DVE, mF[:, 0:1], mF[:, 0:1], self.C(C_Z)[0:8, 2:3], ALU.mult)
                self.tt(DVE, mF[:, 1:2], mF[:, 1:2], self.C(C_Z)[0:8, 3:4], ALU.mult)
                self.tt(DVE, mcol[:, 2:3], mF[:, 0:1], mF[:, 1:2], ALU.add)
                if h0 == 0:
                    self.st(d["o_mm"][pidx, l, :].rearrange("(p o) -> p o", o=1), mcol[:, 2:3], [self.t_out])
                self.ts(DVE, mrep[:], self.C(C_ONE)[0:8, :], mcol[:, 2:3], ALU.mult)
                self.mm(ps_b[:], [(mrep[:], self.C(C_ID)[0:8, 0:8])])
                self.act(em0[:, 8:16], ps_b[:], AF.Exp, scale=-1.0)
            else:
                self.ld(em0[:, 0:8], d["smm"][l, :].partition_broadcast(128))
                self.act(em0[:, 0:8], em0[:, 0:8], AF.Exp)

            def stream(dd, h):
                j = dd * 4 + h0 + h
                S = self.sb(st2, "m_S%d" % j, [128, 129], F32)
                Sb = [self.sb(st2, "m_Sb%d_%d" % (j, i), [128, 129], BF16) for i in range(2)]
                P = [self.sb(st2, "m_P%d_%d" % (j, i), [128, 128], BF16) for i in range(2)]
                kw = [self.sb(st2, "m_kw%d_%d" % (j, i), [128, 128], BF16) for i in range(2)]
                dn = [self.sb(st2, "m_dn%d_%d" % (j, i), [128, 4], F32) for i in range(2)]
                bk = self.psbank(st2, "m_psS%d" % j)
                pa, po, pS = self.sub(bk, 0, 128), self.sub(bk, 128, 257), self.sub(bk, 320, 449)
                A = acc if dd == 0 else accB
                if is_s:
                    self.ld(S[:, 0:128], d["smC"][l, dd, h0 + h])
                    self.ld(S[:, 128:129], d["smn"][l, dd, h0 + h].rearrange("(k o) -> k o", o=1))
                    self.ts(DVE, S[:], S[:], em0[:, j:j + 1], ALU.mult)
                else:
                    self.memset(DVE, S[:], 0.0)
                self.cp(ACT, Sb[0][:], S[:])
                yield
                order = list(range(nblk)) if dd == 0 else list(range(nblk - 1, -1, -1))
                mask = self.C(C_LE) if dd == 0 else self.C(C_GE)
                for bi, b in enumerate(order):
                    sl = slice(b * 128, (b + 1) * 128)
                    Pm, kwt, dnt = P[bi % 2], kw[bi % 2], dn[bi % 2]
                    sb_cur, sb_nxt = Sb[bi % 2], Sb[(bi + 1) % 2]
                    self.mm(pa[:], [(fm[:, 2 + h, sl], fm[:, h, sl])])
                    self.ts(POOL, kwt[:], tm[:, b, h * 128:(h + 1) * 128], wn[:, b, j:j + 1], ALU.mult)
                    yield
                    self.stt(DVE, Pm[:], pa[:], ew[:, b, j:j + 1], mask, ALU.mult, ALU.mult)
                    self.mm(pS[:], [(kwt[:], vaug[:, b, h, :])])
                    yield
                    self.mm(po[:], [(Pm[:], vaug[:, b, h, :]), (fm[:, h, sl], sb_cur[:])])
                    self.stt(DVE, S[:], S[:], ebl[:, b, j:j + 1], pS[:], ALU.mult, ALU.add)
                    yield
                    self.act(dnt[:, 0:1], po[:, 128:129], AF.Abs)
                    self.cp(ACT, sb_nxt[:], S[:])
                    yield
                    self.tt(DVE, dnt[:, 0:1], dnt[:, 0:1], enb[:, b, j:j + 1], ALU.max)
                    self.recip(dnt[:, 1:2], dnt[:, 0:1])
                    yield
                    self.act(A[:, b, h * 128:(h + 1) * 128], po[:, 0:128], AF.Identity, scale=dnt[:, 1:2])
                    yield
                if not is_s:
                    self.ts(DVE, S[:], S[:], em0[:, 8 + j:9 + j], ALU.mult)
                    self.st(d["o_mC"][pidx, l, dd, h0 + h], S[:, 0:128], [self.t_out])
                    self.st(d["o_mn"][pidx, l, dd, h0 + h].rearrange("(k o) -> k o", o=1), S[:, 128:129], [self.t_out])
            _run_streams([stream(dd, h) for dd in range(2) for h in range(2)])
        self.fw.barrier()
        with contextlib.ExitStack() as st3:
            _finish_branch(self, st3, 1, h0, l, t0, Tn, acc, T_MO, "mlstm_norm_w", "mB", accB)


def _ret(self, l, t0, Tn, pidx, h0):
    d = self.d
    is_s = pidx < 0
    nblk = Tn // 128
    with contextlib.ExitStack() as st:
        acc = self.sb(st, "r_acc", [128, nblk, 256], F32)
        accB = self.sb(st, "r_accB", [128, nblk, 256], F32)
        with contextlib.ExitStack() as st2:
            fm, tm = _scan_loads(self, st2, "r", t0, Tn, [(14 + h0, 2), (18 + h0, 2)], [(T_RK + h0 * 128, 256), (T_RV + h0 * 128, 256)])
            lg = self.sb(st2, "r_lg", [128, 8], F32)
            gc = self.sb(st2, "r_gc", [128, 8], F32)
            self.ld(lg[:], d["ret_decay"][l, :].partition_broadcast(128))
            self.act(lg[:], lg[:], AF.Exp, scale=-1.0)
            self.act(lg[:], lg[:], AF.Ln, bias=self.epsc[:, 1:2])
            self.ts(DVE, lg[:], lg[:], -1.0, ALU.mult)
            self.act(gc[:], lg[:], AF.Exp, scale=128.0)

            def stream(dd, h):
                j = dd * 4 + h0 + h
                M = self.sb(st2, "r_M%d" % j, [128, 128], F32)
                xi = self.sb(st2, "r_xi%d" % j, [128, 128], F32)
                ze = self.sb(st2, "r_ze%d" % j, [128, 1], F32)
                S = self.sb(st2, "r_S%d" % j, [128, 128], F32)
                Sb = [self.sb(st2, "r_Sb%d_%d" % (j, i), [128, 128], BF16) for i in range(2)]
                P = [self.sb(st2, "r_P%d_%d" % (j, i), [128, 128], BF16) for i in range(2)]
                qx = [self.sb(st2, "r_qx%d_%d" % (j, i), [128, 128], BF16) for i in range(2)]
                kz = [self.sb(st2, "r_kz%d_%d" % (j, i), [128, 128], BF16) for i in range(2)]
                bk = self.psbank(st2, "r_psS%d" % j)
                pa, po, pS = self.sub(bk, 0, 128), self.sub(bk, 128, 256), self.sub(bk, 256, 384)
                A = acc if dd == 0 else accB
                self.act(M[:], self.C(C_DF if dd == 0 else C_DB), AF.Exp, scale=lg[:, j:j + 1])
                self.act(xi[:], self.C(C_PF if dd == 0 else C_PB), AF.Exp, scale=lg[:, j:j + 1])
                self.act(ze[:], self.C(C_Z)[:, dd:dd + 1], AF.Exp, scale=lg[:, j:j + 1])
                if is_s:
                    self.ld(S[:], d["srS"][l, dd, h0 + h])
                else:
                    self.memset(DVE, S[:], 0.0)
                self.cp(ACT, Sb[0][:], S[:])
                yield
                order = list(range(nblk)) if dd == 0 else list(range(nblk - 1, -1, -1))
                for bi, b in enumerate(order):
                    sl = slice(b * 128, (b + 1) * 128)
                    Pm, q_, kzt = P[bi % 2], qx[bi % 2], kz[bi % 2]
                    sb_cur, sb_nxt = Sb[bi % 2], Sb[(bi + 1) % 2]
                    self.mm(pa[:], [(fm[:, 2 + h, sl], fm[:, h, sl])])
                    self.ts(POOL, kzt[:], tm[:, b, h * 128:(h + 1) * 128], ze[:, 0:1], ALU.mult)
                    self.tt(POOL, q_[:], fm[:, h, sl], xi[:], ALU.mult)
                    yield
                    self.tt(DVE, Pm[:], pa[:], M[:], ALU.mult)
                    self.mm(pS[:], [(kzt[:], tm[:, b, 256 + h * 128:256 + (h + 1) * 128])])
                    yield
                    self.mm(po[:], [(Pm[:], tm[:, b, 256 + h * 128:256 + (h + 1) * 128]), (q_[:], sb_cur[:])])
                    self.stt(DVE, S[:], S[:], gc[:, j:j + 1], pS[:], ALU.mult, ALU.add)
                    yield
                    self.cp(ACT, A[:, b, h * 128:(h + 1) * 128], po[:])
                    self.cp(ACT, sb_nxt[:], S[:])
                    yield
                if not is_s:
                    self.st(d["o_rS"][pidx, l, dd, h0 + h], S[:], [self.t_out])
            _run_streams([stream(dd, h) for dd in range(2) for h in range(2)])
        self.fw.barrier()
        with contextlib.ExitStack() as st3:
            _finish_branch(self, st3, 2, h0, l, t0, Tn, acc, T_RG, "ret_norm_w", "rB", accB)


def _gla(self, l, t0, Tn, pidx, h0):
    d = self.d
    is_s = pidx < 0
    nblk = Tn // 128
    with contextlib.ExitStack() as st:
        acc = self.sb(st, "g_acc", [128, nblk, 256], F32)
        accB = self.sb(st, "g_accB", [128, nblk, 256], F32)
        with contextlib.ExitStack() as st2:
            fm, tm = _scan_loads(self, st2, "g", t0, Tn, [(22 + h0, 2), (26 + h0, 2)], [(T_GK + h0 * 64, 128), (T_GV + h0 * 128, 256)], rows=64)
            lr = self.sb(st2, "g_lr", [16, Tn], F32)
            w2 = self.sb(st2, "g_w2", [16, 2, 256], F32)
            bg = self.sb(st2, "g_bg", [1, 2, 256], F32)
            la = self.sb(st2, "g_la", [128, 2, nblk, 128], F32)
            bkl = self.psbank(st2, "g_psl")
            ps_l = [self.sub(bkl, i * 128, (i + 1) * 128) for i in range(4)]
            bkc = self.psbank(st2, "g_psc")
            self.ld(w2[:], d["gla_w2"][l].rearrange("a r c -> r a c"))
            self.ld(bg[:], d["gla_b"][l:l + 1, :, :])
            c0 = h0 * 64
            for dd in range(2):
                self.ld(lr[:], d["glr"][dd, :, t0:t0 + Tn], _ztiles(self, t0, Tn))
                for b in range(nblk):
                    sl = slice(b * 128, (b + 1) * 128)
                    p = ps_l[b % 4]
                    self.mm(p[:], [(lr[:, sl], w2[:, dd, c0:c0 + 128]), (self.C(C_ONE)[0:1, :], bg[:, dd, c0:c0 + 128])])
                    self.act(la[:, dd, b, :], p[:], AF.Exp, scale=-1.0)
            self.act(la[:], la[:], AF.Ln, bias=self.epsc[:, 1:2])

            def stream(dd, h, si):
                j = dd * 4 + h0 + h
                S = self.sb(st2, "g_S%d" % j, [64, 128], F32)
                Sb = [self.sb(st2, "g_Sb%d_%d" % (j, i), [64, 128], BF16) for i in range(2)]
                eq = [self.sb(st2, "g_eq%d_%d" % (j, i), [64, 128], F32) for i in range(2)]
                ek = [self.sb(st2, "g_ek%d_%d" % (j, i), [64, 128], F32) for i in range(2)]
                qp = [self.sb(st2, "g_qp%d_%d" % (j, i), [64, 128], BF16) for i in range(2)]
                kp = [self.sb(st2, "g_kp%d_%d" % (j, i), [64, 128], BF16) for i in range(2)]
                ekk = [self.sb(st2, "g_ekk%d_%d" % (j, i), [128, 64], F32) for i in range(2)]
                kpp = [self.sb(st2, "g_kpp%d_%d" % (j, i), [128, 64], BF16) for i in range(2)]
                P = [self.sb(st2, "g_P%d_%d" % (j, i), [128, 128], BF16) for i in range(2)]
                bk = self.psbank(st2, "g_psS%d" % j)
                pr, pa, po = self.sub(bk, 0, 64), self.sub(bk, 64, 192), self.sub(bk, 192, 320)
                pS = T(bk[0:64, 320:448])
                pc = T(bkc[0:64, si * 128:(si + 1) * 128])
                A = acc if dd == 0 else accB
                if is_s:
                    self.ld(S[:], d["sgS"][l, dd, h0 + h])
                else:
                    self.memset(DVE, S[:], 0.0)
                self.cp(ACT, Sb[0][:], S[:])
                yield
                order = list(range(nblk)) if dd == 0 else list(range(nblk - 1, -1, -1))
                tri = self.C(C_LE) if dd == 0 else self.C(C_GE)
                trs = self.C(C_GT) if dd == 0 else self.C(C_LT)
                for bi, b in enumerate(order):
                    sl = slice(b * 128, (b + 1) * 128)
                    i2 = bi % 2
                    sb_cur, sb_nxt = Sb[i2], Sb[(bi + 1) % 2]
                    lah = la[:, dd, b, h * 64:(h + 1) * 64]
                    self.mm(pc[:], [(lah, tri)])
                    self.mm(pr[:], [(trs, lah)])
                    yield
                    self.act(eq[i2][:], pc[:], AF.Exp, scale=-1.0 / 16)
                    self.act(ek[i2][:], pc[:], AF.Exp, scale=1.0 / 16)
                    self.act(ekk[i2][:], pr[:], AF.Exp, scale=-1.0 / 16)
                    yield
                    self.tt(DVE, qp[i2][:], fm[0:64, h, sl], eq[i2][:], ALU.mult)
                    self.tt(POOL, kp[i2][:], fm[0:64, 2 + h, sl], ek[i2][:], ALU.mult)
                    self.tt(DVE, kpp[i2][:], tm[:, b, h * 64:(h + 1) * 64], ekk[i2][:], ALU.mult)
                    yield
                    self.mm(pa[:], [(kp[i2][:], qp[i2][:])])
                    self.mm(pS[:], [(kpp[i2][:], tm[:, b, 128 + h * 128:128 + (h + 1) * 128])])
                    yield
                    self.tt(DVE, P[i2][:], pa[:], tri, ALU.mult)
                    lastc = eq[i2][:, 127:128] if dd == 0 else eq[i2][:, 0:1]
                    self.stt(DVE, S[:], S[:], lastc, pS[:], ALU.mult, ALU.add)
                    yield
                    self.mm(po[:], [(P[i2][:], tm[:, b, 128 + h * 128:128 + (h + 1) * 128]), (qp[i2][:], sb_cur[:])])
                    self.cp(ACT, sb_nxt[:], S[:])
                    yield
                    self.cp(ACT, A[:, b, h * 128:(h + 1) * 128], po[:])
                    yield
                if not is_s:
                    self.st(d["o_gS"][pidx, l, dd, h0 + h], S[:], [self.t_out])
            _run_streams([stream(dd, h, dd * 2 + h) for dd in range(2) for h in range(2)])
        self.fw.barrier()
        with contextlib.ExitStack() as st3:
            _finish_branch(self, st3, 3, h0, l, t0, Tn, acc, T_GG, "gla_norm_w", "gB", accB)


def _mlstm_old(self, l, t0, Tn, pidx, h0):
    d = self.d
    is_s = pidx < 0
    nblk = Tn // 128
    with contextlib.ExitStack() as st:
        acc = self.sb(st, "m_acc", [128, nblk, 256], F32)
        with contextlib.ExitStack() as st2:
            fm, tm = _scan_loads(self, st2, "m", t0, Tn, [(6 + h0, 2), (10 + h0, 2)], [(T_MK + h0 * 128, 256), (T_MV + h0 * 128, 256)])
            G = self.sb(st2, "m_G", [128, nblk, 16], F32)
            gb = self.sb(st2, "m_gb", [128, 16], F32)
            lf = self.sb(st2, "m_lf", [128, nblk, 8], F32)
            ig = self.sb(st2, "m_ig", [128, nblk, 8], F32)
            bb = self.sb(st2, "m_b", [128, nblk, 8], F32)
            bl = self.sb(st2, "m_bl", [128, nblk, 8], F32)
            ew = self.sb(st2, "m_ew", [128, nblk, 8], F32)
            wn = self.sb(st2, "m_wn", [128, nblk, 8], F32)
            enb = self.sb(st2, "m_enb", [128, nblk, 8], F32)
            ebl = self.sb(st2, "m_ebl", [128, nblk, 8], F32)
            tmpg = self.sb(st2, "m_tmpg", [128, nblk, 8], F32)
            vaug = self.sb(st2, "m_vaug", [128, nblk, 2, 129], BF16)
            S = self.sb(st2, "m_S", [128, 8, 129], F32)
            Sb = [self.sb(st2, "m_Sb%d" % i, [128, 8, 129], BF16) for i in range(2)]
            em0 = self.sb(st2, "m_em0", [128, 16], F32)
            P = [self.sb(st2, "m_P%d" % i, [128, 128], BF16) for i in range(3)]
            kw = [self.sb(st2, "m_kw%d" % i, [128, 128], BF16) for i in range(3)]
            dn = [self.sb(st2, "m_dn%d" % i, [128, 4], F32) for i in range(4)]
            bk0 = self.psbank(st2, "m_ps0")
            ps_g = self.sub(bk0, 0, 16)
            bk = self.psbank(st2, "m_psa")
            ps_a = [self.sub(bk, i * 256, (i + 1) * 256, 2) for i in range(2)]
            bk1, bk2 = self.psbank(st2, "m_pso0"), self.psbank(st2, "m_pso1")
            ps_o = [self.sub(bk1, 0, 256), self.sub(bk1, 256, 512), self.sub(bk2, 0, 256), self.sub(bk2, 256, 512)]
            bk = self.psbank(st2, "m_pss")
            ps_s = [self.sub(bk, 0, 256), self.sub(bk, 256, 512)]
            self.ld(G[:], d["mg"][t0:t0 + Tn, :].rearrange("(b p) c -> p b c", p=128), _ztiles(self, t0, Tn))
            self.ld(gb[:], d["mlstm_if_b"][l, :].partition_broadcast(128))
            for b in range(nblk):
                self.tt(DVE, G[:, b, :], G[:, b, :], gb[:], ALU.add)
                for dd in range(2):
                    self.cp(DVE, ig[:, b, dd * 4:dd * 4 + 4], G[:, b, dd * 8:dd * 8 + 4])
                    self.act(lf[:, b, dd * 4:dd * 4 + 4], G[:, b, dd * 8 + 4:dd * 8 + 8], AF.Exp, scale=-1.0)
            self.act(lf[:], lf[:], AF.Ln, bias=self.epsc[:, 1:2])
            self.ts(DVE, lf[:], lf[:], -1.0, ALU.mult)
            for b in range(nblk):
                self.mm(ps_g[:, 0:4], [(self.C(C_LE), lf[:, b, 0:4])])
                self.mm(ps_g[:, 4:8], [(self.C(C_GE), lf[:, b, 4:8])])
                self.mm(ps_g[:, 8:16], [(self.C(C_ONE), lf[:, b, :])])
                self.cp(DVE, bb[:, b, :], ps_g[:, 0:8])
                self.cp(DVE, bl[:, b, :], ps_g[:, 8:16])
            self.tt(DVE, tmpg[:], ig[:], bb[:], ALU.subtract)
            self.act(ew[:], tmpg[:], AF.Exp)
            self.tt(DVE, tmpg[:], tmpg[:], bl[:], ALU.add)
            self.act(wn[:], tmpg[:], AF.Exp)
            self.act(enb[:], bb[:], AF.Exp, scale=-1.0)
            self.act(ebl[:], bl[:], AF.Exp)
            self.memset(POOL, vaug[:], 1.0)
            for b in range(nblk):
                self.cp(POOL, vaug[:, b, :, 0:128], tm.v(tm.ap[:, b, 256:512].rearrange("p (h e) -> p h e", h=2)))
            if not is_s:
                assert nblk == 2
                mcol = self.sb(st2, "m_mcol", [8, 4], F32)
                gT = self.sb(st2, "m_gT", [8, 2], F32)
                blc = self.sb(st2, "m_blc", [8, 2], F32)
                mF = self.sb(st2, "m_mF", [8, 2], F32)
                mrep = self.sb(st2, "m_mrep", [8, 128], F32)
                ps_t = T(bk0[0:8, 16:144])
                ps_c = T(bk0[0:8, 144:146])
                ps_b = self.sub(bk0, 160, 168)
                for b in range(nblk):
                    self.tr(ps_t[:], tmpg[:, b, :], self.C(C_ID))
                    self.red(gT[:, b:b + 1], ps_t[:], ALU.max)
                    self.mm(ps_c[:, b:b + 1], [(lf[:, b, :], self.C(C_ONE)[:, 0:1])])
                self.cp(DVE, blc[:], ps_c[:])
                for (col, b0, b1) in ((0, 0, 1), (1, 1, 0)):
                    self.tt(DVE, mcol[:, 0:1], blc[:, b0:b0 + 1], gT[:, b0:b0 + 1], ALU.max)
                    self.tt(DVE, mcol[:, 1:2], mcol[:, 0:1], blc[:, b1:b1 + 1], ALU.add)
                    self.tt(DVE, mF[:, col:col + 1], mcol[:, 1:2], gT[:, b1:b1 + 1], ALU.max)
                self.tt(DVE, mF[:, 0:1], mF[:, 0:1], self.C(C_Z)[0:8, 2:3], ALU.mult)
                self.tt(DVE, mF[:, 1:2], mF[:, 1:2], self.C(C_Z)[0:8, 3:4], ALU.mult)
                self.tt(DVE, mcol[:, 2:3], mF[:, 0:1], mF[:, 1:2], ALU.add)
                if h0 == 0:
                    with self.nc.allow_non_contiguous_dma(reason="tiny"):
                        self.st(d["o_mm"][pidx, l, :].rearrange("(p o) -> p o", o=1), mcol[:, 2:3], [self.t_out])
                self.ts(DVE, mrep[:], self.C(C_ONE)[0:8, :], mcol[:, 2:3], ALU.mult)
                self.mm(ps_b[:], [(mrep[:], self.C(C_ID)[0:8, 0:8])])
                self.act(em0[:, 8:16], ps_b[:], AF.Exp, scale=-1.0)
            if is_s:
                with self.nc.allow_non_contiguous_dma(reason="state n vectors"):
                    for dd in range(2):
                        self.ld(S[:, dd * 4:dd * 4 + 4, 0:128], d["smC"][l, dd].rearrange("h k e -> k h e"))
                        self.ld(S[:, dd * 4:dd * 4 + 4, 128:129], d["smn"][l, dd].rearrange("h (k o) -> k h o", o=1))
                self.ld(em0[:, 0:8], d["smm"][l, :].partition_broadcast(128))
                self.act(em0[:, 0:8], em0[:, 0:8], AF.Exp)
                for j in range(8):
                    self.ts(DVE, S[:, j, :], S[:, j, :], em0[:, j:j + 1], ALU.mult)
            else:
                self.memset(DVE, S[:], 0.0)
            self.cp(ACT, Sb[0][:], S[:])
            it = 0
            for dd in range(2):
                order = list(range(nblk)) if dd == 0 else list(range(nblk - 1, -1, -1))
                mask = self.C(C_LE) if dd == 0 else self.C(C_GE)
                for bi, b in enumerate(order):
                    sl = slice(b * 128, (b + 1) * 128)
                    pa = ps_a[bi % 2]
                    for h in range(2):
                        self.mm(pa[:, h, :], [(fm[:, 2 + h, sl], fm[:, h, sl])])
                    sb_cur = Sb[bi % 2]
                    sb_nxt = Sb[(bi + 1) % 2]
                    for h in range(2):
                        j = dd * 4 + h0 + h
                        Pm = P[it % 3]; kwt = kw[it % 3]; dnt = dn[it % 4]
                        self.stt(DVE, Pm[:], pa[:, h, :], ew[:, b, j:j + 1], mask, ALU.mult, ALU.mult)
                        po = ps_o[it % 4]
                        self.mm(po[:, 0:129], [(Pm[:], vaug[:, b, h, :]), (fm[:, h, sl], sb_cur[:, j, :])])
                        self.act(dnt[:, 0:1], po[:, 128:129], AF.Abs)
                        self.tt(DVE, dnt[:, 0:1], dnt[:, 0:1], enb[:, b, j:j + 1], ALU.max)
                        self.recip(dnt[:, 1:2], dnt[:, 0:1])
                        a_ = acc[:, b, h * 128:(h + 1) * 128]
                        if dd == 0:
                            self.act(a_, po[:, 0:128], AF.Identity, scale=dnt[:, 1:2])
                        else:
                            self.stt(DVE, a_, po[:, 0:128], dnt[:, 1:2], a_, ALU.mult, ALU.add)
                        self.ts(POOL, kwt[:], tm[:, b, h * 128:(h + 1) * 128], wn[:, b, j:j + 1], ALU.mult)
                        pS = ps_s[it % 2]
                        self.mm(pS[:, 0:129], [(kwt[:], vaug[:, b, h, :])])
                        self.stt(DVE, S[:, j, :], S[:, j, :], ebl[:, b, j:j + 1], pS[:, 0:129], ALU.mult, ALU.add)
                        self.cp(ACT, sb_nxt[:, j, :], S[:, j, :])
                        it += 1
            if not is_s:
                with self.nc.allow_non_contiguous_dma(reason="state n vectors"):
                    for dd in range(2):
                        for h in range(2):
                            j = dd * 4 + h0 + h
                            self.ts(DVE, S[:, j, :], S[:, j, :], em0[:, 8 + j:9 + j], ALU.mult)
                            self.st(d["o_mC"][pidx, l, dd, h0 + h], S[:, j, 0:128], [self.t_out])
                            self.st(d["o_mn"][pidx, l, dd, h0 + h].rearrange("(k o) -> k o", o=1), S[:, j, 128:129], [self.t_out])
        self.fw.barrier()
        with contextlib.ExitStack() as st3:
            _finish_branch(self, st3, 1, h0, l, t0, Tn, acc, T_MO, "mlstm_norm_w", "mB")


def _ret_old(self, l, t0, Tn, pidx, h0):
    d = self.d
    is_s = pidx < 0
    nblk = Tn // 128
    with contextlib.ExitStack() as st:
        acc = self.sb(st, "r_acc", [128, nblk, 256], F32)
        with contextlib.ExitStack() as st2:
            fm, tm = _scan_loads(self, st2, "r", t0, Tn, [(14 + h0, 2), (18 + h0, 2)], [(T_RK + h0 * 128, 256), (T_RV + h0 * 128, 256)])
            lg = self.sb(st2, "r_lg", [128, 8], F32)
            M = self.sb(st2, "r_M", [128, 8, 128], F32)
            xi = self.sb(st2, "r_xi", [128, 8, 128], F32)
            ze = self.sb(st2, "r_ze", [128, 8], F32)
            gc = self.sb(st2, "r_gc", [128, 8], F32)
            S = self.sb(st2, "r_S", [128, 8, 128], F32)
            Sb = [self.sb(st2, "r_Sb%d" % i, [128, 8, 128], BF16) for i in range(2)]
            P = [self.sb(st2, "r_P%d" % i, [128, 2, 128], BF16) for i in range(2)]
            qx = [self.sb(st2, "r_qx%d" % i, [128, 2, 128], BF16) for i in range(2)]
            kz = [self.sb(st2, "r_kz%d" % i, [128, 128], BF16) for i in range(3)]
            bk = self.psbank(st2, "r_psa")
            ps_a = [self.sub(bk, i * 256, (i + 1) * 256, 2) for i in range(2)]
            bk = self.psbank(st2, "r_pso")
            ps_o = [self.sub(bk, i * 256, (i + 1) * 256) for i in range(2)]
            bk = self.psbank(st2, "r_pss")
            ps_s = [self.sub(bk, i * 128, (i + 1) * 128) for i in range(3)]
            self.ld(lg[:], d["ret_decay"][l, :].partition_broadcast(128))
            self.act(lg[:], lg[:], AF.Exp, scale=-1.0)
            self.act(lg[:], lg[:], AF.Ln, bias=self.epsc[:, 1:2])
            self.ts(DVE, lg[:], lg[:], -1.0, ALU.mult)
            for dd in range(2):
                for h in range(2):
                    j = dd * 4 + h0 + h
                    self.act(M[:, j, :], self.C(C_DF if dd == 0 else C_DB), AF.Exp, scale=lg[:, j:j + 1])
                    self.act(xi[:, j, :], self.C(C_PF if dd == 0 else C_PB), AF.Exp, scale=lg[:, j:j + 1])
                    self.act(ze[:, j:j + 1], self.C(C_Z)[:, dd:dd + 1], AF.Exp, scale=lg[:, j:j + 1])
            self.act(gc[:], lg[:], AF.Exp, scale=128.0)
            if is_s:
                for dd in range(2):
                    self.ld(S[:, dd * 4:dd * 4 + 4, :], d["srS"][l, dd].rearrange("h k e -> k h e"))
            else:
                self.memset(DVE, S[:], 0.0)
            self.cp(ACT, Sb[0][:], S[:])
            it = 0
            for dd in range(2):
                order = list(range(nblk)) if dd == 0 else list(range(nblk - 1, -1, -1))
                j0 = dd * 4 + h0
                for bi, b in enumerate(order):
                    sl = slice(b * 128, (b + 1) * 128)
                    pa = ps_a[bi % 2]
                    for h in range(2):
                        self.mm(pa[:, h, :], [(fm[:, 2 + h, sl], fm[:, h, sl])])
                    Pm = P[bi % 2]
                    self.tt(DVE, Pm[:], pa[:], M[:, j0:j0 + 2, :], ALU.mult)
                    q_ = qx[bi % 2]
                    self.tt(POOL, q_[:], fm[:, 0:2, sl], xi[:, j0:j0 + 2, :], ALU.mult)
                    sb_cur = Sb[bi % 2]; sb_nxt = Sb[(bi + 1) % 2]
                    po = ps_o[bi % 2]
                    for h in range(2):
                        j = j0 + h
                        self.mm(po[:, h * 128:(h + 1) * 128], [(Pm[:, h, :], tm[:, b, 256 + h * 128:256 + (h + 1) * 128]), (q_[:, h, :], sb_cur[:, j, :])])
                    if dd == 0:
                        self.cp(ACT, acc[:, b, :], po[:])
                    else:
                        self.tt(DVE, acc[:, b, :], acc[:, b, :], po[:], ALU.add)
                    for h in range(2):
                        j = j0 + h
                        kzt = kz[it % 3]
                        self.ts(POOL, kzt[:], tm[:, b, h * 128:(h + 1) * 128], ze[:, j:j + 1], ALU.mult)
                        pS = ps_s[it % 3]
                        self.mm(pS[:], [(kzt[:], tm[:, b, 256 + h * 128:256 + (h + 1) * 128])])
                        self.stt(DVE, S[:, j, :], S[:, j, :], gc[:, j:j + 1], pS[:], ALU.mult, ALU.add)
                        self.cp(ACT, sb_nxt[:, j, :], S[:, j, :])
                        it += 1
            if not is_s:
                for dd in range(2):
                    for h in range(2):
                        self.st(d["o_rS"][pidx, l, dd, h0 + h], S[:, dd * 4 + h0 + h, :], [self.t_out])
        self.fw.barrier()
        with contextlib.ExitStack() as st3:
            _finish_branch(self, st3, 2, h0, l, t0, Tn, acc, T_RG, "ret_norm_w", "rB")


def _gla_old(self, l, t0, Tn, pidx, h0):
    d = self.d
    is_s = pidx < 0
    nblk = Tn // 128
    with contextlib.ExitStack() as st:
        acc = self.sb(st, "g_acc", [128, nblk, 256], F32)
        with contextlib.ExitStack() as st2:
            fm, tm = _scan_loads(self, st2, "g", t0, Tn, [(22 + h0, 2), (26 + h0, 2)], [(T_GK + h0 * 64, 128), (T_GV + h0 * 128, 256)], rows=64)
            lr = self.sb(st2, "g_lr", [16, Tn], F32)
            w2 = self.sb(st2, "g_w2", [16, 2, 256], F32)
            bg = self.sb(st2, "g_bg", [1, 2, 256], F32)
            lap = [self.sb(st2, "g_lap%d" % i, [128, 128], F32) for i in range(2)]
            eq = [self.sb(st2, "g_eq%d" % i, [64, 128], F32) for i in range(3)]
            ek = [self.sb(st2, "g_ek%d" % i, [64, 128], F32) for i in range(3)]
            qp = [self.sb(st2, "g_qp%d" % i, [64, 128], BF16) for i in range(3)]
            kp = [self.sb(st2, "g_kp%d" % i, [64, 128], BF16) for i in range(3)]
            ekk = [self.sb(st2, "g_ekk%d" % i, [128, 64], F32) for i in range(3)]
            kpp = [self.sb(st2, "g_kpp%d" % i, [128, 64], BF16) for i in range(3)]
            P = [self.sb(st2, "g_P%d" % i, [128, 128], BF16) for i in range(3)]
            S = self.sb(st2, "g_S", [64, 8, 128], F32)
            Sb = [self.sb(st2, "g_Sb%d" % i, [64, 8, 128], BF16) for i in range(2)]
            bk0 = self.psbank(st2, "g_ps0")
            ps_l = self.sub(bk0, 0, 128)
            ps_c = [self.sub(bk0, 128, 256), self.sub(bk0, 256, 384)]
            ps_r = [self.sub(bk0, 384, 448), self.sub(bk0, 448, 512)]
            bk1 = self.psbank(st2, "g_ps1")
            ps_a = [self.sub(bk1, 0, 128), self.sub(bk1, 128, 256)]
            ps_s = [T(bk1[0:64, 256:384]), T(bk1[0:64, 384:512])]
            bk = self.psbank(st2, "g_pso")
            ps_o = [self.sub(bk, i * 256, (i + 1) * 256) for i in range(2)]
            with self.nc.allow_non_contiguous_dma(reason="small params"):
                self.ld(w2[:], d["gla_w2"][l].rearrange("a r c -> r a c"))
            self.ld(bg[:], d["gla_b"][l:l + 1, :, :])
            if is_s:
                for dd in range(2):
                    self.ld(S[:, dd * 4:dd * 4 + 4, :], d["sgS"][l, dd].rearrange("h k e -> k h e"))
            else:
                self.memset(DVE, S[:], 0.0)
            self.cp(ACT, Sb[0][:], S[:])
            it = 0
            for dd in range(2):
                self.ld(lr[:], d["glr"][dd, :, t0:t0 + Tn], _ztiles(self, t0, Tn))
                order = list(range(nblk)) if dd == 0 else list(range(nblk - 1, -1, -1))
                tri = self.C(C_LE) if dd == 0 else self.C(C_GE)
                trs = self.C(C_GT) if dd == 0 else self.C(C_LT)
                c0 = h0 * 64
                for bi, b in enumerate(order):
                    sl = slice(b * 128, (b + 1) * 128)
                    la = lap[bi % 2]
                    self.mm(ps_l[:], [(lr[:, sl], w2[:, dd, c0:c0 + 128]), (self.C(C_ONE)[0:1, :], bg[:, dd, c0:c0 + 128])])
                    self.act(la[:], ps_l[:], AF.Exp, scale=-1.0)
                    self.act(la[:], la[:], AF.Ln, bias=self.epsc[:, 1:2])
                    sb_cur = Sb[bi % 2]; sb_nxt = Sb[(bi + 1) % 2]
                    po = ps_o[bi % 2]
                    for h in range(2):
                        j = dd * 4 + h0 + h
                        lah = la[:, h * 64:(h + 1) * 64]
                        pc = ps_c[it % 2]
                        self.mm(pc[0:64, :], [(lah, tri)])
                        e1 = eq[it % 3]; e2 = ek[it % 3]
                        self.act(e1[:], pc[0:64, :], AF.Exp, scale=-1.0 / 16)
                        self.act(e2[:], pc[0:64, :], AF.Exp, scale=1.0 / 16)
                        q_ = qp[it % 3]; k_ = kp[it % 3]
                        self.tt(DVE, q_[:], fm[0:64, h, sl], e1[:], ALU.mult)
                        self.tt(POOL, k_[:], fm[0:64, 2 + h, sl], e2[:], ALU.mult)
                        pa = ps_a[it % 2]
                        self.mm(pa[:], [(k_[:], q_[:])])
                        Pm = P[it % 3]
                        self.tt(DVE, Pm[:], pa[:], tri, ALU.mult)
                        self.mm(po[:, h * 128:(h + 1) * 128], [(Pm[:], tm[:, b, 128 + h * 128:128 + (h + 1) * 128]), (q_[:], sb_cur[:, j, :])])
                        pr = ps_r[it % 2]
                        self.mm(pr[:], [(trs, lah)])
                        e3 = ekk[it % 3]
                        self.act(e3[:], pr[:], AF.Exp, scale=-1.0 / 16)
                        k2 = kpp[it % 3]
                        self.tt(DVE, k2[:], tm[:, b, h * 64:(h + 1) * 64], e3[:], ALU.mult)
                        pS = ps_s[it % 2]
                        self.mm(pS[:], [(k2[:], tm[:, b, 128 + h * 128:128 + (h + 1) * 128])])
                        lastc = e1[:, 127:128] if dd == 0 else e1[:, 0:1]
                        self.stt(DVE, S[:, j, :], S[:, j, :], lastc, pS[:], ALU.mult, ALU.add)
                        self.cp(ACT, sb_nxt[:, j, :], S[:, j, :])
                        it += 1
                    if dd == 0:
                        self.cp(ACT, acc[:, b, :], po[:])
                    else:
                        self.tt(DVE, acc[:, b, :], acc[:, b, :], po[:], ALU.add)
            if not is_s:
                for dd in range(2):
                    for h in range(2):
                        self.st(d["o_gS"][pidx, l, dd, h0 + h], S[:, dd * 4 + h0 + h, :], [self.t_out])
        self.fw.barrier()
        with contextlib.ExitStack() as st3:
            _finish_branch(self, st3, 3, h0, l, t0, Tn, acc, T_GG, "gla_norm_w", "gB")


def make_consts(TS):
    c = np.zeros((128, NCST, 128), np.float32)
    i = np.arange(128)
    s, t = i[:, None], i[None, :]
    c[:, C_ID] = (s == t)
    c[:, C_LE] = (s <= t)
    c[:, C_GE] = (s >= t)
    c[:, C_GT] = (s > t)
    c[:, C_LT] = (s < t)
    c[:, C_DF] = np.where(t >= s, t - s, 1e6)
    c[:, C_DB] = np.where(s >= t, s - t, 1e6)
    c[:, C_ONE] = 1.0
    c[:, C_PF] = (t + 1) * np.ones((128, 1))
    c[:, C_PB] = (128 - t) * np.ones((128, 1))
    c[:, C_Z, 0] = 127 - i
    c[:, C_Z, 1] = i
    c[:, C_Z, 2] = (i < 4)
    c[:, C_Z, 3] = (i >= 4)
    c[:, C_MP] = np.where(t >= s, 0.0, -30000.0)
    c[:, C_MN] = np.where(t <= s, 0.0, -30000.0)
    cst = c.reshape(128, NCST * 128)
    tt = np.arange(TS)
    row = (tt // 64).astype(np.float32)
    col = (tt % 64).astype(np.float32)
    inv = (10000.0 ** (-np.arange(32, dtype=np.float32) / 32)).astype(np.float32)
    C = np.zeros((128, TS), np.float32)
    Sg = np.zeros((128, TS), np.float32)
    for p in range(128):
        pos = row if p < 64 else col
        ang = (pos * inv[p % 32]).astype(np.float32)
        C[p] = np.cos(ang)
        sn = np.sin(ang)
        Sg[p] = -sn if (p % 64) < 32 else sn
    perm = np.zeros((128, 128), np.float32)
    for m in range(128):
        k = m + 32 if (m % 64) < 32 else m - 32
        perm[k, m] = 1.0
    return cst, C, Sg, perm


_CACHE = {}


def kernel(**inp):
    return _run(inp, 8, L)


def _run(inp, NC, nlayers, debug=False):
    inp = {k: np.asarray(v) for k, v in inp.items()}
    TS = inp["x_sample"].shape[1]
    TP = inp["x_prompt"].shape[1]
    NP = inp["x_prompt"].shape[0] // NC
    key = (TS, TP, NP, nlayers, debug)
    if key not in _CACHE:
        _CACHE[key] = K(TS, TP, NP, nlayers, debug).build()
    nc = _CACHE[key]
    cst, rc, rs, perm = make_consts(TS)
    f = lambda a: np.ascontiguousarray(a, dtype=np.float32)
    shared = {k: f(inp[k]) for k in ["w_ada", "b_ada", "norm1_w", "norm2_w", "w_in", "attn_sink", "mlstm_norm_w", "ret_norm_w",
                                      "gla_w2", "gla_b", "gla_norm_w", "w_br", "w_mgate", "w_out", "ffn_w_gu", "ffn_w_down", "final_norm_w"]}
    shared["mlstm_if_b"] = f(inp["mlstm_if_b"].reshape(L, 16))
    shared["ret_decay"] = f(inp["ret_decay"].reshape(L, 8))
    shared.update(cst=cst, ropeC=rc, ropeS=rs, perm=perm)
    in_maps = []
    for c in range(NC):
        m = dict(shared)
        m["x"] = f(np.concatenate([inp["x_sample"][c]] + [inp["x_prompt"][c * NP + i] for i in range(NP)], axis=0))
        m["cak"] = f(inp["cache_attn_k"][c].reshape(L, 256, 256))
        m["cav"] = f(inp["cache_attn_v"][c].reshape(L, 256, 256))
        m["smC"] = f(inp["state_mlstm_C"][c]); m["smn"] = f(inp["state_mlstm_n"][c]); m["smm"] = f(inp["state_mlstm_m"][c].reshape(L, 8))
        m["srS"] = f(inp["state_ret_S"][c]); m["sgS"] = f(inp["state_gla_S"][c])
        m["cond"] = f(np.stack([inp["c"][c], inp["c_ctx"]], axis=0))
        in_maps.append(m)
    res = run_bass_kernel_spmd(nc, in_maps, core_ids=list(range(NC)))
    R = res.results
    if debug:
        global _DBG
        _DBG = R
    y_s = np.stack([R[c]["y"][:TS] for c in range(NC)], axis=0)
    y_p = np.stack([R[c]["y"][TS + i * TP:TS + (i + 1) * TP] for c in range(NC) for i in range(NP)], axis=0)
    cat = lambda k: np.concatenate([R[c][k] for c in range(NC)], axis=0)
    B = NC * NP
    return (y_p.astype(np.float32), y_s.astype(np.float32),
            cat("o_ak").reshape(B, L, TP, 2, 128), cat("o_av").reshape(B, L, TP, 2, 128),
            cat("o_mC"), cat("o_mn"), cat("o_mm").reshape(B, L, 2, 4), cat("o_rS"), cat("o_gS"))
```
